# Optimizing a Trainium2 kernel written in Bass

```python
import math
import jax, jax.numpy as jnp
from jax import lax
import numpy as np

D_MODEL = 1024
BATCH = 8
SEQ = 2048
DEPTH = 2

F32 = jnp.float32
CHUNK = 64
N_BRANCH = 4
BRANCH_WIDTH = D_MODEL
EPS = 1e-6

SSM_HEAD_DIM = 64
SSM_HEADS = BRANCH_WIDTH // SSM_HEAD_DIM
SSM_GROUPS = 4
SSM_STATE = 128
SSM_CONV = 4
SSM_XBC = BRANCH_WIDTH + 2 * SSM_GROUPS * SSM_STATE

LRU_BLOCKS = 16
LRU_BLOCK = BRANCH_WIDTH // LRU_BLOCKS
LRU_CONV = 4
LRU_C = 8.0

RET_HEADS = 8
RET_QK_DIM = 64
RET_V_DIM = BRANCH_WIDTH // RET_HEADS
RET_QK = RET_HEADS * RET_QK_DIM
ROPE_BASE = 10000.0

RWKV_HEAD = 64
RWKV_HEADS = BRANCH_WIDTH // RWKV_HEAD
RWKV_W_LORA = 64
RWKV_A_LORA = 64
RWKV_G_LORA = 128
RWKV_IN = 3 * BRANCH_WIDTH + RWKV_W_LORA + RWKV_A_LORA + RWKV_G_LORA
RWKV_LN_EPS = 64e-5

D_FF = 256 * ((8 * D_MODEL // 3 + 255) // 256)
FFN_CONV = 3

IN_SPLITS = (BRANCH_WIDTH, SSM_XBC, SSM_HEADS,
             BRANCH_WIDTH, BRANCH_WIDTH,
             RET_QK, RET_QK, BRANCH_WIDTH, BRANCH_WIDTH,
             RWKV_IN,
             N_BRANCH * D_MODEL)
N_IN = sum(IN_SPLITS)

kernel_name = "hybrid_ssd_rglru_retnet_rwkv7_block"


def rms_norm(x, w):
    xf = x.astype(F32)
    y = xf * lax.rsqrt(jnp.mean(xf * xf, axis=-1, keepdims=True) + EPS)
    return (y * w.astype(F32)).astype(x.dtype)


def head_norm(y, eps):
    mu = jnp.mean(y, axis=-1, keepdims=True)
    yc = y - mu
    return yc * lax.rsqrt(jnp.mean(yc * yc, axis=-1, keepdims=True) + eps)


def causal_dwconv(x, w, b):
    k = w.shape[0]
    y = lax.conv_general_dilated(x, w[:, None, :].astype(x.dtype), window_strides=(1,),
                                 padding=[(k - 1, 0)],
                                 dimension_numbers=('NWC', 'WIO', 'NWC'),
                                 feature_group_count=x.shape[-1])
    return y + b.astype(x.dtype)


def rotary(x, cos, sin):
    x1, x2 = jnp.split(x, 2, axis=-1)
    c, s = cos[:, None, :], sin[:, None, :]
    return jnp.concatenate([x1 * c - x2 * s, x1 * s + x2 * c], axis=-1)


def mamba2_branch(z, xbc, dt, conv_w, conv_b, dt_bias, a_log, d_skip, norm_w):
    b, s, _ = z.shape
    nc, hg = s // CHUNK, SSM_HEADS // SSM_GROUPS
    xbc = jax.nn.silu(causal_dwconv(xbc, conv_w, conv_b)).astype(F32)
    xs, bm, cm = jnp.split(xbc, [BRANCH_WIDTH, BRANCH_WIDTH + SSM_GROUPS * SSM_STATE], axis=-1)
    xs = xs.reshape(b, nc, CHUNK, SSM_GROUPS, hg, SSM_HEAD_DIM)
    bm = bm.reshape(b, nc, CHUNK, SSM_GROUPS, SSM_STATE)
    cm = cm.reshape(b, nc, CHUNK, SSM_GROUPS, SSM_STATE)
    dt = jax.nn.softplus(dt.astype(F32) + dt_bias.astype(F32)).reshape(b, nc, CHUNK, SSM_GROUPS, hg)
    a = -jnp.exp(a_log.astype(F32)).reshape(SSM_GROUPS, hg)
    a_cum = jnp.cumsum(dt * a, axis=2)
    xdt = xs * dt[..., None]
    causal = jnp.tril(jnp.ones((CHUNK, CHUNK), bool))[:, :, None, None]
    seg = a_cum[:, :, :, None] - a_cum[:, :, None, :]
    decay = jnp.exp(jnp.where(causal, seg, -jnp.inf))
    cb = jnp.einsum('bclgn,bcsgn->bclsg', cm, bm)
    y_diag = jnp.einsum('bclsgh,bcsghp->bclghp', cb[..., None] * decay, xdt)
    decay_to_end = jnp.exp(a_cum[:, :, -1:] - a_cum)
    states = jnp.einsum('bclgn,bclgh,bclghp->bcghpn', bm, decay_to_end, xdt)
    chunk_decay = jnp.exp(a_cum[:, :, -1])

    def step(h, inp):
        st, dec = inp
        return h * dec[..., None, None] + st, h

    h0 = jnp.zeros((b, SSM_GROUPS, hg, SSM_HEAD_DIM, SSM_STATE), F32)
    _, prev = lax.scan(step, h0, (jnp.moveaxis(states, 1, 0), jnp.moveaxis(chunk_decay, 1, 0)))
    prev = jnp.moveaxis(prev, 0, 1)
    y_off = jnp.einsum('bclgn,bcghpn,bclgh->bclghp', cm, prev, jnp.exp(a_cum))
    y = y_diag + y_off + xs * d_skip.astype(F32).reshape(SSM_GROUPS, hg, 1)
    y = y.reshape(b, s, BRANCH_WIDTH) * jax.nn.silu(z.astype(F32))
    return rms_norm(y, norm_w).astype(z.dtype)


def rglru_branch(gate_in, x_in, conv_w, conv_b, w_a, b_a, w_i, b_i, lam):
    b, s, _ = x_in.shape
    xc = causal_dwconv(x_in, conv_w, conv_b).astype(F32)
    xb = xc.reshape(b, s, LRU_BLOCKS, LRU_BLOCK)
    r = jax.nn.sigmoid(jnp.einsum('bsgi,gij->bsgj', xb, w_a.astype(F32)).reshape(b, s, BRANCH_WIDTH) + b_a.astype(F32))
    i = jax.nn.sigmoid(jnp.einsum('bsgi,gij->bsgj', xb, w_i.astype(F32)).reshape(b, s, BRANCH_WIDTH) + b_i.astype(F32))
    log_a = -LRU_C * r * jax.nn.softplus(-lam.astype(F32))
    u = jnp.sqrt(-jnp.expm1(2.0 * log_a)) * (i * xc)

    def combine(p, q):
        a1, u1 = p
        a2, u2 = q
        return a1 * a2, a2 * u1 + u2

    _, h = lax.associative_scan(combine, (jnp.exp(log_a), u), axis=1)
    return (h * jax.nn.gelu(gate_in.astype(F32), approximate=True)).astype(x_in.dtype)


def retention_branch(q, k, v, g, norm_w, cos, sin):
    b, s, _ = q.shape
    nc = s // CHUNK
    out_dtype = g.dtype
    qh = rotary(q.astype(F32).reshape(b, s, RET_HEADS, RET_QK_DIM), cos, sin)
    kh = rotary(k.astype(F32).reshape(b, s, RET_HEADS, RET_QK_DIM), cos, sin) * (RET_QK_DIM ** -0.5)
    vh = v.astype(F32).reshape(b, s, RET_HEADS, RET_V_DIM)
    qh, kh, vh = (t.reshape(b, nc, CHUNK, RET_HEADS, t.shape[-1]) for t in (qh, kh, vh))
    log_gamma = jnp.log1p(-jnp.exp2(-5.0 - jnp.arange(RET_HEADS, dtype=F32)))
    pos = jnp.arange(CHUNK, dtype=F32)
    inner = jnp.exp(log_gamma[:, None, None] * jnp.abs(pos[:, None] - pos[None, :]))
    scores = jnp.einsum('bclhd,bcshd->bchls', qh, kh) * inner
    y_inner = jnp.einsum('bchls,bcshe->bclhe', scores, vh)
    k_to_end = jnp.exp(log_gamma[:, None] * (CHUNK - 1.0 - pos))
    kv = jnp.einsum('bcshd,bcshe,hs->bchde', kh, vh, k_to_end)
    chunk_decay = jnp.exp(log_gamma * CHUNK)

    def step(r_state, kv_c):
        return r_state * chunk_decay[:, None, None] + kv_c, r_state

    r0 = jnp.zeros((b, RET_HEADS, RET_QK_DIM, RET_V_DIM), F32)
    _, prev = lax.scan(step, r0, jnp.moveaxis(kv, 1, 0))
    prev = jnp.moveaxis(prev, 0, 1)
    q_from_start = jnp.exp(log_gamma[:, None] * (pos + 1.0))
    y_cross = jnp.einsum('bclhd,bchde,hl->bclhe', qh, prev, q_from_start)
    y = head_norm((y_inner + y_cross).reshape(b, s, RET_HEADS, RET_V_DIM), EPS)
    y = y.reshape(b, s, BRANCH_WIDTH) * norm_w.astype(F32)
    return (jax.nn.silu(g.astype(F32)) * y).astype(out_dtype)


def rwkv7_branch(rw, mu, w0, w2, a0, a2, g2, k_k, k_a, r_k, ln_w, ln_b):
    b, s, _ = rw.shape
    out_dtype = rw.dtype
    rw = rw.astype(F32)
    prev = jnp.pad(rw, ((0, 0), (1, 0), (0, 0)))[:, :-1]
    rw = rw + (prev - rw) * mu.astype(F32)
    wd = BRANCH_WIDTH
    r, k, v, wl, al, gl = jnp.split(
        rw, [wd, 2 * wd, 3 * wd, 3 * wd + RWKV_W_LORA, 3 * wd + RWKV_W_LORA + RWKV_A_LORA], axis=-1)
    w = -jax.nn.softplus(-(w0.astype(F32) + jnp.tanh(wl) @ w2.astype(F32))) - 0.5
    decay = jnp.exp(-jnp.exp(w))
    a = jax.nn.sigmoid(a0.astype(F32) + al @ a2.astype(F32))
    gate = jax.nn.sigmoid(gl) @ g2.astype(F32)
    heads = lambda t: t.reshape(b, s, RWKV_HEADS, RWKV_HEAD)
    kk = heads(k * k_k.astype(F32))
    kk = kk / jnp.maximum(jnp.sqrt(jnp.sum(kk * kk, axis=-1, keepdims=True)), 1e-12)
    k = k * (1.0 + (a - 1.0) * k_a.astype(F32))
    r_h, k_h, v_h, w_h, a_h = heads(r), heads(k), heads(v), heads(decay), heads(a)

    def step(state, inp):
        r_t, w_t, k_t, v_t, kk_t, b_t = inp
        sk = jnp.einsum('bhvk,bhk->bhv', state, kk_t)
        state = (state * w_t[:, :, None, :] - sk[..., None] * b_t[:, :, None, :]
                 + v_t[..., None] * k_t[:, :, None, :])
        return state, jnp.einsum('bhvk,bhk->bhv', state, r_t)

    s0 = jnp.zeros((b, RWKV_HEADS, RWKV_HEAD, RWKV_HEAD), F32)
    seq_first = lambda t: jnp.moveaxis(t, 1, 0)
    _, o = lax.scan(step, s0, (seq_first(r_h), seq_first(w_h), seq_first(k_h), seq_first(v_h),
                               seq_first(kk), seq_first(kk * a_h)))
    o = jnp.moveaxis(o, 0, 1)
    o = head_norm(o, RWKV_LN_EPS).reshape(b, s, wd) * ln_w.astype(F32) + ln_b.astype(F32)
    bonus = jnp.sum(r_h * k_h * r_k.astype(F32).reshape(RWKV_HEADS, RWKV_HEAD), axis=-1, keepdims=True) * v_h
    o = o + bonus.reshape(b, s, wd)
    return (o * gate).astype(out_dtype)


def hybrid_mixer(h, w_in, ssm_p, lru_p, ret_p, rwkv_p, w_branch, w_out, cos, sin):
    b, s, _ = h.shape
    proj = h @ w_in
    (m_z, m_xbc, m_dt, l_gate, l_x, r_q, r_k, r_v, r_g, rw, gate_logits) = jnp.split(
        proj, np.cumsum(IN_SPLITS)[:-1].tolist(), axis=-1)
    y_ssd = mamba2_branch(m_z, m_xbc, m_dt, *ssm_p)
    y_lru = rglru_branch(l_gate, l_x, *lru_p)
    y_ret = retention_branch(r_q, r_k, r_v, r_g, *ret_p, cos, sin)
    y_rwkv = rwkv7_branch(rw, *rwkv_p)
    ys = jnp.stack([y_ssd, y_lru, y_ret, y_rwkv], axis=2)
    branch = jnp.einsum('bsmw,mwd->bsmd', ys, w_branch)
    gates = jax.nn.sigmoid(gate_logits.astype(F32)).reshape(b, s, N_BRANCH, D_MODEL)
    merged = jnp.sum(gates * branch.astype(F32), axis=2).astype(h.dtype)
    return merged @ w_out


def conv_ffn(h, w_up, conv_w, conv_b, w_down):
    u = causal_dwconv(h @ w_up, conv_w, conv_b)
    val, gt = jnp.split(u, 2, axis=-1)
    return (jax.nn.silu(gt) * val) @ w_down


def setup_inputs(seed: int = 0) -> dict:
    key = jax.random.key(seed)
    ks = iter(jax.random.split(key, 64))
    nrm = lambda shape, scale: jax.random.normal(next(ks), shape, F32) * scale
    unif = lambda shape, lo, hi: jax.random.uniform(next(ks), shape, F32, lo, hi)
    L, W, D = DEPTH, BRANCH_WIDTH, D_MODEL
    gain = lambda shape: 1.0 + nrm(shape, 0.02)
    x = nrm((BATCH, SEQ, D), 1.0)
    norm_mix = gain((L, D))
    w_in = nrm((L, D, N_IN), D ** -0.5)
    ssm_conv_w = nrm((L, SSM_CONV, SSM_XBC), SSM_CONV ** -0.5)
    ssm_conv_b = nrm((L, SSM_XBC), 0.02)
    dt0 = jnp.exp(unif((L, SSM_HEADS), math.log(1e-3), math.log(1e-1)))
    ssm_dt_bias = dt0 + jnp.log(-jnp.expm1(-dt0))
    ssm_a_log = jnp.log(unif((L, SSM_HEADS), 1.0, 16.0))
    ssm_d = 1.0 + nrm((L, SSM_HEADS), 0.1)
    ssm_norm = gain((L, W))
    lru_conv_w = nrm((L, LRU_CONV, W), LRU_CONV ** -0.5)
    lru_conv_b = nrm((L, W), 0.02)
    lru_w_a = nrm((L, LRU_BLOCKS, LRU_BLOCK, LRU_BLOCK), LRU_BLOCK ** -0.5)
    lru_b_a = nrm((L, W), 0.02)
    lru_w_i = nrm((L, LRU_BLOCKS, LRU_BLOCK, LRU_BLOCK), LRU_BLOCK ** -0.5)
    lru_b_i = nrm((L, W), 0.02)
    p = unif((L, W), 0.9, 0.999) ** (1.0 / LRU_C)
    lru_lam = jnp.log(p) - jnp.log1p(-p)
    ret_norm = gain((L, W))
    rwkv_mu = unif((L, RWKV_IN), 0.0, 1.0)
    ratio = jnp.arange(W, dtype=F32) / (W - 1)
    rwkv_w0 = (-7.0 + 5.0 * ratio ** 0.85 + 0.5)[None, :] + nrm((L, W), 0.1)
    rwkv_w2 = nrm((L, RWKV_W_LORA, W), 0.1 * RWKV_W_LORA ** -0.5)
    rwkv_a0 = nrm((L, W), 0.1)
    rwkv_a2 = nrm((L, RWKV_A_LORA, W), 0.5 * RWKV_A_LORA ** -0.5)
    rwkv_g2 = nrm((L, RWKV_G_LORA, W), RWKV_G_LORA ** -0.5)
    rwkv_k_k = 0.85 + nrm((L, W), 0.02)
    rwkv_k_a = 1.0 + nrm((L, W), 0.02)
    rwkv_r_k = nrm((L, W), 0.1)
    rwkv_ln_w = gain((L, W))
    rwkv_ln_b = nrm((L, W), 0.02)
    w_branch = nrm((L, N_BRANCH, W, D), W ** -0.5)
    w_out = nrm((L, D, D), D ** -0.5)
    norm_ffn = gain((L, D))
    ffn_up = nrm((L, D, 2 * D_FF), D ** -0.5)
    ffn_conv_w = nrm((L, FFN_CONV, 2 * D_FF), FFN_CONV ** -0.5)
    ffn_conv_b = nrm((L, 2 * D_FF), 0.02)
    ffn_down = nrm((L, D_FF, D), D_FF ** -0.5)
    norm_final = gain((D,))
    return {"x": x, "norm_mix": norm_mix, "w_in": w_in,
            "ssm_conv_w": ssm_conv_w, "ssm_conv_b": ssm_conv_b, "ssm_dt_bias": ssm_dt_bias,
            "ssm_a_log": ssm_a_log, "ssm_d": ssm_d, "ssm_norm": ssm_norm,
            "lru_conv_w": lru_conv_w, "lru_conv_b": lru_conv_b, "lru_w_a": lru_w_a,
            "lru_b_a": lru_b_a, "lru_w_i": lru_w_i, "lru_b_i": lru_b_i, "lru_lam": lru_lam,
            "ret_norm": ret_norm,
            "rwkv_mu": rwkv_mu, "rwkv_w0": rwkv_w0, "rwkv_w2": rwkv_w2, "rwkv_a0": rwkv_a0,
            "rwkv_a2": rwkv_a2, "rwkv_g2": rwkv_g2, "rwkv_k_k": rwkv_k_k, "rwkv_k_a": rwkv_k_a,
            "rwkv_r_k": rwkv_r_k, "rwkv_ln_w": rwkv_ln_w, "rwkv_ln_b": rwkv_ln_b,
            "w_branch": w_branch, "w_out": w_out, "norm_ffn": norm_ffn,
            "ffn_up": ffn_up, "ffn_conv_w": ffn_conv_w, "ffn_conv_b": ffn_conv_b,
            "ffn_down": ffn_down, "norm_final": norm_final}


def reference(x, norm_mix, w_in, ssm_conv_w, ssm_conv_b, ssm_dt_bias, ssm_a_log, ssm_d, ssm_norm,
              lru_conv_w, lru_conv_b, lru_w_a, lru_b_a, lru_w_i, lru_b_i, lru_lam, ret_norm,
              rwkv_mu, rwkv_w0, rwkv_w2, rwkv_a0, rwkv_a2, rwkv_g2, rwkv_k_k, rwkv_k_a, rwkv_r_k,
              rwkv_ln_w, rwkv_ln_b, w_branch, w_out, norm_ffn, ffn_up, ffn_conv_w, ffn_conv_b,
              ffn_down, norm_final):
    s = x.shape[1]
    pos = jnp.arange(s, dtype=F32)
    inv_freq = ROPE_BASE ** (-jnp.arange(0, RET_QK_DIM, 2, dtype=F32) / RET_QK_DIM)
    ang = pos[:, None] * inv_freq[None, :]
    cos, sin = jnp.cos(ang), jnp.sin(ang)
    for l in range(DEPTH):
        h = rms_norm(x, norm_mix[l])
        ssm_p = (ssm_conv_w[l], ssm_conv_b[l], ssm_dt_bias[l], ssm_a_log[l], ssm_d[l], ssm_norm[l])
        lru_p = (lru_conv_w[l], lru_conv_b[l], lru_w_a[l], lru_b_a[l], lru_w_i[l], lru_b_i[l], lru_lam[l])
        ret_p = (ret_norm[l],)
        rwkv_p = (rwkv_mu[l], rwkv_w0[l], rwkv_w2[l], rwkv_a0[l], rwkv_a2[l], rwkv_g2[l],
                  rwkv_k_k[l], rwkv_k_a[l], rwkv_r_k[l], rwkv_ln_w[l], rwkv_ln_b[l])
        x = x + hybrid_mixer(h, w_in[l], ssm_p, lru_p, ret_p, rwkv_p, w_branch[l], w_out[l], cos, sin)
        h = rms_norm(x, norm_ffn[l])
        x = x + conv_ffn(h, ffn_up[l], ffn_conv_w[l], ffn_conv_b[l], ffn_down[l])
    return rms_norm(x, norm_final)
```

```python
import numpy as np
from contextlib import ExitStack
import concourse.bass as bass
import concourse.mybir as mybir
from concourse.bass_utils import run_bass_kernel_spmd

F32 = mybir.dt.float32
BF16 = mybir.dt.bfloat16
ALU = mybir.AluOpType
AF = mybir.ActivationFunctionType
AX = mybir.AxisListType

KD = 16
ENGS = ["pe", "act", "dve", "pool", "sp"]
DMAQ = ["sp", "pool", "act"]


class Buf:
    __slots__ = ("name", "w", "r")

    def __init__(self, name=""):
        self.name = name
        self.w = None
        self.r = {}


class Sched:
    def __init__(self, nc):
        self.nc = nc
        self.ops = {e: [] for e in ENGS}
        self.cnt = {e: 0 for e in ENGS}
        self.dcnt = {e: 0 for e in DMAQ}
        self.known = {e: {} for e in ENGS}
        self.bar = {e: None for e in ENGS}
        self.nops = 0
        self.rec = None

    def record(self, f):
        assert self.rec is None
        self.rec = []
        f()
        r, self.rec = self.rec, None
        return r

    def replay_interleaved(self, a, b):
        na, nb = len(a), len(b)
        ia = ib = 0
        while ia < na or ib < nb:
            if ib >= nb or (ia < na and ia * nb <= ib * na):
                self.add(*a[ia])
                ia += 1
            else:
                self.add(*b[ib])
                ib += 1

    @staticmethod
    def _eng_of(key):
        return key[1]

    def add(self, eng, fn, reads=(), writes=(), dma=False):
        if self.rec is not None:
            self.rec.append((eng, fn, tuple(reads), tuple(writes), dma))
            return None
        deps = {}

        def dep(kv):
            if kv is None:
                return
            k, v = kv
            if deps.get(k, 0) < v:
                deps[k] = v

        for b in reads:
            dep(b.w)
        for b in writes:
            dep(b.w)
            for k, v in b.r.items():
                if (not dma) and k == ("c", eng) and eng == "pe":
                    continue
                dep((k, v))
        if self.bar[eng] is not None:
            for k, v in self.bar[eng].items():
                dep((k, v))
            self.bar[eng] = None
        if dma:
            idx = self.dcnt[eng]
            self.dcnt[eng] += 1
            if idx >= KD:
                dep((("d", eng, (idx - KD) % KD), 16 * ((idx - KD) // KD + 1)))
            tok = (("d", eng, idx % KD), 16 * (idx // KD + 1))
        else:
            self.cnt[eng] += 1
            tok = (("c", eng), self.cnt[eng])
        waits = []
        kn = self.known[eng]
        for k, v in deps.items():
            if k == ("c", "pe") and eng == "pe" and not dma:
                continue
            if kn.get(k, 0) >= v:
                continue
            kn[k] = v
            waits.append((k, v))
        self.ops[eng].append((fn, waits, tok, dma))
        self.nops += 1
        for b in writes:
            b.w = tok
            b.r = {}
        for b in reads:
            if b in writes:
                continue
            k, v = tok
            if b.r.get(k, 0) < v:
                b.r[k] = v
        return tok

    def barrier(self):
        snap = {}
        for e in ENGS:
            if self.cnt[e] > 0:
                snap[("c", e)] = self.cnt[e]
        for q in DMAQ:
            n = self.dcnt[q]
            for idx in range(max(0, n - KD), n):
                k = ("d", q, idx % KD)
                v = 16 * (idx // KD + 1)
                if snap.get(k, 0) < v:
                    snap[k] = v
        for e in ENGS:
            cur = self.bar[e] or {}
            for k, v in snap.items():
                if cur.get(k, 0) < v:
                    cur[k] = v
            self.bar[e] = cur

    def emit(self, final_wait_eng="sp"):
        nc = self.nc
        self.barrier()
        fin = self.bar[final_wait_eng]
        needed = {e: set() for e in ENGS}
        for e in ENGS:
            for fn, waits, tok, dma in self.ops[e]:
                for k, v in waits:
                    if k[0] == "c":
                        needed[k[1]].add(v)
        for k, v in fin.items():
            if k[0] == "c":
                needed[k[1]].add(v)
        remap = {}
        for e in ENGS:
            s = sorted(needed[e])
            remap[e] = {v: i + 1 for i, v in enumerate(s)}
        with ExitStack() as st:
            sems = {}
            for e in ENGS:
                sems[("c", e)] = st.enter_context(nc.semaphore("c_" + e))
            for q in DMAQ:
                if self.dcnt[q] > 0:
                    for j in range(KD):
                        sems[("d", q, j)] = st.enter_context(nc.semaphore("d_%s_%d" % (q, j)))
            block = st.enter_context(nc.Block())

            def run(engname, e):
                for fn, waits, tok, dma in self.ops[engname]:
                    for k, v in waits:
                        if k[0] == "c":
                            v = remap[k[1]][v]
                        e.wait_ge(sems[k], v)
                    ins = fn(e)
                    k, v = tok
                    if dma:
                        ins.then_inc(sems[k], 16)
                    elif v in remap[engname]:
                        ins.then_inc(sems[k], 1)
                if engname == final_wait_eng:
                    for k, v in fin.items():
                        if k[0] == "c":
                            v = remap[k[1]][v]
                        e.wait_ge(sems[k], v)

            @block.tensor
            def _(e):
                run("pe", e)

            @block.scalar
            def _(e):
                run("act", e)

            @block.vector
            def _(e):
                run("dve", e)

            @block.gpsimd
            def _(e):
                run("pool", e)

            @block.sync
            def _(e):
                run("sp", e)


class Arena:
    def __init__(self, nc, st, name, ncols_f32):
        self.t = st.enter_context(nc.sbuf_tensor(name, [128, ncols_f32], F32))
        self.n = ncols_f32
        self.off = 0
        self.marks = []

    def alloc(self, shape, dtype=F32, parts=128):
        ne = int(np.prod(shape))
        if dtype == BF16:
            ncol = (ne + 1) // 2
        else:
            ncol = ne
        assert self.off + ncol <= self.n, "arena overflow %d + %d > %d" % (self.off, ncol, self.n)
        ap = self.t[0:parts, self.off:self.off + ncol]
        self.off += ncol
        if dtype == BF16:
            ap = ap.bitcast(BF16)
            if ne % 2:
                ap = ap[:, 0:ne]
        if len(shape) == 2:
            ap = ap.rearrange("p (a b) -> p a b", a=shape[0])
        elif len(shape) == 3:
            ap = ap.rearrange("p (a b c) -> p a b c", a=shape[0], b=shape[1])
        return ap

    def mark(self):
        self.marks.append(self.off)

    def release(self):
        self.off = self.marks.pop()

import math

SKEW = False
SEQ = 2048
DM = 1024
NT = 16
EPS = 1e-6
IN_SPLITS = (1024, 2048, 16, 1024, 1024, 512, 512, 1024, 1024, 3328, 4096)
NTM = 4112
NFM = 11520
DFF = 2816
F_XBC, F_LX, F_LG, F_RG, F_WR, F_WK, F_LO, F_GL, F_GATE = 0, 16, 24, 32, 40, 48, 56, 57, 58
T_Z, T_QK, T_RV, T_WV, T_DT = 0, 1024, 2048, 3072, 4096


def _cols():
    off = np.cumsum([0] + list(IN_SPLITS))
    rwo = off[9]
    ar = np.arange
    tm = np.concatenate([ar(off[0], off[1]), ar(off[5], off[7]), ar(off[7], off[8]),
                         ar(rwo + 2048, rwo + 3072), ar(off[2], off[3])])
    fm = np.concatenate([ar(off[1], off[2]), ar(off[4], off[5]), ar(off[3], off[4]), ar(off[8], off[9]),
                         ar(rwo, rwo + 2048), ar(rwo + 3072, rwo + 3328), ar(off[10], off[11])])
    assert len(tm) == NTM and len(fm) == NFM
    return tm, fm


def _fm(v):
    return np.ascontiguousarray(v.reshape(-1, 128).T)


PFM_SPEC = [("ssm_cw", 64), ("ssm_cb", 16), ("lru_cw", 32), ("lru_cb", 8), ("lru_ba", 8), ("lru_bi", 8),
            ("lru_lam", 8), ("ret_nw", 8), ("mu_fm", 18), ("w0", 8), ("a0", 8), ("k_k", 8), ("k_a", 8),
            ("r_k", 8), ("ffn_cw", 132), ("ffn_cb", 44)]
PROW_SPEC = [("norm_mix", 1024), ("norm_ffn", 1024), ("ssm_norm", 1024), ("mu_v", 1024), ("ln_w", 1024),
             ("ln_b", 1024), ("dt_bias", 16), ("a_log", 16), ("ssm_d", 16)]
CST_SPEC = [("ident", 128), ("tri", 128), ("ones", 128), ("negmask4", 128), ("GT", 1024), ("gqA", 1024),
            ("gqB", 1024), ("kend", 8), ("cdt", 8), ("cos", 512), ("sin", 512), ("maskA", 384), ("maskB", 256),
            ("headsel", 2), ("blockones", 128)]


def _offs(spec):
    d = {}
    o = 0
    for n, c in spec:
        d[n] = (o, c)
        o += c
    return d, o


PFM_OFF, NPF = _offs(PFM_SPEC)
PROW_OFF, NROW = _offs(PROW_SPEC)
CST_OFF, NCST = _offs(CST_SPEC)


def make_cst():
    c = {}
    p = np.arange(128)
    c["ident"] = np.eye(128)
    c["tri"] = (p[:, None] <= p[None, :]).astype(np.float64)
    c["ones"] = np.ones((128, 128))
    nm = np.where(p[:, None] <= p[None, :], 0.0, -30000.0)
    c["negmask4"] = nm
    lg = np.log1p(-np.exp2(-5.0 - np.arange(8)))
    same = (p[:, None] // 64) == (p[None, :] // 64)
    GT = np.zeros((128, 8, 128))
    for h in range(8):
        GT[:, h, :] = np.where(same, np.exp(lg[h] * np.abs(p[:, None] - p[None, :])), 0.0)
    c["GT"] = GT.reshape(128, 1024)
    gqA = np.zeros((128, 8, 128))
    gqB = np.zeros((128, 8, 128))
    for h in range(8):
        gq = np.exp(lg[h] * ((p % 64) + 1.0))
        gqA[:, h, :] = np.where(p < 64, gq, 0.0)[None, :]
        gqB[:, h, :] = np.where(p >= 64, gq, 0.0)[None, :]
    c["gqA"] = gqA.reshape(128, 1024)
    c["gqB"] = gqB.reshape(128, 1024)
    c["kend"] = np.exp(lg[None, :] * (63.0 - (p % 64))[:, None]) * 0.125
    c["cdt"] = np.tile(np.exp(lg * 64.0)[None, :], (128, 1))
    pos = np.arange(SEQ, dtype=np.float32)
    inv = (np.float32(10000.0) ** (-np.arange(0, 64, 2, dtype=np.float32) / np.float32(64))).astype(np.float32)
    ang = (pos[:, None] * inv[None, :]).astype(np.float32)
    c["cos"] = np.cos(ang).reshape(16, 128, 32).transpose(1, 0, 2).reshape(128, 512)
    c["sin"] = np.sin(ang).reshape(16, 128, 32).transpose(1, 0, 2).reshape(128, 512)
    su = (p[:, None] < p[None, :]).astype(np.float64)
    sl = (p[:, None] > p[None, :]).astype(np.float64)
    iu = (p[:, None] <= p[None, :]).astype(np.float64)
    c["maskA"] = np.concatenate([su, sl, su], axis=1)
    c["maskB"] = np.concatenate([iu, iu], axis=1)
    hs = np.zeros((128, 2))
    hs[:64, 0] = 1
    hs[64:, 1] = 1
    c["headsel"] = hs
    c["blockones"] = same.astype(np.float64)
    out = np.zeros((128, NCST), np.float32)
    for n, (o, w) in CST_OFF.items():
        assert c[n].shape == (128, w), (n, c[n].shape)
        out[:, o:o + w] = c[n]
    return out


def prep_inputs(inp):
    L = inp["w_in"].shape[0]
    tm, fm = _cols()
    g = {}
    g["w_tm"] = np.ascontiguousarray(inp["w_in"][:, :, tm])
    g["w_fm"] = np.ascontiguousarray(inp["w_in"][:, :, fm])
    g["w_branch"] = np.ascontiguousarray(inp["w_branch"])
    g["w_out"] = np.ascontiguousarray(inp["w_out"])
    g["ffn_up"] = np.ascontiguousarray(inp["ffn_up"])
    g["ffn_down"] = np.ascontiguousarray(inp["ffn_down"])
    for nm in ("lru_w_a", "lru_w_i"):
        bd = np.zeros((L, 8, 128, 128), np.float32)
        w = inp[nm]
        for t in range(8):
            bd[:, t, 0:64, 0:64] = w[:, 2 * t]
            bd[:, t, 64:128, 64:128] = w[:, 2 * t + 1]
        g[nm + "_bd"] = bd
    g["w2a2"] = np.ascontiguousarray(np.concatenate([inp["rwkv_w2"], inp["rwkv_a2"]], axis=1))
    g["g2"] = np.ascontiguousarray(inp["rwkv_g2"])
    pfm = np.zeros((L, 128, NPF), np.float32)
    prow = np.zeros((L, NROW), np.float32)
    for l in range(L):
        d = {}
        d["ssm_cw"] = np.stack([_fm(inp["ssm_conv_w"][l, j]) for j in range(4)], axis=2).reshape(128, 64)
        d["ssm_cb"] = _fm(inp["ssm_conv_b"][l])
        d["lru_cw"] = np.stack([_fm(inp["lru_conv_w"][l, j]) for j in range(4)], axis=2).reshape(128, 32)
        d["lru_cb"] = _fm(inp["lru_conv_b"][l])
        d["lru_ba"] = _fm(inp["lru_b_a"][l])
        d["lru_bi"] = _fm(inp["lru_b_i"][l])
        d["lru_lam"] = _fm(inp["lru_lam"][l])
        d["ret_nw"] = _fm(inp["ret_norm"][l])
        mu = inp["rwkv_mu"][l]
        d["mu_fm"] = _fm(np.concatenate([mu[0:2048], mu[3072:3328]]))
        d["w0"] = _fm(inp["rwkv_w0"][l])
        d["a0"] = _fm(inp["rwkv_a0"][l])
        d["k_k"] = _fm(inp["rwkv_k_k"][l])
        d["k_a"] = _fm(inp["rwkv_k_a"][l])
        d["r_k"] = _fm(inp["rwkv_r_k"][l])
        d["ffn_cw"] = np.stack([_fm(inp["ffn_conv_w"][l, j]) for j in range(3)], axis=2).reshape(128, 132)
        d["ffn_cb"] = _fm(inp["ffn_conv_b"][l])
        for n, (o, w) in PFM_OFF.items():
            pfm[l, :, o:o + w] = d[n]
        r = {"norm_mix": inp["norm_mix"][l], "norm_ffn": inp["norm_ffn"][l], "ssm_norm": inp["ssm_norm"][l],
             "mu_v": inp["rwkv_mu"][l][2048:3072], "ln_w": inp["rwkv_ln_w"][l], "ln_b": inp["rwkv_ln_b"][l],
             "dt_bias": inp["ssm_dt_bias"][l], "a_log": inp["ssm_a_log"][l], "ssm_d": inp["ssm_d"][l]}
        for n, (o, w) in PROW_OFF.items():
            prow[l, o:o + w] = r[n]
    g["pfm"] = pfm
    g["prow"] = prow
    g["norm_final"] = np.ascontiguousarray(inp["norm_final"])
    g["cst"] = make_cst()
    return g


SHARED_SHAPES = lambda L: {
    "w_tm": [L, 1024, NTM], "w_fm": [L, 1024, NFM], "w_branch": [L, 4, 1024, 1024], "w_out": [L, 1024, 1024],
    "ffn_up": [L, 1024, 2 * DFF], "ffn_down": [L, DFF, 1024], "lru_w_a_bd": [L, 8, 128, 128],
    "lru_w_i_bd": [L, 8, 128, 128], "w2a2": [L, 128, 1024], "g2": [L, 128, 1024], "pfm": [L, 128, NPF],
    "prow": [L, NROW], "norm_final": [1024], "cst": [128, NCST]}


class KB:
    def __init__(self, nc, L, dbg=False, phases=None):
        self.nc = nc
        self.L = L
        self.dbg = dbg
        self.S = Sched(nc)
        self.phases = phases
        okind = "ExternalOutput" if dbg else "Internal"
        self.x = nc.dram_tensor("x", [SEQ, DM], F32, kind="ExternalInput").ap()
        self.d = {n: nc.dram_tensor(n, s, F32, kind="ExternalInput").ap() for n, s in SHARED_SHAPES(L).items()}
        self.out = nc.dram_tensor("out", [SEQ, DM], F32, kind="ExternalOutput").ap()
        self.xres = nc.dram_tensor("xres", [SEQ, DM], F32, kind=okind).ap()
        self.pTM = nc.dram_tensor("pTM", [SEQ, NTM], F32, kind=okind).ap()
        self.pFM = nc.dram_tensor("pFM", [NFM, SEQ], F32, kind=okind).ap()
        self.ys = nc.dram_tensor("ys", [4, DM, SEQ], BF16, kind=okind).ap()
        self.mTM = nc.dram_tensor("mTM", [SEQ, 1536], F32).ap()
        self.wFM = nc.dram_tensor("wFM", [4, DM, SEQ], BF16).ap()
        self.wTM = nc.dram_tensor("wTM", [SEQ, 8, 2, 128], BF16).ap()
        self.wgate = nc.dram_tensor("wgate", [DM, SEQ], F32).ap()
        self.b_xres = [Buf() for _ in range(NT)]
        self.b_pTM = [[Buf() for _ in range(9)] for _ in range(NT)]
        self.b_pFM = [Buf() for _ in range(90)]
        self.b_ys = [[Buf() for _ in range(NT)] for _ in range(4)]
        self.b_ysF = [Buf() for _ in range(4)]
        self.b_mTM = Buf()
        self.b_wFM = [Buf() for _ in range(8)]
        self.b_wTM = [Buf() for _ in range(8)]
        self.b_wgate = [Buf() for _ in range(8)]
        self.b_out = Buf()
        self.ev = 0
        self.psi = 0
        self.ps_pool = list(range(8))
        self.force_evac = None

    def dma(self, out, in_, reads=(), writes=(), q="sp"):
        return self.S.add(q, lambda e: e.dma_start(out=out, in_=in_), reads=reads, writes=writes, dma=True)

    def mm(self, out, lhsT, rhs, start, stop, reads, writes):
        return self.S.add("pe", lambda e: e.matmul(out, lhsT=lhsT, rhs=rhs, start=start, stop=stop),
                          reads=reads, writes=writes)

    def tr(self, out, in_, reads, writes):
        ident = self.c("ident")
        return self.S.add("pe", lambda e: e.transpose(out, in_, ident), reads=list(reads) + [self.b_cst], writes=writes)

    def act(self, out, in_, func, reads, writes, bias=None, scale=None, accum=None):
        kw = {}
        if bias is not None:
            kw["bias"] = bias
        if scale is not None:
            kw["scale"] = scale
        if accum is not None:
            kw["accum_out"] = accum
        return self.S.add("act", lambda e: e.activation(out=out, in_=in_, func=func, **kw), reads=reads, writes=writes)

    def tt(self, out, in0, in1, op, reads, writes, eng="dve"):
        return self.S.add(eng, lambda e: e.tensor_tensor(out=out, in0=in0, in1=in1, op=op), reads=reads, writes=writes)

    def ts(self, out, in0, s1, s2, op0, op1=None, reads=(), writes=(), eng="dve"):
        if eng in ("pool", "act"):
            if op0 == ALU.mult and (op1 is None or op1 == ALU.add):
                kw = {"scale": s1}
                if op1 is not None:
                    kw["bias"] = s2
                return self.S.add("act", lambda e: e.activation(out=out, in_=in0, func=AF.Identity, **kw),
                                  reads=reads, writes=writes)
            eng = "dve"
        if op1 is None:
            op1 = ALU.add
            s2 = 0.0
        return self.S.add(eng, lambda e: e.tensor_scalar(out=out, in0=in0, scalar1=s1, scalar2=s2, op0=op0, op1=op1),
                          reads=reads, writes=writes)

    def stt(self, out, in0, scalar, in1, op0, op1, reads, writes, eng="dve"):
        eng = "dve"
        return self.S.add(eng, lambda e: e.scalar_tensor_tensor(out=out, in0=in0, scalar=scalar, in1=in1, op0=op0, op1=op1),
                          reads=reads, writes=writes)

    def cp(self, out, in_, reads, writes, eng=None):
        if eng is None and self.force_evac is not None:
            eng = self.force_evac
        if eng is None:
            self.ev ^= 1
            eng = "act" if self.ev else "dve"
        if eng == "act":
            return self.S.add("act", lambda e: e.copy(out=out, in_=in_), reads=reads, writes=writes)
        return self.S.add(eng, lambda e: e.tensor_copy(out=out, in_=in_), reads=reads, writes=writes)

    def memset(self, ap, val, writes, eng="dve"):
        return self.S.add(eng, lambda e: e.memset(ap, val), writes=writes)

    def recip(self, out, in_, reads, writes):
        return self.S.add("dve", lambda e: e.reciprocal(out=out, in_=in_), reads=reads, writes=writes)

    def psum(self):
        pool = self.ps_pool
        i = pool[self.psi % len(pool)]
        self.psi += 1
        return self.ps[i], self.psb[i]

    def c(self, name):
        o, w = CST_OFF[name]
        return self.cst[:, o:o + w]

    def pf(self, name):
        o, w = PFM_OFF[name]
        return self.pfm[:, o:o + w]

    def rowb(self, l, name, dst, buf):
        o, w = PROW_OFF[name]
        self.dma(dst, self.d["prow"][l, o:o + w].partition_broadcast(128), writes=[buf])

    def build(self):
        nc = self.nc
        with ExitStack() as st:
            self.st = st
            A = self.A = Arena(nc, st, "arena", 52000)
            self.ps = [st.enter_context(nc.psum_tensor("ps%d" % i, [128, 512], F32))[:, :] for i in range(8)]
            self.psb = [Buf("ps%d" % i) for i in range(8)]
            self.cst = A.alloc([NCST])
            self.b_cst = Buf("cst")
            self.dma(self.cst, self.d["cst"], writes=[self.b_cst])
            self.identb = A.alloc([128], BF16)
            self.b_identb = Buf()
            self.cp(self.identb, self.c("ident"), [self.b_cst], [self.b_identb], eng="dve")
            self.hT = A.alloc([8, SEQ], BF16)
            self.bh = [Buf("hT%d" % i) for i in range(NT)]
            self.pfm = A.alloc([NPF])
            self.b_pfm = Buf("pfm")
            self.nrow = A.alloc([DM])
            self.b_nrow = Buf("nrow")
            for l in range(self.L):
                self.layer(l)
            self.S.emit()
        return nc

    def on(self, name):
        return self.phases is None or name in self.phases

    def layer(self, l):
        A = self.A
        self.dma(self.pfm, self.d["pfm"][l], writes=[self.b_pfm])
        if l == 0:
            self.rowb(l, "norm_mix", self.nrow, self.b_nrow)
            A.mark()
            xin = [A.alloc([DM]) for _ in range(2)]
            bxin = [Buf() for _ in range(2)]
            self.norm_alloc()
            for i in range(NT):
                xt, bx = xin[i % 2], bxin[i % 2]
                self.dma(xt, self.x[i * 128:(i + 1) * 128, :], writes=[bx])
                self.norm_tile(xt, bx, i)
            A.release()
            self.S.barrier()
        fuse_lru = self.on("proj") and self.on("lru")
        if self.on("proj"):
            self.proj(l, with_lru=fuse_lru)
            self.S.barrier()
        if self.on("mamba"):
            self.mamba(l)
            self.S.barrier()
        if self.on("lru") and not fuse_lru:
            self.lru(l)
            self.S.barrier()
        if self.on("ret"):
            self.retention(l)
            self.S.barrier()
        if self.on("rwkv"):
            self.rwkv(l)
            self.S.barrier()
        if self.on("merge"):
            self.merge(l)
            self.S.barrier()
        if self.on("ffn"):
            self.ffn(l)
            self.S.barrier()

    def norm_alloc(self):
        A = self.A
        self.n_junk = A.alloc([DM])
        self.n_xn = [A.alloc([DM]) for _ in range(2)]
        self.n_ss = [A.alloc([2]) for _ in range(2)]
        self.b_n = [Buf() for _ in range(2)]
        self.b_njunk = Buf()
        self.n_i = 0

    def rms_scale(self, xt, bx, ss, bss, junk, bjunk, n=DM, eps=EPS):
        self.act(junk, xt, AF.Square, [bx], [bjunk, bss], accum=ss[:, 0:1])
        self.act(ss[:, 1:2], ss[:, 0:1], AF.Sqrt, [bss], [bss], scale=1.0 / n, bias=eps)
        self.recip(ss[:, 0:1], ss[:, 1:2], [bss], [bss])

    def norm_tile(self, xt, bx, i, final_out=None):
        j = self.n_i
        self.n_i ^= 1
        xn, ss, bn = self.n_xn[j], self.n_ss[j], self.b_n[j]
        self.rms_scale(xt, bx, ss, bn, self.n_junk, self.b_njunk)
        self.stt(xn, xt, ss[:, 0:1], self.nrow, ALU.mult, ALU.mult, [bx, bn, self.b_nrow], [bn])
        if final_out is not None:
            self.dma(final_out, xn, reads=[bn], writes=[self.b_out])
            return
        for half in range(2):
            ps, bp = self.psum()
            for k in range(4):
                kk = half * 4 + k
                self.tr(ps[:, k * 128:(k + 1) * 128], xn[:, kk * 128:(kk + 1) * 128], [bn], [bp])
            self.cp(self.hT[:, half * 4:half * 4 + 4, i * 128:(i + 1) * 128],
                    ps.rearrange("p (a b) -> p a b", a=4), [bp], [self.bh[i]])

    def wslab_init(self, n=2):
        A = self.A
        self.wsl = [A.alloc([8, 512], BF16) for _ in range(n)]
        self.b_wsl = [Buf() for _ in range(n)]
        self.wsl_i = 0

    def wslab(self, src2d, c0, wd):
        j = self.wsl_i
        self.wsl_i = (self.wsl_i + 1) % len(self.wsl)
        t, b = self.wsl[j], self.b_wsl[j]
        self.dma(t[:, :, 0:wd], src2d.rearrange("(k p) n -> p k n", p=128)[:, :, c0:c0 + wd], writes=[b], q="pool")
        return t, b

    def proj(self, l, with_lru=False):
        A = self.A
        A.mark()
        self.wslab_init(2)
        stg = [A.alloc([512]) for _ in range(4)]
        bstg = [Buf() for _ in range(4)]
        stF = [A.alloc([SEQ]) for _ in range(2)]
        bstF = [Buf() for _ in range(2)]
        hT, bh = self.hT, self.bh
        cnt = {"si": 0, "fi": 0}

        def tm_slab(sidx):
            c0 = sidx * 512
            wd = min(512, NTM - c0)
            wt, bw = self.wslab(self.d["w_tm"][l], c0, wd)
            for i in range(NT):
                ps, bp = self.psum()
                for k in range(8):
                    self.mm(ps[:, 0:wd], hT[:, k, i * 128:(i + 1) * 128], wt[:, k, 0:wd], k == 0, k == 7, [bh[i], bw], [bp])
                s, bs = stg[cnt["si"] % 4], bstg[cnt["si"] % 4]
                cnt["si"] += 1
                self.cp(s[:, 0:wd], ps[:, 0:wd], [bp], [bs])
                self.dma(self.pTM[i * 128:(i + 1) * 128, c0:c0 + wd], s[:, 0:wd], reads=[bs], writes=[self.b_pTM[i][sidx]])

        def fm_slab(sidx):
            c0 = sidx * 512
            wd = min(512, NFM - c0)
            wt, bw = self.wslab(self.d["w_fm"][l], c0, wd)
            for cc in range(wd // 128):
                f = c0 // 128 + cc
                s, bs = stF[cnt["fi"] % 2], bstF[cnt["fi"] % 2]
                cnt["fi"] += 1
                for tb in range(4):
                    ps, bp = self.psum()
                    for k in range(8):
                        self.mm(ps, wt[:, k, cc * 128:(cc + 1) * 128], hT[:, k, tb * 512:(tb + 1) * 512], k == 0, k == 7,
                                [bw] + bh[4 * tb:4 * tb + 4], [bp])
                    self.cp(s[:, tb * 512:(tb + 1) * 512], ps, [bp], [bs])
                self.dma(self.pFM[f * 128:(f + 1) * 128, :], s, reads=[bs], writes=[self.b_pFM[f]])

        def rest():
            for sidx in range(9):
                tm_slab(sidx)
            for sidx in list(range(0, 4)) + list(range(8, 23)):
                fm_slab(sidx)

        for sidx in range(4, 8):
            fm_slab(sidx)
        if with_lru:
            self.ps_pool = [0, 1, 2, 3, 4, 5]
            self.force_evac = "act"
            oa = self.S.record(rest)
            self.force_evac = None
            self.ps_pool = [6, 7]
            ob = self.S.record(lambda: self.lru(l, q="act"))
            self.ps_pool = list(range(8))
            self.S.replay_interleaved(oa, ob)
        else:
            rest()
        A.release()

    def conv_fm(self, out, cin, K, w, b, reads, writes, n, eng="dve"):
        self.ts(out, cin[:, 0:n], w[:, 0:1], b, ALU.mult, ALU.add, reads=reads, writes=writes, eng=eng)
        for j in range(1, K):
            self.stt(out, cin[:, j:j + n], w[:, j:j + 1], out, ALU.mult, ALU.add, reads, writes, eng=eng)

    def mamba(self, l):
        A = self.A
        A.mark()
        cw = self.pf("ssm_cw").rearrange("p (t j) -> p t j", j=4)
        cb = self.pf("ssm_cb")
        BT = A.alloc([4, SEQ], BF16)
        CT = A.alloc([4, SEQ], BF16)
        bBT, bCT = Buf(), Buf()
        A.mark()
        cin = [A.alloc([3 + SEQ]) for _ in range(2)]
        bcin = [Buf() for _ in range(2)]
        acc = [A.alloc([SEQ]) for _ in range(2)]
        bacc = [Buf() for _ in range(2)]
        tst = [A.alloc([NT, 128]) for _ in range(2)]
        btst = [Buf() for _ in range(2)]
        cinb = [A.alloc([3 + SEQ], BF16) for _ in range(2)]
        bcinb = [Buf() for _ in range(2)]
        dgm = [A.alloc([4, 128], BF16) for _ in range(2)]
        bdgm = [Buf() for _ in range(2)]
        for j in range(2):
            self.memset(cin[j][:, 0:3], 0.0, [bcin[j]])
        mTMv = self.mTM.rearrange("(i p) c -> p i c", p=128)
        for f in range(16):
            ci, bci, ac, bac = cin[f % 2], bcin[f % 2], acc[f % 2], bacc[f % 2]
            self.dma(ci[:, 3:3 + SEQ], self.pFM[(F_XBC + f) * 128:(F_XBC + f + 1) * 128, :], reads=[self.b_pFM[F_XBC + f]], writes=[bci])
            cbf, bcbf, dgm_, bdgm_ = cinb[f % 2], bcinb[f % 2], dgm[f % 2], bdgm[f % 2]
            self.cp(cbf, ci, [bci], [bcbf], eng="dve")
            for j in range(4):
                self.ts(dgm_[:, j, :], self.identb, cw[:, f, j:j + 1], None, ALU.mult, reads=[self.b_identb, self.b_pfm], writes=[bdgm_], eng="act")
            for tb in range(4):
                ps, bp = self.psum()
                for j in range(4):
                    self.mm(ps, dgm_[:, j, :], cbf[:, tb * 512 + j:tb * 512 + j + 512], j == 0, j == 3, [bdgm_, bcbf], [bp])
                self.act(ac[:, tb * 512:(tb + 1) * 512], ps, AF.Silu, [bp, self.b_pfm], [bac], bias=cb[:, f:f + 1])
            if f < 12:
                ts_, bts = tst[f % 2], btst[f % 2]
                for q in range(4):
                    ps, bp = self.psum()
                    for k in range(4):
                        i = q * 4 + k
                        self.tr(ps[:, k * 128:(k + 1) * 128], ac[:, i * 128:(i + 1) * 128], [bac], [bp])
                    self.cp(ts_[:, q * 4:q * 4 + 4, :], ps.rearrange("p (a b) -> p a b", a=4), [bp], [bts])
                self.dma(mTMv[:, :, f * 128:(f + 1) * 128], ts_, reads=[bts], writes=[self.b_mTM], q="pool")
            if 8 <= f < 12:
                self.cp(BT[:, f - 8, :], ac, [bac], [bBT], eng="pool")
            if f >= 12:
                self.cp(CT[:, f - 12, :], ac, [bac], [bCT], eng="pool")
        A.release()
        self.S.barrier()
        rows = A.alloc([3, 16])
        brows = Buf()
        self.rowb(l, "dt_bias", rows[:, 0, :], brows)
        self.rowb(l, "a_log", rows[:, 1, :], brows)
        self.rowb(l, "ssm_d", rows[:, 2, :], brows)
        self.act(rows[:, 1, :], rows[:, 1, :], AF.Exp, [brows], [brows])
        self.ts(rows[:, 1, :], rows[:, 1, :], -1.0, None, ALU.mult, reads=[brows], writes=[brows])
        nw = A.alloc([DM])
        bnw = Buf()
        self.rowb(l, "ssm_norm", nw, bnw)
        H = A.alloc([16, 64])
        Hb = A.alloc([16, 64], BF16)
        bH, bHb = Buf(), Buf()
        self.memset(H, 0.0, [bH])
        self.memset(Hb, 0.0, [bHb])
        mt = [A.alloc([1536]) for _ in range(2)]
        zt = [A.alloc([DM]) for _ in range(2)]
        dtr = [A.alloc([16]) for _ in range(2)]
        bld = [Buf() for _ in range(2)]
        sm2 = [A.alloc([8, 16]) for _ in range(2)]
        bsm2 = [Buf() for _ in range(2)]
        dtabc = A.alloc([2, 16, 128])
        bdtabc = Buf()
        decay = A.alloc([16, 128], BF16)
        bdecay = Buf()
        CBs = A.alloc([4, 128], BF16)
        bCBs = Buf()
        Lm2 = [A.alloc([16, 128], BF16) for _ in range(2)]
        bLm2 = [Buf() for _ in range(2)]
        xdt2 = [A.alloc([2, 16, 64], BF16) for _ in range(2)]
        bxdt2 = [Buf() for _ in range(2)]
        Bbf2 = [A.alloc([512], BF16) for _ in range(2)]
        bBbf2 = [Buf() for _ in range(2)]
        y = A.alloc([16, 64])
        t1 = A.alloc([16, 64])
        by, bt1 = Buf(), Buf()
        ss = A.alloc([2])
        yst = [A.alloc([8, 128], BF16) for _ in range(2)]
        byst = [Buf() for _ in range(2)]
        tri, ones, ident, negm = self.c("tri"), self.c("ones"), self.c("ident"), self.c("negmask4")
        bc = self.b_cst
        ysv = self.ys[0].rearrange("(c p) t -> p c t", p=128)
        def stA(i):
                m, z, dr, bl = mt[i % 2], zt[i % 2], dtr[i % 2], bld[i % 2]
                sl = slice(i * 128, (i + 1) * 128)
                sm, bsm, Lm, bLm, xdt, bxdt, Bbf, bBbf = sm2[i % 2], bsm2[i % 2], Lm2[i % 2], bLm2[i % 2], xdt2[i % 2], bxdt2[i % 2], Bbf2[i % 2], bBbf2[i % 2]
                xs = m[:, 0:1024].rearrange("p (h d) -> p h d", h=16)
                t, ab, dt, dta, acum, eac, dte, cd = (sm[:, q, :] for q in range(8))
                self.dma(m, self.mTM[sl, :], reads=[self.b_mTM], writes=[bl])
                self.dma(z, self.pTM[sl, T_Z:T_Z + 1024], reads=self.b_pTM[i][0:2], writes=[bl])
                self.dma(dr, self.pTM[sl, T_DT:T_DT + 16], reads=[self.b_pTM[i][8]], writes=[bl])
                self.tt(t, dr, rows[:, 0, :], ALU.add, [bl, brows], [bsm])
                self.act(ab, t, AF.Abs, [bsm], [bsm])
                self.act(ab, ab, AF.Exp, [bsm], [bsm], scale=-1.0)
                self.act(ab, ab, AF.Ln, [bsm], [bsm], bias=1.0)
                self.stt(dt, t, 0.0, ab, ALU.max, ALU.add, [bsm], [bsm])
                self.tt(dta, dt, rows[:, 1, :], ALU.mult, [bsm, brows], [bsm])
                self.cp(dtabc[:, 0, :, :], dta.unsqueeze(2).to_broadcast([128, 16, 128]), [bsm], [bdtabc], eng="dve")
                self.ts(dtabc[:, 1, :, :], dtabc[:, 0, :, :], -1.0, None, ALU.mult, reads=[bdtabc], writes=[bdtabc], eng="pool")
                ps, bp = self.psum()
                self.mm(ps[:, 0:16], tri, dta, True, True, [bc, bsm], [bp])
                self.mm(ps[:, 16:32], ones, dta, True, True, [bc, bsm], [bp])
                self.cp(acum, ps[:, 0:16], [bp], [bsm], eng="dve")
                self.act(eac, ps[:, 0:16], AF.Exp, [bp], [bsm])
                self.act(cd, ps[:, 16:32], AF.Exp, [bp], [bsm])
                self.tt(dte, ps[:, 16:32], acum, ALU.subtract, [bp, bsm], [bsm])
                self.act(dte, dte, AF.Exp, [bsm], [bsm])
                self.tt(dte, dte, dt, ALU.mult, [bsm], [bsm])
                self.tt(xdt[:, 0, :, :], xs, dt.unsqueeze(2).to_broadcast([128, 16, 64]), ALU.mult, [bl, bsm], [bxdt])
                self.tt(xdt[:, 1, :, :], xs, dte.unsqueeze(2).to_broadcast([128, 16, 64]), ALU.mult, [bl, bsm], [bxdt], eng="pool")
                self.cp(Bbf, m[:, 1024:1536], [bl], [bBbf], eng="pool")
                ps, bp = self.psum()
                for g in range(4):
                    self.mm(ps[:, g * 128:(g + 1) * 128], BT[:, g, sl], CT[:, g, sl], True, True, [bBT, bCT], [bp])
                self.cp(CBs, ps.rearrange("p (a b) -> p a b", a=4), [bp], [bCBs], eng="act")
                for g in range(4):
                    ps, bp = self.psum()
                    for hh in range(4):
                        h = g * 4 + hh
                        o_ = ps[:, hh * 128:(hh + 1) * 128]
                        self.mm(o_, dtabc[:, 0, h, :], tri, True, False, [bdtabc, bc], [bp])
                        self.mm(o_, tri, dtabc[:, 1, h, :], False, False, [bdtabc, bc], [bp])
                        self.mm(o_, ident, negm[:, 0:128], False, True, [bc], [bp])
                    self.act(decay[:, 4 * g:4 * g + 4, :], ps.rearrange("p (a b) -> p a b", a=4), AF.Exp, [bp], [bdecay])
                self.tt(Lm.rearrange("p (g a) b -> p g a b", g=4), decay.rearrange("p (g a) b -> p g a b", g=4),
                        CBs.unsqueeze(2).to_broadcast([128, 4, 4, 128]), ALU.mult, [bdecay, bCBs], [bLm])

        def stB(i):
                m, z, dr, bl = mt[i % 2], zt[i % 2], dtr[i % 2], bld[i % 2]
                sl = slice(i * 128, (i + 1) * 128)
                sm, bsm, Lm, bLm, xdt, bxdt, Bbf, bBbf = sm2[i % 2], bsm2[i % 2], Lm2[i % 2], bLm2[i % 2], xdt2[i % 2], bxdt2[i % 2], Bbf2[i % 2], bBbf2[i % 2]
                xs = m[:, 0:1024].rearrange("p (h d) -> p h d", h=16)
                t, ab, dt, dta, acum, eac, dte, cd = (sm[:, q, :] for q in range(8))
                yv = y.rearrange("p h d -> p (h d)")
                t1v = t1.rearrange("p h d -> p (h d)")
                for half in range(2):
                    psd, bpd = self.psum()
                    pso, bpo = self.psum()
                    for hh in range(8):
                        h = half * 8 + hh
                        self.mm(psd[:, hh * 64:(hh + 1) * 64], Lm[:, h, :], xdt[:, 0, h, :], True, True, [bLm, bxdt], [bpd])
                        self.mm(pso[:, hh * 64:(hh + 1) * 64], CT[:, h // 4, sl], Hb[:, h, :], True, True, [bCT, bHb], [bpo])
                    hs = slice(half * 8, half * 8 + 8)
                    self.tt(t1[:, hs, :], pso.rearrange("p (h d) -> p h d", h=8), eac[:, hs].unsqueeze(2).to_broadcast([128, 8, 64]),
                            ALU.mult, [bpo, bsm], [bt1])
                    self.tt(y[:, hs, :], psd.rearrange("p (h d) -> p h d", h=8), t1[:, hs, :], ALU.add, [bpd, bt1], [by])
                self.tt(t1, xs, rows[:, 2, :].unsqueeze(2).to_broadcast([128, 16, 64]), ALU.mult, [bl, brows], [bt1], eng="pool")
                self.tt(y, y, t1, ALU.add, [by, bt1], [by])
                for half in range(2):
                    pss, bps = self.psum()
                    for hh in range(8):
                        h = half * 8 + hh
                        g = h // 4
                        self.mm(pss[:, hh * 64:(hh + 1) * 64], Bbf[:, g * 128:(g + 1) * 128], xdt[:, 1, h, :], True, True, [bBbf, bxdt], [bps])
                    hs = slice(half * 8, half * 8 + 8)
                    self.tt(H[:, hs, :], H[:, hs, :], cd[:, hs].unsqueeze(2).to_broadcast([128, 8, 64]), ALU.mult, [bH, bsm, bpo], [bH])
                    self.tt(H[:, hs, :], H[:, hs, :], pss.rearrange("p (h d) -> p h d", h=8), ALU.add, [bH, bps], [bH])
                self.cp(Hb, H, [bH], [bHb], eng="act")
                self.act(z, z, AF.Silu, [bl], [bl])
                self.tt(yv, yv, z, ALU.mult, [by, bl], [by])
                self.rms_scale(yv, by, ss, bt1, t1v, bt1)
                self.stt(yv, yv, ss[:, 0:1], nw, ALU.mult, ALU.mult, [by, bt1, bnw], [by])
                yo, byo = yst[i % 2], byst[i % 2]
                for half in range(2):
                    ps, bp = self.psum()
                    for k in range(4):
                        kk = half * 4 + k
                        self.tr(ps[:, k * 128:(k + 1) * 128], yv[:, kk * 128:(kk + 1) * 128], [by], [bp])
                    self.cp(yo[:, half * 4:half * 4 + 4, :], ps.rearrange("p (a b) -> p a b", a=4), [bp], [byo])
                self.dma(ysv[:, :, sl], yo, reads=[byo], writes=[self.b_ys[0][i]], q="pool")

        PA, PB = [0, 1, 2, 3], [4, 5, 6, 7]
        self.ps_pool = PA
        stA(0)
        for i in range(NT):
            self.ps_pool = PB
            ob = self.S.record(lambda: stB(i))
            oa = []
            if i + 1 < NT:
                self.ps_pool = PA
                oa = self.S.record(lambda: stA(i + 1))
            self.S.replay_interleaved(oa, ob)
        self.ps_pool = list(range(8))
        A.release()

    def lru(self, l, q=None):
        A = self.A
        A.mark()
        cw = self.pf("lru_cw").rearrange("p (t j) -> p t j", j=4)
        cb, ba, bi, lam = self.pf("lru_cb"), self.pf("lru_ba"), self.pf("lru_bi"), self.pf("lru_lam")
        wa = A.alloc([8, 128], BF16)
        wi = A.alloc([8, 128], BF16)
        bwa = Buf()
        self.dma(wa, self.d["lru_w_a_bd"][l].rearrange("t p n -> p t n"), writes=[bwa], q="pool")
        self.dma(wi, self.d["lru_w_i_bd"][l].rearrange("t p n -> p t n"), writes=[bwa], q="pool")
        cf = A.alloc([2, 8])
        bcf = Buf()
        tmp = A.alloc([2, 8])
        self.act(tmp[:, 0, :], lam, AF.Abs, [self.b_pfm], [bcf])
        self.act(tmp[:, 0, :], tmp[:, 0, :], AF.Exp, [bcf], [bcf], scale=-1.0)
        self.act(tmp[:, 0, :], tmp[:, 0, :], AF.Ln, [bcf], [bcf], bias=1.0)
        self.ts(tmp[:, 1, :], lam, -1.0, 0.0, ALU.mult, ALU.max, reads=[self.b_pfm], writes=[bcf])
        self.tt(tmp[:, 0, :], tmp[:, 0, :], tmp[:, 1, :], ALU.add, [bcf], [bcf])
        self.ts(cf[:, 0, :], tmp[:, 0, :], -8.0, None, ALU.mult, reads=[bcf], writes=[bcf])
        self.ts(cf[:, 1, :], tmp[:, 0, :], -16.0, None, ALU.mult, reads=[bcf], writes=[bcf])
        cin = [A.alloc([3 + SEQ]) for _ in range(2)]
        gin = [A.alloc([SEQ]) for _ in range(2)]
        bin_ = [Buf() for _ in range(2)]
        for j in range(2):
            self.memset(cin[j][:, 0:3], 0.0, [bin_[j]])
        xc = A.alloc([SEQ])
        xcb = A.alloc([SEQ], BF16)
        r = A.alloc([SEQ])
        ii = A.alloc([SEQ])
        a = A.alloc([SEQ])
        u = A.alloc([SEQ])
        hh_ = r
        g2 = A.alloc([SEQ])
        yo = [A.alloc([SEQ], BF16) for _ in range(2)]
        byo = [Buf() for _ in range(2)]
        b_xc, b_xcb, b_r, b_i, b_a, b_u, b_h, b_g = (Buf() for _ in range(8))
        b_h = b_r
        bpf = self.b_pfm
        for f in range(8):
            ci, gi, bl = cin[f % 2], gin[f % 2], bin_[f % 2]
            self.dma(ci[:, 3:3 + SEQ], self.pFM[(F_LX + f) * 128:(F_LX + f + 1) * 128, :], reads=[self.b_pFM[F_LX + f]], writes=[bl], q=q or "sp")
            self.dma(gi, self.pFM[(F_LG + f) * 128:(F_LG + f + 1) * 128, :], reads=[self.b_pFM[F_LG + f]], writes=[bl], q=q or "sp")
            self.conv_fm(xc, ci, 4, cw[:, f, :], cb[:, f:f + 1], [bl, bpf], [b_xc], SEQ)
            self.cp(xcb, xc, [b_xc], [b_xcb], eng="act")
            for tb in range(4):
                ts_ = slice(tb * 512, (tb + 1) * 512)
                ps, bp = self.psum()
                self.mm(ps, wa[:, f, :], xcb[:, ts_], True, True, [bwa, b_xcb], [bp])
                self.act(r[:, ts_], ps, AF.Sigmoid, [bp, bpf], [b_r], bias=ba[:, f:f + 1])
                ps, bp = self.psum()
                self.mm(ps, wi[:, f, :], xcb[:, ts_], True, True, [bwa, b_xcb], [bp])
                self.act(ii[:, ts_], ps, AF.Sigmoid, [bp, bpf], [b_i], bias=bi[:, f:f + 1])
            self.act(a, r, AF.Exp, [b_r, bcf], [b_a], scale=cf[:, 0, f:f + 1])
            self.act(u, r, AF.Exp, [b_r, bcf], [b_u], scale=cf[:, 1, f:f + 1])
            self.act(u, u, AF.Sqrt, [b_u], [b_u], scale=-1.0, bias=1.0)
            self.tt(ii, ii, xc, ALU.mult, [b_i, b_xc], [b_i], eng="pool")
            self.tt(u, u, ii, ALU.mult, [b_u, b_i], [b_u])
            self.S.add("dve", lambda e, a=a, u=u, o=hh_: e.tensor_tensor_scan(out=o, data0=a, data1=u, initial=0.0, op0=ALU.mult, op1=ALU.add),
                       reads=[b_a, b_u], writes=[b_h])
            self.tt(g2, gi, gi, ALU.mult, [bl], [b_g], eng="pool")
            self.ts(g2, g2, 0.044715, 1.0, ALU.mult, ALU.add, reads=[b_g], writes=[b_g], eng="pool")
            self.tt(g2, g2, gi, ALU.mult, [b_g, bl], [b_g], eng="pool")
            self.act(g2, g2, AF.Sigmoid, [b_g], [b_g], scale=1.5957691216057308)
            self.tt(g2, g2, gi, ALU.mult, [b_g, bl], [b_g], eng="pool")
            self.tt(yo[f % 2], hh_, g2, ALU.mult, [b_h, b_g], [byo[f % 2]])
            self.dma(self.ys[1][f * 128:(f + 1) * 128, :], yo[f % 2], reads=[byo[f % 2]], writes=[self.b_ysF[1]], q=q or "pool")
        A.release()

    def retention(self, l):
        A = self.A
        A.mark()
        nwt = self.pf("ret_nw")
        cosT = self.c("cos").rearrange("p (i d) -> p i d", i=16)
        sinT = self.c("sin").rearrange("p (i d) -> p i d", i=16)
        GT = self.c("GT")
        gqA = self.c("gqA")[0:64, :].rearrange("p (f t) -> p f t", f=8)
        gqB = self.c("gqB")[0:64, :].rearrange("p (f t) -> p f t", f=8)
        kend, cdt = self.c("kend"), self.c("cdt")
        bc = self.b_cst
        qk = [A.alloc([16, 64]) for _ in range(2)]
        vt = [A.alloc([DM]) for _ in range(2)]
        gt = [A.alloc([8, 128]) for _ in range(2)]
        bl = [Buf() for _ in range(2)]
        rot = A.alloc([16, 64])
        tmp = A.alloc([16, 32])
        brot = Buf()
        ksc = A.alloc([8, 64], BF16)
        bksc = Buf()
        qT = A.alloc([8, 128], BF16, parts=64)
        kT = A.alloc([8, 128], BF16, parts=64)
        qTA = A.alloc([8, 128], BF16, parts=64)
        qTB = A.alloc([8, 128], BF16, parts=64)
        bqT = Buf()
        vb = A.alloc([DM], BF16)
        bvb = Buf()
        ST = A.alloc([8, 128], BF16)
        bST = Buf()
        R = A.alloc([3, 8, 128], parts=64)
        Rb = A.alloc([3, 8, 128], BF16, parts=64)
        bR, bRb = Buf(), Buf()
        self.memset(R, 0.0, [bR])
        self.memset(Rb, 0.0, [bRb])
        ysb = A.alloc([8, 128])
        ysq = A.alloc([8, 128])
        bys = Buf()
        mean = A.alloc([8, 128])
        var = A.alloc([8, 128])
        bmv = Buf()
        yo = [A.alloc([8, 128], BF16) for _ in range(2)]
        byo = [Buf() for _ in range(2)]
        onesm = A.alloc([128])
        bon = Buf()
        self.ts(onesm, self.c("ones"), 1.0 / 128.0, None, ALU.mult, reads=[bc], writes=[bon])
        ysv = self.ys[2].rearrange("(c p) t -> p c t", p=128)
        gv = self.pFM[F_RG * 128:(F_RG + 8) * 128, :].rearrange("(c p) t -> p c t", p=128)
        pend_st = None
        for i in range(NT):
            q_, v_, g_, b_ = qk[i % 2], vt[i % 2], gt[i % 2], bl[i % 2]
            sl = slice(i * 128, (i + 1) * 128)
            self.dma(q_.rearrange("p h d -> p (h d)"), self.pTM[sl, T_QK:T_QK + 1024], reads=self.b_pTM[i][2:4], writes=[b_])
            self.dma(v_, self.pTM[sl, T_RV:T_RV + 1024], reads=self.b_pTM[i][4:6], writes=[b_])
            self.dma(g_, gv[:, :, sl], reads=self.b_pFM[F_RG:F_RG + 8], writes=[b_])
            if pend_st is not None:
                pend_st()
                pend_st = None
            x1, x2 = q_[:, :, 0:32], q_[:, :, 32:64]
            cb_ = cosT[:, i, :].unsqueeze(1).to_broadcast([128, 16, 32])
            sb_ = sinT[:, i, :].unsqueeze(1).to_broadcast([128, 16, 32])
            self.tt(rot[:, :, 0:32], x1, cb_, ALU.mult, [b_, bc], [brot])
            self.tt(tmp, x2, sb_, ALU.mult, [b_, bc], [brot], eng="pool")
            self.tt(rot[:, :, 0:32], rot[:, :, 0:32], tmp, ALU.subtract, [brot], [brot])
            self.tt(rot[:, :, 32:64], x1, sb_, ALU.mult, [b_, bc], [brot])
            self.tt(tmp, x2, cb_, ALU.mult, [b_, bc], [brot], eng="pool")
            self.tt(rot[:, :, 32:64], rot[:, :, 32:64], tmp, ALU.add, [brot], [brot])
            self.tt(ksc, rot[:, 8:16, :], kend.unsqueeze(2).to_broadcast([128, 8, 64]), ALU.mult, [brot, bc], [bksc])
            self.cp(vb, v_, [b_], [bvb], eng="act")
            rv = rot.rearrange("p h d -> p (h d)")
            for quad in range(4):
                ps, bp = self.psum()
                for k in range(4):
                    hh = quad * 4 + k
                    self.tr(ps[0:64, k * 128:(k + 1) * 128], rv[:, hh * 64:(hh + 1) * 64], [brot], [bp])
                psv = ps[0:64, :].rearrange("p (a b) -> p a b", a=4)
                if quad < 2:
                    hs = slice(quad * 4, quad * 4 + 4)
                    self.cp(qT[:, hs, :], psv, [bp], [bqT], eng="act")
                    self.tt(qTA[:, hs, :], psv, gqA[:, hs, :], ALU.mult, [bp, bc], [bqT])
                    self.tt(qTB[:, hs, :], psv, gqB[:, hs, :], ALU.mult, [bp, bc], [bqT])
                else:
                    hs = slice((quad - 2) * 4, (quad - 2) * 4 + 4)
                    self.ts(kT[:, hs, :], psv, 0.125, None, ALU.mult, reads=[bp], writes=[bqT])
            for half in range(2):
                ps, bp = self.psum()
                for hh in range(4):
                    h = half * 4 + hh
                    self.mm(ps[:, hh * 128:(hh + 1) * 128], kT[:, h, :], qT[:, h, :], True, True, [bqT], [bp])
                self.tt(ST[:, half * 4:half * 4 + 4, :].rearrange("p a b -> p (a b)"), ps, GT[:, half * 512:(half + 1) * 512],
                        ALU.mult, [bp, bc], [bST])
            for cch in range(2):
                pr = slice(cch * 64, (cch + 1) * 64)
                for half in range(2):
                    ps, bp = self.psum()
                    for hh in range(4):
                        h = half * 4 + hh
                        self.mm(ps[0:64, hh * 128:(hh + 1) * 128], ksc[pr, h, :], vb[pr, h * 128:(h + 1) * 128], True, True, [bksc, bvb], [bp])
                    hs = slice(half * 4, half * 4 + 4)
                    self.tt(R[:, cch + 1, hs, :], R[:, cch, hs, :], cdt[0:64, hs].unsqueeze(2).to_broadcast([64, 4, 128]), ALU.mult, [bR, bc], [bR])
                    self.tt(R[:, cch + 1, hs, :], R[:, cch + 1, hs, :], ps[0:64, :].rearrange("p (a b) -> p a b", a=4), ALU.add, [bR, bp], [bR])
                self.cp(Rb[:, cch + 1, :, :], R[:, cch + 1, :, :], [bR], [bRb], eng="act")
            for half in range(2):
                ps, bp = self.psum()
                for hh in range(4):
                    h = half * 4 + hh
                    o = ps[:, hh * 128:(hh + 1) * 128]
                    self.mm(o, vb[:, h * 128:(h + 1) * 128], ST[:, h, :], True, False, [bvb, bST], [bp])
                    self.mm(o, Rb[:, 0, h, :], qTA[:, h, :], False, False, [bRb, bqT], [bp])
                    self.mm(o, Rb[:, 1, h, :], qTB[:, h, :], False, True, [bRb, bqT], [bp])
                hs = slice(half * 4, half * 4 + 4)
                self.cp(ysb[:, hs, :], ps.rearrange("p (a b) -> p a b", a=4), [bp], [bys], eng="dve")
                self.act(ysq[:, hs, :], ps.rearrange("p (a b) -> p a b", a=4), AF.Square, [bp], [bys])
            self.cp(R[:, 0, :, :], R[:, 2, :, :], [bR], [bR], eng="pool")
            self.cp(Rb[:, 0, :, :], Rb[:, 2, :, :], [bRb], [bRb], eng="pool")
            for half in range(2):
                hs = slice(half * 4, half * 4 + 4)
                ps, bp = self.psum()
                self.mm(ps, onesm, ysb[:, hs, :].rearrange("p a b -> p (a b)"), True, True, [bon, bys], [bp])
                self.cp(mean[:, hs, :], ps.rearrange("p (a b) -> p a b", a=4), [bp], [bmv], eng="act")
                ps2, bp2 = self.psum()
                self.mm(ps2, onesm, ysq[:, hs, :].rearrange("p a b -> p (a b)"), True, True, [bon, bys], [bp2])
                self.tt(var[:, hs, :], mean[:, hs, :], mean[:, hs, :], ALU.mult, [bmv], [bmv], eng="pool")
                self.tt(var[:, hs, :], ps2.rearrange("p (a b) -> p a b", a=4), var[:, hs, :], ALU.subtract, [bp2, bmv], [bmv])
            self.act(var, var, AF.Ln, [bmv], [bmv], bias=EPS)
            self.act(var, var, AF.Exp, [bmv], [bmv], scale=-0.5)
            self.tt(ysb, ysb, mean, ALU.subtract, [bys, bmv], [bys])
            self.tt(ysb, ysb, var, ALU.mult, [bys, bmv], [bys])
            self.tt(ysb, ysb, nwt.unsqueeze(2).to_broadcast([128, 8, 128]), ALU.mult, [bys, self.b_pfm], [bys], eng="pool")
            self.act(g_, g_, AF.Silu, [b_], [b_])
            self.tt(yo[i % 2], ysb, g_, ALU.mult, [bys, b_], [byo[i % 2]])
            pend_st = (lambda i=i, sl=sl: self.dma(ysv[:, :, sl], yo[i % 2], reads=[byo[i % 2]], writes=[self.b_ys[2][i]], q="sp"))
        pend_st()
        A.release()

    def rwkv(self, l):
        A = self.A
        bc = self.b_cst
        A.mark()
        mu = self.pf("mu_fm")
        w0, a0, k_k, k_a, r_k = (self.pf(n) for n in ("w0", "a0", "k_k", "k_a", "r_k"))
        w2a2 = A.alloc([DM], BF16)
        g2w = A.alloc([DM], BF16)
        bwl = Buf()
        self.dma(w2a2, self.d["w2a2"][l], writes=[bwl], q="pool")
        self.dma(g2w, self.d["g2"][l], writes=[bwl], q="pool")
        omka = A.alloc([8])
        bomka = Buf()
        self.ts(omka, k_a, -1.0, 1.0, ALU.mult, ALU.add, reads=[self.b_pfm], writes=[bomka])
        ecd = A.alloc([2, 16])
        becd = [Buf() for _ in range(2)]
        self.ecdD = self.nc.dram_tensor("ecdD%d" % l, [DM, 16], F32).ap()
        becdD = Buf()
        rks = A.alloc([16, 16])
        brks = Buf()
        A.mark()
        HS = SEQ // 2
        NH = NT // 2
        rmask = A.alloc([NH, 128], BF16)
        brm = Buf()
        self.memset(rmask, 1.0, [brm])
        self.memset(rmask[:, :, 0:1], 0.0, [brm])
        rmaskf = rmask.rearrange("p a b -> p (a b)")
        lo = A.alloc([SEQ], BF16)
        gl = A.alloc([SEQ], BF16)
        blo = Buf()
        lin = [[A.alloc([1 + HS]) for _ in range(2)] for _ in range(2)]
        blin = [[Buf() for _ in range(2)] for _ in range(2)]
        Wt = [[A.alloc([HS]) for _ in range(9)] for _ in range(2)]
        bWt = [[Buf() for _ in range(9)] for _ in range(2)]
        tstt = [A.alloc([NH, 2, 128], BF16) for _ in range(2)]
        btstt = [Buf() for _ in range(2)]
        fstt = [[A.alloc([HS], BF16) for _ in range(4)] for _ in range(2)]
        bfstt = [[Buf() for _ in range(4)] for _ in range(2)]
        ecd2 = [A.alloc([NH]) for _ in range(2)]
        becd2 = [Buf() for _ in range(2)]
        bpf = self.b_pfm
        wTMv = self.wTM.rearrange("(i p) f c d -> p i f c d", p=128)

        def mixload(ft, mcol, dst, bdst, st, k, half):
            li, bl = lin[st][k], blin[st][k]
            if half == 0:
                self.memset(li[:, 0:1], 0.0, [bl])
                self.dma(li[:, 1:1 + HS], self.pFM[ft * 128:(ft + 1) * 128, 0:HS], reads=[self.b_pFM[ft]], writes=[bl])
            else:
                self.dma(li, self.pFM[ft * 128:(ft + 1) * 128, HS - 1:SEQ], reads=[self.b_pFM[ft]], writes=[bl])
            self.tt(dst, li[:, 0:HS], li[:, 1:1 + HS], ALU.subtract, [bl], [bdst], eng="pool")
            self.stt(dst, dst, mu[:, mcol:mcol + 1], li[:, 1:1 + HS], ALU.mult, ALU.add, [bdst, bl, bpf], [bdst])

        for half in range(2):
            hsl = slice(half * HS, (half + 1) * HS)
            mixload(F_LO, 16, Wt[half][0], bWt[half][0], half, 0, half)
            self.act(lo[0:64, hsl], Wt[half][0][0:64, :], AF.Tanh, [bWt[half][0]], [blo])
            self.cp(lo[64:128, hsl], Wt[half][0][64:128, :], [bWt[half][0]], [blo], eng="dve")
            mixload(F_GL, 17, Wt[half][1], bWt[half][1], half, 1, half)
            self.act(gl[:, hsl], Wt[half][1], AF.Sigmoid, [bWt[half][1]], [blo])

        def unit(f, half, st):
            rm, km, lw, aa, kk, t0, t1, t2, t3 = Wt[st]
            b_rm, b_km, b_lw, b_aa, b_kk, b_t0, b_t1, b_t2, b_t3 = bWt[st]
            fst, bfst, tst, btst = fstt[st], bfstt[st], tstt[st], btstt[st]
            fs = slice(f * 128, (f + 1) * 128)
            T0 = half * HS
            hsl = slice(T0, T0 + HS)
            mixload(F_WR + f, f, rm, b_rm, st, 0, half)
            mixload(F_WK + f, 8 + f, km, b_km, st, 1, half)
            for fn_ in defer[st]:
                fn_()
            defer[st] = []
            for tb in range(2):
                ts_ = slice(tb * 512, (tb + 1) * 512)
                gs_ = slice(T0 + tb * 512, T0 + (tb + 1) * 512)
                ps, bp = self.psum()
                self.mm(ps, w2a2[0:64, fs], lo[0:64, gs_], True, True, [bwl, blo], [bp])
                self.act(lw[:, ts_], ps, AF.Sigmoid, [bp, bpf], [b_lw], bias=w0[:, f:f + 1])
                ps, bp = self.psum()
                self.mm(ps, w2a2[64:128, fs], lo[64:128, gs_], True, True, [bwl, blo], [bp])
                self.act(aa[:, ts_], ps, AF.Sigmoid, [bp, bpf], [b_aa], bias=a0[:, f:f + 1])
                ps, bp = self.psum()
                self.mm(ps, g2w[:, fs], gl[:, gs_], True, True, [bwl, blo], [bp])
                self.cp(t0[:, ts_], ps, [bp], [b_t0])
            self.dma(self.wgate[fs, hsl], t0, reads=[b_t0], writes=[self.b_wgate[f]], q="pool")
            self.ts(lw, lw, -0.6065306597126334, None, ALU.mult, reads=[b_lw], writes=[b_lw], eng="pool")
            self.ts(kk, km, k_k[:, f:f + 1], None, ALU.mult, reads=[b_km, bpf], writes=[b_kk], eng="act")
            self.act(t1, kk, AF.Square, [b_kk], [b_t1])
            for tb in range(2):
                ts_ = slice(tb * 512, (tb + 1) * 512)
                ps, bp = self.psum()
                self.mm(ps, self.c("blockones"), t1[:, ts_], True, True, [bc, b_t1], [bp])
                self.ts(t2[:, ts_], ps, 1e-24, None, ALU.max, reads=[bp], writes=[b_t2])
            self.act(t2, t2, AF.Ln, [b_t2], [b_t2])
            self.act(t2, t2, AF.Exp, [b_t2], [b_t2], scale=-0.5)
            self.tt(kk, kk, t2, ALU.mult, [b_kk, b_t2], [b_kk])
            self.ts(t3, aa, k_a[:, f:f + 1], omka[:, f:f + 1], ALU.mult, ALU.add, reads=[b_aa, bpf, bomka], writes=[b_t3], eng="act")
            self.tt(km, km, t3, ALU.mult, [b_km, b_t3], [b_km])
            self.tt(t1, rm, km, ALU.mult, [b_rm, b_km], [b_t1], eng="pool")
            self.ts(t1, t1, r_k[:, f:f + 1], None, ALU.mult, reads=[b_t1, bpf], writes=[b_t1], eng="act")
            ps, bp = self.psum()
            for i in range(NH):
                self.mm(ps[:, 2 * i:2 * i + 2], t1[:, i * 128:(i + 1) * 128], self.c("headsel"), True, True, [b_t1, bc], [bp])
            self.cp(rks[:, half * NH:(half + 1) * NH, 2 * f:2 * f + 2], ps[:, 0:2 * NH].rearrange("p (i j) -> p i j", j=2), [bp], [brks], eng="act")
            cl, b_cl = t2, b_t2
            self.S.add("dve", lambda e, o=cl, m=rmaskf, d1=lw: e.tensor_tensor_scan(out=o, data0=m, data1=d1, initial=0.0, op0=ALU.mult, op1=ALU.add),
                       reads=[b_lw, brm], writes=[b_cl])
            clv = cl.rearrange("p (i t) -> p i t", i=NH)
            self.act(ecd2[st], clv[:, :, 127], AF.Exp, [b_cl], [becd2[st]])
            defer[st].append(lambda f=f, half=half, st=st, fs=fs: self.dma(self.ecdD[fs, half * NH:(half + 1) * NH], ecd2[st], reads=[becd2[st]], writes=[becdD], q="sp"))
            self.tt(aa, aa, kk, ALU.mult, [b_aa, b_kk], [b_aa])
            self.act(t3, cl, AF.Exp, [b_cl], [b_t3])
            self.tt(fst[0], rm, t3, ALU.mult, [b_rm, b_t3], [bfst[0]], eng="pool")
            self.tt(t0, cl, lw, ALU.subtract, [b_cl, b_lw], [b_t0], eng="pool")
            self.act(t0, t0, AF.Exp, [b_t0], [b_t0])
            self.stt(fst[1], kk, -1.0, t0, ALU.mult, ALU.mult, [b_kk, b_t0], [bfst[1]])
            self.act(t3, cl, AF.Exp, [b_cl], [b_t3], scale=-1.0)
            self.tt(fst[2], aa, t3, ALU.mult, [b_aa, b_t3], [bfst[2]])
            self.tt(fst[3], km, t3, ALU.mult, [b_km, b_t3], [bfst[3]], eng="pool")
            for q in range(4):
                defer[st].append(lambda q=q, f=f, fs=fs, hsl=hsl, fst=fst, bfst=bfst: self.dma(self.wFM[q][fs, hsl], fst[q], reads=[bfst[q]], writes=[self.b_wFM[f]], q="sp"))
            t1v = t1.rearrange("p (i t) -> p i t", i=NH)
            self.tt(t1v, clv[:, :, 127:128].to_broadcast([128, NH, 128]), clv, ALU.subtract, [b_cl], [b_t1])
            self.act(t1, t1, AF.Exp, [b_t1], [b_t1])
            self.tt(aa, aa, t1, ALU.mult, [b_aa, b_t1], [b_aa])
            self.tt(km, km, t1, ALU.mult, [b_km, b_t1], [b_km], eng="pool")
            for cidx, (src, bsrc) in enumerate(((aa, b_aa), (km, b_km))):
                for q in range(2):
                    ps, bp = self.psum()
                    for k in range(4):
                        i = q * 4 + k
                        self.tr(ps[:, k * 128:(k + 1) * 128], src[:, i * 128:(i + 1) * 128], [bsrc], [bp])
                    self.cp(tst[:, q * 4:q * 4 + 4, cidx, :], ps.rearrange("p (a b) -> p a b", a=4), [bp], [btst])
            defer[st].append(lambda f=f, half=half, tst=tst, btst=btst: self.dma(wTMv[:, half * NH:(half + 1) * NH, f, :, :], tst, reads=[btst], writes=[self.b_wTM[f]], q="sp"))

        defer = [[], []]
        opsA, opsB = [], []
        for f in range(8):
            self.ps_pool = [0, 1, 2, 3]
            opsA += self.S.record(lambda: unit(f, 0, 0))
            self.ps_pool = [4, 5, 6, 7]
            opsB += self.S.record(lambda: unit(f, 1, 1))
        self.ps_pool = list(range(8))
        n0 = (len(opsA) // 8) // 2 if SKEW else 0
        for op in opsA[:n0]:
            self.S.add(*op)
        self.S.replay_interleaved(opsA[n0:], opsB)
        for st_ in range(2):
            for fn_ in defer[st_]:
                fn_()
        A.release()
        self.S.barrier()
        muv = A.alloc([DM])
        lnw = A.alloc([DM])
        lnb = A.alloc([DM])
        brow = Buf()
        self.rowb(l, "mu_v", muv, brow)
        self.rowb(l, "ln_w", lnw, brow)
        self.rowb(l, "ln_b", lnb, brow)
        H = A.alloc([16, 64], parts=64)
        Hb = A.alloc([16, 64], BF16, parts=64)
        ecdh = A.alloc([16, 16], parts=64)
        becdh = Buf()
        self.dma(ecdh, self.ecdD.rearrange("(h p) c -> p h c", p=64), reads=[becdD], writes=[becdh])
        bH = [Buf() for _ in range(16)]
        bHb = [Buf() for _ in range(16)]
        fm4 = [A.alloc([4, 16, 128], BF16, parts=64)]
        bF4 = Buf()
        tm2 = [A.alloc([8, 2, 128], BF16) for _ in range(2)]
        vc = [A.alloc([DM]) for _ in range(2)]
        vp = [A.alloc([DM]) for _ in range(2)]
        gtl = [A.alloc([8, 128]) for _ in range(2)]
        bld = [Buf() for _ in range(2)]
        vf = A.alloc([DM])
        vb = A.alloc([DM], BF16)
        bv = Buf()
        MA = [A.alloc([3, 128], BF16) for _ in range(16)]
        MBp = [A.alloc([2, 2, 128], BF16) for _ in range(8)]
        XYp = [[A.alloc([2, 2, 128], BF16) for _ in range(2)] for _ in range(8)]
        Zq = [[A.alloc([4, 128], BF16) for _ in range(2)] for _ in range(4)]
        Xg8 = [A.alloc([8, 64], BF16) for _ in range(2)]
        U8 = [A.alloc([8, 64], BF16) for _ in range(2)]
        bM = [Buf() for _ in range(16)]
        bMB = [Buf() for _ in range(8)]
        bXY = [[Buf() for _ in range(2)] for _ in range(8)]
        bZ = [[Buf() for _ in range(2)] for _ in range(4)]
        bXg = [Buf() for _ in range(2)]
        bU = [Buf() for _ in range(2)]
        bHg = [Buf() for _ in range(2)]
        bHbg = [Buf() for _ in range(2)]
        otm = A.alloc([16, 64])
        bo = Buf()
        st2 = A.alloc([4, 16])
        bst2 = Buf()
        osq = A.alloc([16, 64])
        yo = [A.alloc([8, 128], BF16) for _ in range(2)]
        byo = [Buf() for _ in range(2)]
        maskA, maskB, identb = self.c("maskA"), self.c("maskB"), self.identb
        maskAb = A.alloc([384], BF16)
        maskBb = A.alloc([256], BF16)
        bmk = Buf()
        self.cp(maskAb, maskA, [bc], [bmk], eng="dve")
        self.cp(maskBb, maskB, [bc], [bmk], eng="dve")
        MAr = [A.alloc([384], BF16) for _ in range(2)]
        MBr = [A.alloc([2, 256], BF16) for _ in range(2)]
        bMAr = [Buf() for _ in range(2)]
        bMBr = [Buf() for _ in range(2)]
        wFMv = [self.wFM[q].rearrange("(h p) t -> p h t", p=64) for q in range(4)]
        gatev = self.wgate.rearrange("(c p) t -> p c t", p=128)
        ysv = self.ys[3].rearrange("(c p) t -> p c t", p=128)
        pend_st = None
        for i in range(NT):
            F4, T2, vcu, vpr, gt_, bl = fm4[0], tm2[i % 2], vc[i % 2], vp[i % 2], gtl[i % 2], bld[i % 2]
            sl = slice(i * 128, (i + 1) * 128)
            for q in range(4):
                self.dma(F4[:, q, :, :], wFMv[q][:, :, sl], reads=self.b_wFM, writes=[bF4])
            self.dma(T2, self.wTM[sl, :, :, :], reads=self.b_wTM, writes=[bl])
            self.dma(vcu, self.pTM[sl, T_WV:T_WV + 1024], reads=self.b_pTM[i][6:8], writes=[bl])
            if i == 0:
                self.memset(vpr[0:1, :], 0.0, [bl])
                self.dma(vpr[1:128, :], self.pTM[0:127, T_WV:T_WV + 1024], reads=self.b_pTM[0][6:8], writes=[bl])
            else:
                self.dma(vpr, self.pTM[i * 128 - 1:(i + 1) * 128 - 1, T_WV:T_WV + 1024],
                         reads=self.b_pTM[i][6:8] + self.b_pTM[i - 1][6:8], writes=[bl])
            self.dma(gt_, gatev[:, :, sl], reads=self.b_wgate, writes=[bl])
            if pend_st is not None:
                pend_st()
                pend_st = None
            self.tt(vf, vpr, vcu, ALU.subtract, [bl], [bv], eng="pool")
            self.tt(vf, vf, muv, ALU.mult, [bv, brow], [bv], eng="pool")
            self.tt(vf, vf, vcu, ALU.add, [bv, bl], [bv], eng="pool")
            self.cp(vb, vf, [bv], [bv], eng="act")
            rT, aT, bT, kT = (F4[:, q, :, :] for q in range(4))
            if i == 0:
                self.memset(H, 0.0, bHg)
                self.memset(Hb, 0.0, bHbg)
            for h in range(16):
                ps, bp = self.psum()
                self.mm(ps[:, 0:128], bT[:, h, :], aT[:, h, :], True, True, [bF4], [bp])
                self.mm(ps[:, 128:256], aT[:, h, :], bT[:, h, :], True, True, [bF4], [bp])
                self.mm(ps[:, 256:384], kT[:, h, :], aT[:, h, :], True, True, [bF4], [bp])
                if True:
                    self.tt(MA[h].rearrange("p a b -> p (a b)"), ps[:, 0:384], maskA, ALU.mult, [bp, bc], [bM[h]])
                else:
                    r_, br_ = MAr[(h // 2) % 2], bMAr[(h // 2) % 2]
                    self.cp(r_, ps[:, 0:384], [bp], [br_], eng="act")
                    self.tt(MA[h].rearrange("p a b -> p (a b)"), r_, maskAb, ALU.mult, [br_, bmk], [bM[h]], eng="pool")
                self.tt(Zq[h // 4][0][:, h % 4, :], MA[h][:, 0, :], identb, ALU.add, [bM[h], self.b_identb], [bZ[h // 4][0]], eng="pool")
            for p in range(8):
                ps, bp = self.psum()
                for j in range(2):
                    h = 2 * p + j
                    self.mm(ps[:, j * 256:j * 256 + 128], bT[:, h, :], rT[:, h, :], True, True, [bF4], [bp])
                    self.mm(ps[:, j * 256 + 128:j * 256 + 256], kT[:, h, :], rT[:, h, :], True, True, [bF4], [bp])
                if True:
                    self.tt(MBp[p].rearrange("p j a b -> p j (a b)"), ps.rearrange("p (j c) -> p j c", j=2),
                            maskB.unsqueeze(1).to_broadcast([128, 2, 256]), ALU.mult, [bp, bc], [bMB[p]])
                else:
                    r_, br_ = MBr[(p // 2) % 2], bMBr[(p // 2) % 2]
                    self.cp(r_.rearrange("p j c -> p (j c)"), ps, [bp], [br_], eng="act")
                    self.tt(MBp[p].rearrange("p j a b -> p j (a b)"), r_, maskBb.unsqueeze(1).to_broadcast([128, 2, 256]), ALU.mult,
                            [br_, bmk], [bMB[p]], eng="pool")
            for lv in range(6):
                a_, b_ = lv % 2, (lv + 1) % 2
                for p in range(8):
                    ps, bp = self.psum()
                    rd = []
                    for j in range(2):
                        h = 2 * p + j
                        if lv == 0:
                            X, Y = MA[h][:, 1, :], MA[h][:, 0, :]
                            rd = [bM[2 * p], bM[2 * p + 1]]
                        else:
                            X, Y = XYp[p][a_][:, j, 0, :], XYp[p][a_][:, j, 1, :]
                            rd = [bXY[p][a_]]
                        self.mm(ps[:, j * 256:j * 256 + 128], Y, X, True, True, rd, [bp])
                        if lv < 5:
                            self.mm(ps[:, j * 256 + 128:j * 256 + 256], X, Y, True, True, rd, [bp])
                    ev_ = "dve" if p == 7 else "act"
                    if lv < 5:
                        self.cp(XYp[p][b_].rearrange("p j a b -> p (j a b)"), ps, [bp], [bXY[p][b_]], eng=ev_)
                    else:
                        self.cp(XYp[p][b_][:, :, 0, :], ps.rearrange("p (j a b) -> p j a b", j=2, a=2)[:, :, 0, :], [bp], [bXY[p][b_]], eng=ev_)
                for q in range(4):
                    ps, bp = self.psum()
                    for j in range(4):
                        h = 4 * q + j
                        X2 = XYp[h // 2][b_][:, h % 2, 0, :]
                        self.mm(ps[:, j * 128:(j + 1) * 128], X2, Zq[q][a_][:, j, :], True, True, [bXY[h // 2][b_], bZ[q][a_]], [bp])
                    self.tt(Zq[q][b_].rearrange("p j b -> p (j b)"), ps, Zq[q][a_].rearrange("p j b -> p (j b)"), ALU.add,
                            [bp, bZ[q][a_]], [bZ[q][b_]])
            for g in range(2):
                ps, bp = self.psum()
                for j in range(8):
                    h = 8 * g + j
                    vh = vb[:, h * 64:(h + 1) * 64]
                    o_ = ps[:, j * 64:(j + 1) * 64]
                    self.mm(o_, aT[:, h, :], Hb[:, h, :], True, False, [bF4, bHbg[g]], [bp])
                    self.mm(o_, MA[h][:, 2, :], vh, False, True, [bM[h], bv], [bp])
                self.cp(Xg8[g].rearrange("p j d -> p (j d)"), ps, [bp], [bXg[g]])
            for g in range(2):
                ps, bp = self.psum()
                for j in range(8):
                    h = 8 * g + j
                    self.mm(ps[:, j * 64:(j + 1) * 64], Zq[h // 4][0][:, h % 4, :], Xg8[g][:, j, :], True, True, [bZ[h // 4][0], bXg[g]], [bp])
                self.cp(U8[g].rearrange("p j d -> p (j d)"), ps, [bp], [bU[g]])
            for g in range(2):
                ps, bp = self.psum()
                for j in range(8):
                    h = 8 * g + j
                    vh = vb[:, h * 64:(h + 1) * 64]
                    o_ = ps[:, j * 64:(j + 1) * 64]
                    self.mm(o_, rT[:, h, :], Hb[:, h, :], True, False, [bF4, bHbg[g]], [bp])
                    self.mm(o_, MBp[h // 2][:, h % 2, 0, :], U8[g][:, j, :], False, False, [bMB[h // 2], bU[g]], [bp])
                    self.mm(o_, MBp[h // 2][:, h % 2, 1, :], vh, False, True, [bMB[h // 2], bv], [bp])
                self.cp(otm[:, 8 * g:8 * g + 8, :].rearrange("p j d -> p (j d)"), ps, [bp], [bo])
                ps, bp = self.psum()
                for j in range(8):
                    h = 8 * g + j
                    vh = vb[:, h * 64:(h + 1) * 64]
                    o_ = ps[0:64, j * 64:(j + 1) * 64]
                    self.mm(o_, T2[:, h // 2, 0, (h % 2) * 64:(h % 2 + 1) * 64], U8[g][:, j, :], True, False, [bl, bU[g]], [bp])
                    self.mm(o_, T2[:, h // 2, 1, (h % 2) * 64:(h % 2 + 1) * 64], vh, False, True, [bl, bv], [bp])
                Hg = H[:, 8 * g:8 * g + 8, :]
                self.tt(Hg, Hg, ecdh[:, 8 * g:8 * g + 8, i:i + 1].to_broadcast([64, 8, 64]), ALU.mult, [bHg[g], becdh], [bHg[g]])
                self.tt(Hg, Hg, ps[0:64, :].rearrange("p (j d) -> p j d", j=8), ALU.add, [bHg[g], bp], [bHg[g]])
                self.cp(Hb[:, 8 * g:8 * g + 8, :], Hg, [bHg[g]], [bHbg[g]], eng="act")
            mean, var_, rs_, tmp_ = (st2[:, q, :] for q in range(4))
            self.S.add("dve", lambda e, o=mean, x=otm: e.reduce_sum(out=o, in_=x, axis=AX.X), reads=[bo], writes=[bst2])
            self.ts(mean, mean, 1.0 / 64.0, None, ALU.mult, reads=[bst2], writes=[bst2])
            self.tt(otm, otm, mean.unsqueeze(2).to_broadcast([128, 16, 64]), ALU.subtract, [bo, bst2], [bo])
            self.tt(osq, otm, otm, ALU.mult, [bo], [bo], eng="pool")
            self.S.add("dve", lambda e, o=var_, x=osq: e.reduce_sum(out=o, in_=x, axis=AX.X), reads=[bo], writes=[bst2])
            self.act(rs_, var_, AF.Sqrt, [bst2], [bst2], scale=1.0 / 64.0, bias=64e-5)
            self.recip(rs_, rs_, [bst2], [bst2])
            self.tt(otm, otm, rs_.unsqueeze(2).to_broadcast([128, 16, 64]), ALU.mult, [bo, bst2], [bo])
            ov = otm.rearrange("p h d -> p (h d)")
            self.tt(ov, ov, lnw, ALU.mult, [bo, brow], [bo])
            self.tt(ov, ov, lnb, ALU.add, [bo, brow], [bo])
            self.tt(osq, vf.rearrange("p (h d) -> p h d", h=16), rks[:, i, :].unsqueeze(2).to_broadcast([128, 16, 64]), ALU.mult,
                    [bv, brks], [bo], eng="pool")
            self.tt(otm, otm, osq, ALU.add, [bo], [bo])
            for half in range(2):
                ps, bp = self.psum()
                for k in range(4):
                    kk_ = half * 4 + k
                    self.tr(ps[:, k * 128:(k + 1) * 128], ov[:, kk_ * 128:(kk_ + 1) * 128], [bo], [bp])
                self.tt(yo[i % 2][:, half * 4:half * 4 + 4, :], ps.rearrange("p (a b) -> p a b", a=4), gt_[:, half * 4:half * 4 + 4, :],
                        ALU.mult, [bp, bl], [byo[i % 2]])
            pend_st = (lambda i=i, sl=sl: self.dma(ysv[:, :, sl], yo[i % 2], reads=[byo[i % 2]], writes=[self.b_ys[3][i]], q="sp"))
        pend_st()
        A.release()

    def merge(self, l):
        A = self.A
        A.mark()
        wo = A.alloc([8, DM], BF16)
        bwo = Buf()
        for hlf in range(2):
            self.dma(wo[:, :, hlf * 512:(hlf + 1) * 512],
                     self.d["w_out"][l].rearrange("(k p) n -> p k n", p=128)[:, :, hlf * 512:(hlf + 1) * 512], writes=[bwo], q="pool")
        self.wslab_init(2)
        acc = A.alloc([8, SEQ])
        bacc = [Buf() for _ in range(8)]
        ysb = [A.alloc([8, 512], BF16) for _ in range(2)]
        bysb = [Buf() for _ in range(2)]
        gtile = [A.alloc([512]) for _ in range(6)]
        bg = [Buf() for _ in range(6)]
        mT = self.hT
        bm = self.bh
        gi = 0
        yi = 0
        for m in range(4):
            yv = self.ys[m].rearrange("(c p) t -> p c t", p=128)
            for dg in range(2):
                wt, bw = self.wslab(self.d["w_branch"][l, m], dg * 512, 512)
                for tb in range(4):
                    ts_ = slice(tb * 512, (tb + 1) * 512)
                    if dg == 0 or True:
                        yt, byt = ysb[yi % 2], bysb[yi % 2]
                        yi += 1
                        self.dma(yt, yv[:, :, ts_], reads=self.b_ys[m] + [self.b_ysF[m]], writes=[byt])
                    for cc in range(4):
                        dch = dg * 4 + cc
                        ps, bp = self.psum()
                        for k in range(8):
                            self.mm(ps, wt[:, k, cc * 128:(cc + 1) * 128], yt[:, k, :], k == 0, k == 7, [bw, byt], [bp])
                        g, bgt = gtile[gi % 6], bg[gi % 6]
                        gi += 1
                        ft = F_GATE + m * 8 + dch
                        self.dma(g, self.pFM[ft * 128:(ft + 1) * 128, ts_], reads=[self.b_pFM[ft]], writes=[bgt])
                        self.act(g, g, AF.Sigmoid, [bgt], [bgt])
                        if m == 0:
                            self.tt(acc[:, dch, ts_], ps, g, ALU.mult, [bp, bgt], [bacc[dch]])
                        else:
                            self.tt(g, ps, g, ALU.mult, [bp, bgt], [bgt])
                            if m < 3:
                                self.tt(acc[:, dch, ts_], acc[:, dch, ts_], g, ALU.add, [bacc[dch], bgt], [bacc[dch]])
                            else:
                                self.tt(mT[:, dch, ts_], acc[:, dch, ts_], g, ALU.add, [bacc[dch], bgt], bm[4 * tb:4 * tb + 4])
        A.release()
        self.S.barrier()
        A.mark()
        wo_ = A.alloc([8, DM], BF16)
        self.rowb(l, "norm_ffn", self.nrow, self.b_nrow)
        xin = [A.alloc([DM]) for _ in range(2)]
        bxin = [Buf() for _ in range(2)]
        self.norm_alloc()
        src = self.x if l == 0 else self.xres
        def ldx(i):
            self.dma(xin[i % 2], src[i * 128:(i + 1) * 128, :], reads=[self.b_xres[i]], writes=[bxin[i % 2]])
        def mm_stage(i):
            sl = slice(i * 128, (i + 1) * 128)
            out = []
            for hlf in range(2):
                ps, bp = self.psum()
                for k in range(8):
                    self.mm(ps, mT[:, k, sl], wo[:, k, hlf * 512:(hlf + 1) * 512], k == 0, k == 7, [bm[i], bwo], [bp])
                out.append((ps, bp))
            return out

        ldx(0)
        pss = mm_stage(0)
        for i in range(NT):
            xt, bx = xin[i % 2], bxin[i % 2]
            sl = slice(i * 128, (i + 1) * 128)
            nxt = None
            if i + 1 < NT:
                ldx(i + 1)
                nxt = mm_stage(i + 1)
            for hlf, (ps, bp) in enumerate(pss):
                self.tt(xt[:, hlf * 512:(hlf + 1) * 512], xt[:, hlf * 512:(hlf + 1) * 512], ps, ALU.add, [bx, bp], [bx])
            self.dma(self.xres[sl, :], xt, reads=[bx], writes=[self.b_xres[i]])
            self.norm_tile(xt, bx, i)
            pss = nxt
        A.release()

    def ffn(self, l):
        A = self.A
        A.mark()
        self.norm_alloc()
        h2, bh2 = self.hT, self.bh
        cw = self.pf("ffn_cw").rearrange("p (t j) -> p t j", j=3)
        cb = self.pf("ffn_cb")
        bpf = self.b_pfm
        wd = A.alloc([22, DM], BF16)
        bwd = Buf()
        wdv = self.d["ffn_down"][l].rearrange("(k p) n -> p k n", p=128)
        for q in range(4):
            self.dma(wd[:, :, q * 256:(q + 1) * 256], wdv[:, :, q * 256:(q + 1) * 256], writes=[bwd], q="pool")
        aT = [A.alloc([22, 1024], BF16)]
        baT = [Buf()]
        halo = A.alloc([44, 2], BF16)
        bhalo = Buf()
        wv = [A.alloc([8, 256], BF16) for _ in range(2)]
        wg = [A.alloc([8, 256], BF16) for _ in range(2)]
        bws = [Buf() for _ in range(2)]
        ub = [[A.alloc([2 + 512], BF16) for _ in range(2)] for _ in range(2)]
        bub = [[Buf() for _ in range(2)] for _ in range(2)]
        dg = [A.alloc([2, 3, 128], BF16) for _ in range(3)]
        bdg = [Buf() for _ in range(3)]
        sg = [A.alloc([512]) for _ in range(2)]
        bsg = [Buf() for _ in range(2)]
        last = (l == self.L - 1)
        if last:
            self.dma(self.nrow, self.d["norm_final"].partition_broadcast(128), writes=[self.b_nrow])
        else:
            o, w = PROW_OFF["norm_mix"]
            self.dma(self.nrow, self.d["prow"][l + 1, o:o + w].partition_broadcast(128), writes=[self.b_nrow])
        xin = [A.alloc([DM]) for _ in range(2)]
        bxin = [Buf() for _ in range(2)]
        upv = self.d["ffn_up"][l].rearrange("(k p) n -> p k n", p=128)
        ubi = 0
        pending = None
        for hh in range(2):
            aTq, baTq = aT[0], baT[0]
            for sb in range(11):
                wv_, wg_, bw_ = wv[sb % 2], wg[sb % 2], bws[sb % 2]
                self.dma(wv_, upv[:, :, sb * 256:(sb + 1) * 256], writes=[bw_], q="pool")
                self.dma(wg_, upv[:, :, DFF + sb * 256:DFF + (sb + 1) * 256], writes=[bw_], q="pool")
                for jj in range(2):
                  jf = 2 * sb + jj
                  d_, bd_ = dg[jf % 3], bdg[jf % 3]
                  for vg in range(2):
                      ft = jf + 22 * vg
                      for j in range(3):
                          self.ts(d_[:, vg, j, :], self.identb, cw[:, ft, j:j + 1], None, ALU.mult,
                                  reads=[self.b_identb, bpf], writes=[bd_], eng="act")
                  for blk in range(2):
                    hf = hh * 2 + blk
                    t0 = hf * 512
                    ubi += 1
                    for vg in range(2):
                        ft = jf + 22 * vg
                        w_ = wv_ if vg == 0 else wg_
                        uu, buu = ub[vg][ubi % 2], bub[vg][ubi % 2]
                        if hf == 0:
                            self.memset(uu[:, 0:2], 0.0, [buu])
                        else:
                            self.cp(uu[:, 0:2], halo[:, ft, :], [bhalo], [buu], eng="act")
                        ps, bp = self.psum()
                        for k in range(8):
                            self.mm(ps, w_[:, k, jj * 128:(jj + 1) * 128], h2[:, k, t0:t0 + 512], k == 0, k == 7,
                                    [bw_] + bh2[(t0 // 128):(t0 // 128) + 4], [bp])
                        self.cp(uu[:, 2:514], ps, [bp], [buu])
                        if hf < 3:
                            self.cp(halo[:, ft, :], uu[:, 512:514], [buu], [bhalo], eng="act")
                    if pending is not None:
                        pending()

                    def conv_stage(jf=jf, blk=blk, d_=d_, bd_=bd_, k_=ubi % 2):
                        conv_ps = []
                        for vg in range(2):
                            ft = jf + 22 * vg
                            uu, buu = ub[vg][k_], bub[vg][k_]
                            ps2, bp2 = self.psum()
                            for j in range(3):
                                self.mm(ps2, d_[:, vg, j, :], uu[:, j:j + 512], j == 0, j == 2, [bd_, buu], [bp2])
                            conv_ps.append((ps2, bp2, ft))
                        (psv, bpv, ftv), (psg, bpg, ftg) = conv_ps
                        sg_, bsg_ = sg[k_], bsg[k_]
                        self.act(sg_, psg, AF.Silu, [bpg, bpf], [bsg_], bias=cb[:, ftg:ftg + 1])
                        self.stt(aTq[:, jf, blk * 512:(blk + 1) * 512], psv, cb[:, ftv:ftv + 1], sg_, ALU.add, ALU.mult, [bpv, bpf, bsg_], [baTq])
                    pending = conv_stage
            if pending is not None:
                pending()
                pending = None
            def dn_stage(it):
                out = []
                for hlf in range(2):
                    ps, bp = self.psum()
                    for k in range(22):
                        self.mm(ps, aTq[:, k, it * 128:(it + 1) * 128], wd[:, k, hlf * 512:(hlf + 1) * 512], k == 0, k == 21, [baTq, bwd], [bp])
                    out.append((ps, bp))
                return out

            pss = dn_stage(0)
            for it in range(8):
                i = hh * 8 + it
                sl = slice(i * 128, (i + 1) * 128)
                xt, bx = xin[i % 2], bxin[i % 2]
                if it == 0:
                    self.dma(xt, self.xres[sl, :], reads=[self.b_xres[i]], writes=[bx])
                nxt = None
                if it + 1 < 8:
                    self.dma(xin[(i + 1) % 2], self.xres[(i + 1) * 128:(i + 2) * 128, :], reads=[self.b_xres[i + 1]], writes=[bxin[(i + 1) % 2]])
                    nxt = dn_stage(it + 1)
                for hlf, (ps, bp) in enumerate(pss):
                    self.tt(xt[:, hlf * 512:(hlf + 1) * 512], xt[:, hlf * 512:(hlf + 1) * 512], ps, ALU.add, [bx, bp], [bx])
                pss = nxt
                if last:
                    self.norm_tile(xt, bx, i, final_out=self.out[sl, :])
                else:
                    self.dma(self.xres[sl, :], xt, reads=[bx], writes=[self.b_xres[i]])
                    self.norm_tile(xt, bx, i)
        A.release()


_CACHE = {}


def build_program(L, dbg=False, phases=None):
    nc = bass.Bass("TRN2", target_bir_lowering=False)
    kb = KB(nc, L, dbg=dbg, phases=phases)
    kb.build()
    return nc


def kernel(**inputs):
    inp = {k: np.asarray(v) for k, v in inputs.items()}
    L = inp["w_in"].shape[0]
    B = inp["x"].shape[0]
    g = prep_inputs(inp)
    nc = build_program(L)
    in_maps = []
    for b in range(B):
        m = dict(g)
        m["x"] = np.ascontiguousarray(inp["x"][b])
        in_maps.append(m)
    res = run_bass_kernel_spmd(nc, in_maps, core_ids=list(range(B)))
    out = np.stack([np.asarray(r["out"]) for r in res.results], axis=0).astype(np.float32)
    return out
```

```python
import numpy as np
from contextlib import ExitStack
import concourse.bass as bass
import concourse.mybir as mybir
from concourse.bass_utils import run_bass_kernel_spmd

F32 = mybir.dt.float32
BF16 = mybir.dt.bfloat16
ALU = mybir.AluOpType
AF = mybir.ActivationFunctionType
AX = mybir.AxisListType

KD = 16
ENGS = ["pe", "act", "dve", "pool", "sp"]
DMAQ = ["sp", "pool", "act"]


class Buf:
    __slots__ = ("name", "w", "r")

    def __init__(self, name=""):
        self.name = name
        self.w = None
        self.r = {}


class Sched:
    def __init__(self, nc):
        self.nc = nc
        self.ops = {e: [] for e in ENGS}
        self.cnt = {e: 0 for e in ENGS}
        self.dcnt = {e: 0 for e in DMAQ}
        self.known = {e: {} for e in ENGS}
        self.bar = {e: None for e in ENGS}
        self.nops = 0
        self.rec = None

    def record(self, f):
        assert self.rec is None
        self.rec = []
        f()
        r, self.rec = self.rec, None
        return r

    def replay_interleaved(self, a, b):
        na, nb = len(a), len(b)
        ia = ib = 0
        while ia < na or ib < nb:
            if ib >= nb or (ia < na and ia * nb <= ib * na):
                self.add(*a[ia])
                ia += 1
            else:
                self.add(*b[ib])
                ib += 1

    @staticmethod
    def _eng_of(key):
        return key[1]

    def add(self, eng, fn, reads=(), writes=(), dma=False):
        if self.rec is not None:
            self.rec.append((eng, fn, tuple(reads), tuple(writes), dma))
            return None
        deps = {}

        def dep(kv):
            if kv is None:
                return
            k, v = kv
            if deps.get(k, 0) < v:
                deps[k] = v

        for b in reads:
            dep(b.w)
        for b in writes:
            dep(b.w)
            for k, v in b.r.items():
                if (not dma) and k == ("c", eng) and eng == "pe":
                    continue
                dep((k, v))
        if self.bar[eng] is not None:
            for k, v in self.bar[eng].items():
                dep((k, v))
            self.bar[eng] = None
        if dma:
            idx = self.dcnt[eng]
            self.dcnt[eng] += 1
            if idx >= KD:
                dep((("d", eng, (idx - KD) % KD), 16 * ((idx - KD) // KD + 1)))
            tok = (("d", eng, idx % KD), 16 * (idx // KD + 1))
        else:
            self.cnt[eng] += 1
            tok = (("c", eng), self.cnt[eng])
        waits = []
        kn = self.known[eng]
        for k, v in deps.items():
            if k == ("c", "pe") and eng == "pe" and not dma:
                continue
            if kn.get(k, 0) >= v:
                continue
            kn[k] = v
            waits.append((k, v))
        self.ops[eng].append((fn, waits, tok, dma))
        self.nops += 1
        for b in writes:
            b.w = tok
            b.r = {}
        for b in reads:
            if b in writes:
                continue
            k, v = tok
            if b.r.get(k, 0) < v:
                b.r[k] = v
        return tok

    def barrier(self):
        snap = {}
        for e in ENGS:
            if self.cnt[e] > 0:
                snap[("c", e)] = self.cnt[e]
        for q in DMAQ:
            n = self.dcnt[q]
            for idx in range(max(0, n - KD), n):
                k = ("d", q, idx % KD)
                v = 16 * (idx // KD + 1)
                if snap.get(k, 0) < v:
                    snap[k] = v
        for e in ENGS:
            cur = self.bar[e] or {}
            for k, v in snap.items():
                if cur.get(k, 0) < v:
                    cur[k] = v
            self.bar[e] = cur

    def emit(self, final_wait_eng="sp"):
        nc = self.nc
        self.barrier()
        fin = self.bar[final_wait_eng]
        needed = {e: set() for e in ENGS}
        for e in ENGS:
            for fn, waits, tok, dma in self.ops[e]:
                for k, v in waits:
                    if k[0] == "c":
                        needed[k[1]].add(v)
        for k, v in fin.items():
            if k[0] == "c":
                needed[k[1]].add(v)
        remap = {}
        for e in ENGS:
            s = sorted(needed[e])
            remap[e] = {v: i + 1 for i, v in enumerate(s)}
        with ExitStack() as st:
            sems = {}
            for e in ENGS:
                sems[("c", e)] = st.enter_context(nc.semaphore("c_" + e))
            for q in DMAQ:
                if self.dcnt[q] > 0:
                    for j in range(KD):
                        sems[("d", q, j)] = st.enter_context(nc.semaphore("d_%s_%d" % (q, j)))
            block = st.enter_context(nc.Block())

            def run(engname, e):
                for fn, waits, tok, dma in self.ops[engname]:
                    for k, v in waits:
                        if k[0] == "c":
                            v = remap[k[1]][v]
                        e.wait_ge(sems[k], v)
                    ins = fn(e)
                    k, v = tok
                    if dma:
                        ins.then_inc(sems[k], 16)
                    elif v in remap[engname]:
                        ins.then_inc(sems[k], 1)
                if engname == final_wait_eng:
                    for k, v in fin.items():
                        if k[0] == "c":
                            v = remap[k[1]][v]
                        e.wait_ge(sems[k], v)

            @block.tensor
            def _(e):
                run("pe", e)

            @block.scalar
            def _(e):
                run("act", e)

            @block.vector
            def _(e):
                run("dve", e)

            @block.gpsimd
            def _(e):
                run("pool", e)

            @block.sync
            def _(e):
                run("sp", e)


class Arena:
    def __init__(self, nc, st, name, ncols_f32):
        self.t = st.enter_context(nc.sbuf_tensor(name, [128, ncols_f32], F32))
        self.n = ncols_f32
        self.off = 0
        self.marks = []

    def alloc(self, shape, dtype=F32, parts=128):
        ne = int(np.prod(shape))
        if dtype == BF16:
            ncol = (ne + 1) // 2
        else:
            ncol = ne
        assert self.off + ncol <= self.n, "arena overflow %d + %d > %d" % (self.off, ncol, self.n)
        ap = self.t[0:parts, self.off:self.off + ncol]
        self.off += ncol
        if dtype == BF16:
            ap = ap.bitcast(BF16)
            if ne % 2:
                ap = ap[:, 0:ne]
        if len(shape) == 2:
            ap = ap.rearrange("p (a b) -> p a b", a=shape[0])
        elif len(shape) == 3:
            ap = ap.rearrange("p (a b c) -> p a b c", a=shape[0], b=shape[1])
        return ap

    def mark(self):
        self.marks.append(self.off)

    def release(self):
        self.off = self.marks.pop()

import math

SKEW = False
SEQ = 2048
DM = 1024
NT = 16
EPS = 1e-6
IN_SPLITS = (1024, 2048, 16, 1024, 1024, 512, 512, 1024, 1024, 3328, 4096)
NTM = 4112
NFM = 11520
DFF = 2816
F_XBC, F_LX, F_LG, F_RG, F_WR, F_WK, F_LO, F_GL, F_GATE = 0, 16, 24, 32, 40, 48, 56, 57, 58
T_Z, T_QK, T_RV, T_WV, T_DT = 0, 1024, 2048, 3072, 4096


def _cols():
    off = np.cumsum([0] + list(IN_SPLITS))
    rwo = off[9]
    ar = np.arange
    tm = np.concatenate([ar(off[0], off[1]), ar(off[5], off[7]), ar(off[7], off[8]),
                         ar(rwo + 2048, rwo + 3072), ar(off[2], off[3])])
    fm = np.concatenate([ar(off[1], off[2]), ar(off[4], off[5]), ar(off[3], off[4]), ar(off[8], off[9]),
                         ar(rwo, rwo + 2048), ar(rwo + 3072, rwo + 3328), ar(off[10], off[11])])
    assert len(tm) == NTM and len(fm) == NFM
    return tm, fm


def _fm(v):
    return np.ascontiguousarray(v.reshape(-1, 128).T)


PFM_SPEC = [("ssm_cw", 64), ("ssm_cb", 16), ("lru_cw", 32), ("lru_cb", 8), ("lru_ba", 8), ("lru_bi", 8),
            ("lru_lam", 8), ("ret_nw", 8), ("mu_fm", 18), ("w0", 8), ("a0", 8), ("k_k", 8), ("k_a", 8),
            ("r_k", 8), ("ffn_cw", 132), ("ffn_cb", 44)]
PROW_SPEC = [("norm_mix", 1024), ("norm_ffn", 1024), ("ssm_norm", 1024), ("mu_v", 1024), ("ln_w", 1024),
             ("ln_b", 1024), ("dt_bias", 16), ("a_log", 16), ("ssm_d", 16)]
CST_SPEC = [("ident", 128), ("tri", 128), ("ones", 128), ("negmask4", 128), ("GT", 1024), ("gqA", 1024),
            ("gqB", 1024), ("kend", 8), ("cdt", 8), ("cos", 512), ("sin", 512), ("maskA", 384), ("maskB", 256),
            ("headsel", 2), ("blockones", 128)]


def _offs(spec):
    d = {}
    o = 0
    for n, c in spec:
        d[n] = (o, c)
        o += c
    return d, o


PFM_OFF, NPF = _offs(PFM_SPEC)
PROW_OFF, NROW = _offs(PROW_SPEC)
CST_OFF, NCST = _offs(CST_SPEC)


def make_cst():
    c = {}
    p = np.arange(128)
    c["ident"] = np.eye(128)
    c["tri"] = (p[:, None] <= p[None, :]).astype(np.float64)
    c["ones"] = np.ones((128, 128))
    nm = np.where(p[:, None] <= p[None, :], 0.0, -30000.0)
    c["negmask4"] = nm
    lg = np.log1p(-np.exp2(-5.0 - np.arange(8)))
    same = (p[:, None] // 64) == (p[None, :] // 64)
    GT = np.zeros((128, 8, 128))
    for h in range(8):
        GT[:, h, :] = np.where(same, np.exp(lg[h] * np.abs(p[:, None] - p[None, :])), 0.0)
    c["GT"] = GT.reshape(128, 1024)
    gqA = np.zeros((128, 8, 128))
    gqB = np.zeros((128, 8, 128))
    for h in range(8):
        gq = np.exp(lg[h] * ((p % 64) + 1.0))
        gqA[:, h, :] = np.where(p < 64, gq, 0.0)[None, :]
        gqB[:, h, :] = np.where(p >= 64, gq, 0.0)[None, :]
    c["gqA"] = gqA.reshape(128, 1024)
    c["gqB"] = gqB.reshape(128, 1024)
    c["kend"] = np.exp(lg[None, :] * (63.0 - (p % 64))[:, None]) * 0.125
    c["cdt"] = np.tile(np.exp(lg * 64.0)[None, :], (128, 1))
    pos = np.arange(SEQ, dtype=np.float32)
    inv = (np.float32(10000.0) ** (-np.arange(0, 64, 2, dtype=np.float32) / np.float32(64))).astype(np.float32)
    ang = (pos[:, None] * inv[None, :]).astype(np.float32)
    c["cos"] = np.cos(ang).reshape(16, 128, 32).transpose(1, 0, 2).reshape(128, 512)
    c["sin"] = np.sin(ang).reshape(16, 128, 32).transpose(1, 0, 2).reshape(128, 512)
    su = (p[:, None] < p[None, :]).astype(np.float64)
    sl = (p[:, None] > p[None, :]).astype(np.float64)
    iu = (p[:, None] <= p[None, :]).astype(np.float64)
    c["maskA"] = np.concatenate([su, sl, su], axis=1)
    c["maskB"] = np.concatenate([iu, iu], axis=1)
    hs = np.zeros((128, 2))
    hs[:64, 0] = 1
    hs[64:, 1] = 1
    c["headsel"] = hs
    c["blockones"] = same.astype(np.float64)
    out = np.zeros((128, NCST), np.float32)
    for n, (o, w) in CST_OFF.items():
        assert c[n].shape == (128, w), (n, c[n].shape)
        out[:, o:o + w] = c[n]
    return out


def prep_inputs(inp):
    L = inp["w_in"].shape[0]
    tm, fm = _cols()
    g = {}
    g["w_tm"] = np.ascontiguousarray(inp["w_in"][:, :, tm])
    g["w_fm"] = np.ascontiguousarray(inp["w_in"][:, :, fm])
    g["w_branch"] = np.ascontiguousarray(inp["w_branch"])
    g["w_out"] = np.ascontiguousarray(inp["w_out"])
    g["ffn_up"] = np.ascontiguousarray(inp["ffn_up"])
    g["ffn_down"] = np.ascontiguousarray(inp["ffn_down"])
    for nm in ("lru_w_a", "lru_w_i"):
        bd = np.zeros((L, 8, 128, 128), np.float32)
        w = inp[nm]
        for t in range(8):
            bd[:, t, 0:64, 0:64] = w[:, 2 * t]
            bd[:, t, 64:128, 64:128] = w[:, 2 * t + 1]
        g[nm + "_bd"] = bd
    g["w2a2"] = np.ascontiguousarray(np.concatenate([inp["rwkv_w2"], inp["rwkv_a2"]], axis=1))
    g["g2"] = np.ascontiguousarray(inp["rwkv_g2"])
    pfm = np.zeros((L, 128, NPF), np.float32)
    prow = np.zeros((L, NROW), np.float32)
    for l in range(L):
        d = {}
        d["ssm_cw"] = np.stack([_fm(inp["ssm_conv_w"][l, j]) for j in range(4)], axis=2).reshape(128, 64)
        d["ssm_cb"] = _fm(inp["ssm_conv_b"][l])
        d["lru_cw"] = np.stack([_fm(inp["lru_conv_w"][l, j]) for j in range(4)], axis=2).reshape(128, 32)
        d["lru_cb"] = _fm(inp["lru_conv_b"][l])
        d["lru_ba"] = _fm(inp["lru_b_a"][l])
        d["lru_bi"] = _fm(inp["lru_b_i"][l])
        d["lru_lam"] = _fm(inp["lru_lam"][l])
        d["ret_nw"] = _fm(inp["ret_norm"][l])
        mu = inp["rwkv_mu"][l]
        d["mu_fm"] = _fm(np.concatenate([mu[0:2048], mu[3072:3328]]))
        d["w0"] = _fm(inp["rwkv_w0"][l])
        d["a0"] = _fm(inp["rwkv_a0"][l])
        d["k_k"] = _fm(inp["rwkv_k_k"][l])
        d["k_a"] = _fm(inp["rwkv_k_a"][l])
        d["r_k"] = _fm(inp["rwkv_r_k"][l])
        d["ffn_cw"] = np.stack([_fm(inp["ffn_conv_w"][l, j]) for j in range(3)], axis=2).reshape(128, 132)
        d["ffn_cb"] = _fm(inp["ffn_conv_b"][l])
        for n, (o, w) in PFM_OFF.items():
            pfm[l, :, o:o + w] = d[n]
        r = {"norm_mix": inp["norm_mix"][l], "norm_ffn": inp["norm_ffn"][l], "ssm_norm": inp["ssm_norm"][l],
             "mu_v": inp["rwkv_mu"][l][2048:3072], "ln_w": inp["rwkv_ln_w"][l], "ln_b": inp["rwkv_ln_b"][l],
             "dt_bias": inp["ssm_dt_bias"][l], "a_log": inp["ssm_a_log"][l], "ssm_d": inp["ssm_d"][l]}
        for n, (o, w) in PROW_OFF.items():
            prow[l, o:o + w] = r[n]
    g["pfm"] = pfm
    g["prow"] = prow
    g["norm_final"] = np.ascontiguousarray(inp["norm_final"])
    g["cst"] = make_cst()
    return g


SHARED_SHAPES = lambda L: {
    "w_tm": [L, 1024, NTM], "w_fm": [L, 1024, NFM], "w_branch": [L, 4, 1024, 1024], "w_out": [L, 1024, 1024],
    "ffn_up": [L, 1024, 2 * DFF], "ffn_down": [L, DFF, 1024], "lru_w_a_bd": [L, 8, 128, 128],
    "lru_w_i_bd": [L, 8, 128, 128], "w2a2": [L, 128, 1024], "g2": [L, 128, 1024], "pfm": [L, 128, NPF],
    "prow": [L, NROW], "norm_final": [1024], "cst": [128, NCST]}


class KB:
    def __init__(self, nc, L, dbg=False, phases=None):
        self.nc = nc
        self.L = L
        self.dbg = dbg
        self.S = Sched(nc)
        self.phases = phases
        okind = "ExternalOutput" if dbg else "Internal"
        self.x = nc.dram_tensor("x", [SEQ, DM], F32, kind="ExternalInput").ap()
        self.d = {n: nc.dram_tensor(n, s, F32, kind="ExternalInput").ap() for n, s in SHARED_SHAPES(L).items()}
        self.out = nc.dram_tensor("out", [SEQ, DM], F32, kind="ExternalOutput").ap()
        self.xres = nc.dram_tensor("xres", [SEQ, DM], F32, kind=okind).ap()
        self.pTM = nc.dram_tensor("pTM", [SEQ, NTM], F32, kind=okind).ap()
        self.pFM = nc.dram_tensor("pFM", [NFM, SEQ], F32, kind=okind).ap()
        self.ys = nc.dram_tensor("ys", [4, DM, SEQ], BF16, kind=okind).ap()
        self.mTM = nc.dram_tensor("mTM", [SEQ, 1536], F32).ap()
        self.wFM = nc.dram_tensor("wFM", [4, DM, SEQ], BF16).ap()
        self.wTM = nc.dram_tensor("wTM", [SEQ, 8, 2, 128], BF16).ap()
        self.wgate = nc.dram_tensor("wgate", [DM, SEQ], F32).ap()
        self.b_xres = [Buf() for _ in range(NT)]
        self.b_pTM = [[Buf() for _ in range(9)] for _ in range(NT)]
        self.b_pFM = [Buf() for _ in range(90)]
        self.b_ys = [[Buf() for _ in range(NT)] for _ in range(4)]
        self.b_ysF = [Buf() for _ in range(4)]
        self.b_mTM = Buf()
        self.b_wFM = [Buf() for _ in range(8)]
        self.b_wTM = [Buf() for _ in range(8)]
        self.b_wgate = [Buf() for _ in range(8)]
        self.b_out = Buf()
        self.ev = 0
        self.psi = 0
        self.ps_pool = list(range(8))
        self.force_evac = None

    def dma(self, out, in_, reads=(), writes=(), q="sp"):
        return self.S.add(q, lambda e: e.dma_start(out=out, in_=in_), reads=reads, writes=writes, dma=True)

    def mm(self, out, lhsT, rhs, start, stop, reads, writes):
        return self.S.add("pe", lambda e: e.matmul(out, lhsT=lhsT, rhs=rhs, start=start, stop=stop),
                          reads=reads, writes=writes)

    def tr(self, out, in_, reads, writes):
        ident = self.c("ident")
        return self.S.add("pe", lambda e: e.transpose(out, in_, ident), reads=list(reads) + [self.b_cst], writes=writes)

    def act(self, out, in_, func, reads, writes, bias=None, scale=None, accum=None):
        kw = {}
        if bias is not None:
            kw["bias"] = bias
        if scale is not None:
            kw["scale"] = scale
        if accum is not None:
            kw["accum_out"] = accum
        return self.S.add("act", lambda e: e.activation(out=out, in_=in_, func=func, **kw), reads=reads, writes=writes)

    def tt(self, out, in0, in1, op, reads, writes, eng="dve"):
        return self.S.add(eng, lambda e: e.tensor_tensor(out=out, in0=in0, in1=in1, op=op), reads=reads, writes=writes)

    def ts(self, out, in0, s1, s2, op0, op1=None, reads=(), writes=(), eng="dve"):
        if eng in ("pool", "act"):
            if op0 == ALU.mult and (op1 is None or op1 == ALU.add):
                kw = {"scale": s1}
                if op1 is not None:
                    kw["bias"] = s2
                return self.S.add("act", lambda e: e.activation(out=out, in_=in0, func=AF.Identity, **kw),
                                  reads=reads, writes=writes)
            eng = "dve"
        if op1 is None:
            op1 = ALU.add
            s2 = 0.0
        return self.S.add(eng, lambda e: e.tensor_scalar(out=out, in0=in0, scalar1=s1, scalar2=s2, op0=op0, op1=op1),
                          reads=reads, writes=writes)

    def stt(self, out, in0, scalar, in1, op0, op1, reads, writes, eng="dve"):
        eng = "dve"
        return self.S.add(eng, lambda e: e.scalar_tensor_tensor(out=out, in0=in0, scalar=scalar, in1=in1, op0=op0, op1=op1),
                          reads=reads, writes=writes)

    def cp(self, out, in_, reads, writes, eng=None):
        if eng is None and self.force_evac is not None:
            eng = self.force_evac
        if eng is None:
            self.ev ^= 1
            eng = "act" if self.ev else "dve"
        if eng == "act":
            return self.S.add("act", lambda e: e.copy(out=out, in_=in_), reads=reads, writes=writes)
        return self.S.add(eng, lambda e: e.tensor_copy(out=out, in_=in_), reads=reads, writes=writes)

    def memset(self, ap, val, writes, eng="dve"):
        return self.S.add(eng, lambda e: e.memset(ap, val), writes=writes)

    def recip(self, out, in_, reads, writes):
        return self.S.add("dve", lambda e: e.reciprocal(out=out, in_=in_), reads=reads, writes=writes)

    def psum(self):
        pool = self.ps_pool
        i = pool[self.psi % len(pool)]
        self.psi += 1
        return self.ps[i], self.psb[i]

    def c(self, name):
        o, w = CST_OFF[name]
        return self.cst[:, o:o + w]

    def pf(self, name):
        o, w = PFM_OFF[name]
        return self.pfm[:, o:o + w]

    def rowb(self, l, name, dst, buf):
        o, w = PROW_OFF[name]
        self.dma(dst, self.d["prow"][l, o:o + w].partition_broadcast(128), writes=[buf])

    def build(self):
        nc = self.nc
        with ExitStack() as st:
            self.st = st
            A = self.A = Arena(nc, st, "arena", 52000)
            self.ps = [st.enter_context(nc.psum_tensor("ps%d" % i, [128, 512], F32))[:, :] for i in range(8)]
            self.psb = [Buf("ps%d" % i) for i in range(8)]
            self.cst = A.alloc([NCST])
            self.b_cst = Buf("cst")
            self.dma(self.cst, self.d["cst"], writes=[self.b_cst])
            self.identb = A.alloc([128], BF16)
            self.b_identb = Buf()
            self.cp(self.identb, self.c("ident"), [self.b_cst], [self.b_identb], eng="dve")
            self.hT = A.alloc([8, SEQ], BF16)
            self.bh = [Buf("hT%d" % i) for i in range(NT)]
            self.pfm = A.alloc([NPF])
            self.b_pfm = Buf("pfm")
            self.nrow = A.alloc([DM])
            self.b_nrow = Buf("nrow")
            for l in range(self.L):
                self.layer(l)
            self.S.emit()
        return nc

    def on(self, name):
        return self.phases is None or name in self.phases

    def layer(self, l):
        A = self.A
        self.dma(self.pfm, self.d["pfm"][l], writes=[self.b_pfm])
        if l == 0:
            self.rowb(l, "norm_mix", self.nrow, self.b_nrow)
            A.mark()
            xin = [A.alloc([DM]) for _ in range(2)]
            bxin = [Buf() for _ in range(2)]
            self.norm_alloc()
            for i in range(NT):
                xt, bx = xin[i % 2], bxin[i % 2]
                self.dma(xt, self.x[i * 128:(i + 1) * 128, :], writes=[bx])
                self.norm_tile(xt, bx, i)
            A.release()
            self.S.barrier()
        fuse_lru = self.on("proj") and self.on("lru")
        if self.on("proj"):
            self.proj(l, with_lru=fuse_lru)
            self.S.barrier()
        if self.on("mamba"):
            self.mamba(l)
            self.S.barrier()
        if self.on("lru") and not fuse_lru:
            self.lru(l)
            self.S.barrier()
        if self.on("ret"):
            self.retention(l)
            self.S.barrier()
        if self.on("rwkv"):
            self.rwkv(l)
            self.S.barrier()
        if self.on("merge"):
            self.merge(l)
            self.S.barrier()
        if self.on("ffn"):
            self.ffn(l)
            self.S.barrier()

    def norm_alloc(self):
        A = self.A
        self.n_junk = A.alloc([DM])
        self.n_xn = [A.alloc([DM]) for _ in range(2)]
        self.n_ss = [A.alloc([2]) for _ in range(2)]
        self.b_n = [Buf() for _ in range(2)]
        self.b_njunk = Buf()
        self.n_i = 0

    def rms_scale(self, xt, bx, ss, bss, junk, bjunk, n=DM, eps=EPS):
        self.act(junk, xt, AF.Square, [bx], [bjunk, bss], accum=ss[:, 0:1])
        self.act(ss[:, 1:2], ss[:, 0:1], AF.Sqrt, [bss], [bss], scale=1.0 / n, bias=eps)
        self.recip(ss[:, 0:1], ss[:, 1:2], [bss], [bss])

    def norm_tile(self, xt, bx, i, final_out=None):
        j = self.n_i
        self.n_i ^= 1
        xn, ss, bn = self.n_xn[j], self.n_ss[j], self.b_n[j]
        self.rms_scale(xt, bx, ss, bn, self.n_junk, self.b_njunk)
        self.stt(xn, xt, ss[:, 0:1], self.nrow, ALU.mult, ALU.mult, [bx, bn, self.b_nrow], [bn])
        if final_out is not None:
            self.dma(final_out, xn, reads=[bn], writes=[self.b_out])
            return
        for half in range(2):
            ps, bp = self.psum()
            for k in range(4):
                kk = half * 4 + k
                self.tr(ps[:, k * 128:(k + 1) * 128], xn[:, kk * 128:(kk + 1) * 128], [bn], [bp])
            self.cp(self.hT[:, half * 4:half * 4 + 4, i * 128:(i + 1) * 128],
                    ps.rearrange("p (a b) -> p a b", a=4), [bp], [self.bh[i]])

    def wslab_init(self, n=2):
        A = self.A
        self.wsl = [A.alloc([8, 512], BF16) for _ in range(n)]
        self.b_wsl = [Buf() for _ in range(n)]
        self.wsl_i = 0

    def wslab(self, src2d, c0, wd):
        j = self.wsl_i
        self.wsl_i = (self.wsl_i + 1) % len(self.wsl)
        t, b = self.wsl[j], self.b_wsl[j]
        self.dma(t[:, :, 0:wd], src2d.rearrange("(k p) n -> p k n", p=128)[:, :, c0:c0 + wd], writes=[b], q="pool")
        return t, b

    def proj(self, l, with_lru=False):
        A = self.A
        A.mark()
        self.wslab_init(2)
        stg = [A.alloc([512]) for _ in range(4)]
        bstg = [Buf() for _ in range(4)]
        stF = [A.alloc([SEQ]) for _ in range(2)]
        bstF = [Buf() for _ in range(2)]
        hT, bh = self.hT, self.bh
        cnt = {"si": 0, "fi": 0}

        def tm_slab(sidx):
            c0 = sidx * 512
            wd = min(512, NTM - c0)
            wt, bw = self.wslab(self.d["w_tm"][l], c0, wd)
            for i in range(NT):
                ps, bp = self.psum()
                for k in range(8):
                    self.mm(ps[:, 0:wd], hT[:, k, i * 128:(i + 1) * 128], wt[:, k, 0:wd], k == 0, k == 7, [bh[i], bw], [bp])
                s, bs = stg[cnt["si"] % 4], bstg[cnt["si"] % 4]
                cnt["si"] += 1
                self.cp(s[:, 0:wd], ps[:, 0:wd], [bp], [bs])
                self.dma(self.pTM[i * 128:(i + 1) * 128, c0:c0 + wd], s[:, 0:wd], reads=[bs], writes=[self.b_pTM[i][sidx]])

        def fm_slab(sidx):
            c0 = sidx * 512
            wd = min(512, NFM - c0)
            wt, bw = self.wslab(self.d["w_fm"][l], c0, wd)
            for cc in range(wd // 128):
                f = c0 // 128 + cc
                s, bs = stF[cnt["fi"] % 2], bstF[cnt["fi"] % 2]
                cnt["fi"] += 1
                for tb in range(4):
                    ps, bp = self.psum()
                    for k in range(8):
                        self.mm(ps, wt[:, k, cc * 128:(cc + 1) * 128], hT[:, k, tb * 512:(tb + 1) * 512], k == 0, k == 7,
                                [bw] + bh[4 * tb:4 * tb + 4], [bp])
                    self.cp(s[:, tb * 512:(tb + 1) * 512], ps, [bp], [bs])
                self.dma(self.pFM[f * 128:(f + 1) * 128, :], s, reads=[bs], writes=[self.b_pFM[f]])

        def rest():
            for sidx in range(9):
                tm_slab(sidx)
            for sidx in list(range(0, 4)) + list(range(8, 23)):
                fm_slab(sidx)

        for sidx in range(4, 8):
            fm_slab(sidx)
        if with_lru:
            self.ps_pool = [0, 1, 2, 3, 4, 5]
            self.force_evac = "act"
            oa = self.S.record(rest)
            self.force_evac = None
            self.ps_pool = [6, 7]
            ob = self.S.record(lambda: self.lru(l, q="act"))
            self.ps_pool = list(range(8))
            self.S.replay_interleaved(oa, ob)
        else:
            rest()
        A.release()

    def conv_fm(self, out, cin, K, w, b, reads, writes, n, eng="dve"):
        self.ts(out, cin[:, 0:n], w[:, 0:1], b, ALU.mult, ALU.add, reads=reads, writes=writes, eng=eng)
        for j in range(1, K):
            self.stt(out, cin[:, j:j + n], w[:, j:j + 1], out, ALU.mult, ALU.add, reads, writes, eng=eng)

    def mamba(self, l):
        A = self.A
        A.mark()
        cw = self.pf("ssm_cw").rearrange("p (t j) -> p t j", j=4)
        cb = self.pf("ssm_cb")
        BT = A.alloc([4, SEQ], BF16)
        CT = A.alloc([4, SEQ], BF16)
        bBT, bCT = Buf(), Buf()
        A.mark()
        cin = [A.alloc([3 + SEQ]) for _ in range(2)]
        bcin = [Buf() for _ in range(2)]
        acc = [A.alloc([SEQ]) for _ in range(2)]
        bacc = [Buf() for _ in range(2)]
        tst = [A.alloc([NT, 128]) for _ in range(2)]
        btst = [Buf() for _ in range(2)]
        cinb = [A.alloc([3 + SEQ], BF16) for _ in range(2)]
        bcinb = [Buf() for _ in range(2)]
        dgm = [A.alloc([4, 128], BF16) for _ in range(2)]
        bdgm = [Buf() for _ in range(2)]
        for j in range(2):
            self.memset(cin[j][:, 0:3], 0.0, [bcin[j]])
        mTMv = self.mTM.rearrange("(i p) c -> p i c", p=128)
        for f in range(16):
            ci, bci, ac, bac = cin[f % 2], bcin[f % 2], acc[f % 2], bacc[f % 2]
            self.dma(ci[:, 3:3 + SEQ], self.pFM[(F_XBC + f) * 128:(F_XBC + f + 1) * 128, :], reads=[self.b_pFM[F_XBC + f]], writes=[bci])
            cbf, bcbf, dgm_, bdgm_ = cinb[f % 2], bcinb[f % 2], dgm[f % 2], bdgm[f % 2]
            self.cp(cbf, ci, [bci], [bcbf], eng="dve")
            for j in range(4):
                self.ts(dgm_[:, j, :], self.identb, cw[:, f, j:j + 1], None, ALU.mult, reads=[self.b_identb, self.b_pfm], writes=[bdgm_], eng="act")
            for tb in range(4):
                ps, bp = self.psum()
                for j in range(4):
                    self.mm(ps, dgm_[:, j, :], cbf[:, tb * 512 + j:tb * 512 + j + 512], j == 0, j == 3, [bdgm_, bcbf], [bp])
                self.act(ac[:, tb * 512:(tb + 1) * 512], ps, AF.Silu, [bp, self.b_pfm], [bac], bias=cb[:, f:f + 1])
            if f < 12:
                ts_, bts = tst[f % 2], btst[f % 2]
                for q in range(4):
                    ps, bp = self.psum()
                    for k in range(4):
                        i = q * 4 + k
                        self.tr(ps[:, k * 128:(k + 1) * 128], ac[:, i * 128:(i + 1) * 128], [bac], [bp])
                    self.cp(ts_[:, q * 4:q * 4 + 4, :], ps.rearrange("p (a b) -> p a b", a=4), [bp], [bts])
                self.dma(mTMv[:, :, f * 128:(f + 1) * 128], ts_, reads=[bts], writes=[self.b_mTM], q="pool")
            if 8 <= f < 12:
                self.cp(BT[:, f - 8, :], ac, [bac], [bBT], eng="pool")
            if f >= 12:
                self.cp(CT[:, f - 12, :], ac, [bac], [bCT], eng="pool")
        A.release()
        self.S.barrier()
        rows = A.alloc([3, 16])
        brows = Buf()
        self.rowb(l, "dt_bias", rows[:, 0, :], brows)
        self.rowb(l, "a_log", rows[:, 1, :], brows)
        self.rowb(l, "ssm_d", rows[:, 2, :], brows)
        self.act(rows[:, 1, :], rows[:, 1, :], AF.Exp, [brows], [brows])
        self.ts(rows[:, 1, :], rows[:, 1, :], -1.0, None, ALU.mult, reads=[brows], writes=[brows])
        nw = A.alloc([DM])
        bnw = Buf()
        self.rowb(l, "ssm_norm", nw, bnw)
        H = A.alloc([16, 64])
        Hb = A.alloc([16, 64], BF16)
        bH, bHb = Buf(), Buf()
        self.memset(H, 0.0, [bH])
        self.memset(Hb, 0.0, [bHb])
        mt = [A.alloc([1536]) for _ in range(2)]
        zt = [A.alloc([DM]) for _ in range(2)]
        dtr = [A.alloc([16]) for _ in range(2)]
        bld = [Buf() for _ in range(2)]
        sm2 = [A.alloc([8, 16]) for _ in range(2)]
        bsm2 = [Buf() for _ in range(2)]
        dtabc = A.alloc([2, 16, 128])
        bdtabc = Buf()
        decay = A.alloc([16, 128], BF16)
        bdecay = Buf()
        CBs = A.alloc([4, 128], BF16)
        bCBs = Buf()
        Lm2 = [A.alloc([16, 128], BF16) for _ in range(2)]
        bLm2 = [Buf() for _ in range(2)]
        xdt2 = [A.alloc([2, 16, 64], BF16) for _ in range(2)]
        bxdt2 = [Buf() for _ in range(2)]
        Bbf2 = [A.alloc([512], BF16) for _ in range(2)]
        bBbf2 = [Buf() for _ in range(2)]
        y = A.alloc([16, 64])
        t1 = A.alloc([16, 64])
        by, bt1 = Buf(), Buf()
        ss = A.alloc([2])
        yst = [A.alloc([8, 128], BF16) for _ in range(2)]
        byst = [Buf() for _ in range(2)]
        tri, ones, ident, negm = self.c("tri"), self.c("ones"), self.c("ident"), self.c("negmask4")
        bc = self.b_cst
        ysv = self.ys[0].rearrange("(c p) t -> p c t", p=128)
        def stA(i):
                m, z, dr, bl = mt[i % 2], zt[i % 2], dtr[i % 2], bld[i % 2]
                sl = slice(i * 128, (i + 1) * 128)
                sm, bsm, Lm, bLm, xdt, bxdt, Bbf, bBbf = sm2[i % 2], bsm2[i % 2], Lm2[i % 2], bLm2[i % 2], xdt2[i % 2], bxdt2[i % 2], Bbf2[i % 2], bBbf2[i % 2]
                xs = m[:, 0:1024].rearrange("p (h d) -> p h d", h=16)
                t, ab, dt, dta, acum, eac, dte, cd = (sm[:, q, :] for q in range(8))
                self.dma(m, self.mTM[sl, :], reads=[self.b_mTM], writes=[bl])
                self.dma(z, self.pTM[sl, T_Z:T_Z + 1024], reads=self.b_pTM[i][0:2], writes=[bl])
                self.dma(dr, self.pTM[sl, T_DT:T_DT + 16], reads=[self.b_pTM[i][8]], writes=[bl])
                self.tt(t, dr, rows[:, 0, :], ALU.add, [bl, brows], [bsm])
                self.act(ab, t, AF.Abs, [bsm], [bsm])
                self.act(ab, ab, AF.Exp, [bsm], [bsm], scale=-1.0)
                self.act(ab, ab, AF.Ln, [bsm], [bsm], bias=1.0)
                self.stt(dt, t, 0.0, ab, ALU.max, ALU.add, [bsm], [bsm])
                self.tt(dta, dt, rows[:, 1, :], ALU.mult, [bsm, brows], [bsm])
                self.cp(dtabc[:, 0, :, :], dta.unsqueeze(2).to_broadcast([128, 16, 128]), [bsm], [bdtabc], eng="dve")
                self.ts(dtabc[:, 1, :, :], dtabc[:, 0, :, :], -1.0, None, ALU.mult, reads=[bdtabc], writes=[bdtabc], eng="pool")
                ps, bp = self.psum()
                self.mm(ps[:, 0:16], tri, dta, True, True, [bc, bsm], [bp])
                self.mm(ps[:, 16:32], ones, dta, True, True, [bc, bsm], [bp])
                self.cp(acum, ps[:, 0:16], [bp], [bsm], eng="dve")
                self.act(eac, ps[:, 0:16], AF.Exp, [bp], [bsm])
                self.act(cd, ps[:, 16:32], AF.Exp, [bp], [bsm])
                self.tt(dte, ps[:, 16:32], acum, ALU.subtract, [bp, bsm], [bsm])
                self.act(dte, dte, AF.Exp, [bsm], [bsm])
                self.tt(dte, dte, dt, ALU.mult, [bsm], [bsm])
                self.tt(xdt[:, 0, :, :], xs, dt.unsqueeze(2).to_broadcast([128, 16, 64]), ALU.mult, [bl, bsm], [bxdt])
                self.tt(xdt[:, 1, :, :], xs, dte.unsqueeze(2).to_broadcast([128, 16, 64]), ALU.mult, [bl, bsm], [bxdt], eng="pool")
                self.cp(Bbf, m[:, 1024:1536], [bl], [bBbf], eng="pool")
                ps, bp = self.psum()
                for g in range(4):
                    self.mm(ps[:, g * 128:(g + 1) * 128], BT[:, g, sl], CT[:, g, sl], True, True, [bBT, bCT], [bp])
                self.cp(CBs, ps.rearrange("p (a b) -> p a b", a=4), [bp], [bCBs], eng="act")
                for g in range(4):
                    ps, bp = self.psum()
                    for hh in range(4):
                        h = g * 4 + hh
                        o_ = ps[:, hh * 128:(hh + 1) * 128]
                        self.mm(o_, dtabc[:, 0, h, :], tri, True, False, [bdtabc, bc], [bp])
                        self.mm(o_, tri, dtabc[:, 1, h, :], False, False, [bdtabc, bc], [bp])
                        self.mm(o_, ident, negm[:, 0:128], False, True, [bc], [bp])
                    self.act(decay[:, 4 * g:4 * g + 4, :], ps.rearrange("p (a b) -> p a b", a=4), AF.Exp, [bp], [bdecay])
                self.tt(Lm.rearrange("p (g a) b -> p g a b", g=4), decay.rearrange("p (g a) b -> p g a b", g=4),
                        CBs.unsqueeze(2).to_broadcast([128, 4, 4, 128]), ALU.mult, [bdecay, bCBs], [bLm])

        def stB(i):
                m, z, dr, bl = mt[i % 2], zt[i % 2], dtr[i % 2], bld[i % 2]
                sl = slice(i * 128, (i + 1) * 128)
                sm, bsm, Lm, bLm, xdt, bxdt, Bbf, bBbf = sm2[i % 2], bsm2[i % 2], Lm2[i % 2], bLm2[i % 2], xdt2[i % 2], bxdt2[i % 2], Bbf2[i % 2], bBbf2[i % 2]
                xs = m[:, 0:1024].rearrange("p (h d) -> p h d", h=16)
                t, ab, dt, dta, acum, eac, dte, cd = (sm[:, q, :] for q in range(8))
                yv = y.rearrange("p h d -> p (h d)")
                t1v = t1.rearrange("p h d -> p (h d)")
                for half in range(2):
                    psd, bpd = self.psum()
                    pso, bpo = self.psum()
                    for hh in range(8):
                        h = half * 8 + hh
                        self.mm(psd[:, hh * 64:(hh + 1) * 64], Lm[:, h, :], xdt[:, 0, h, :], True, True, [bLm, bxdt], [bpd])
                        self.mm(pso[:, hh * 64:(hh + 1) * 64], CT[:, h // 4, sl], Hb[:, h, :], True, True, [bCT, bHb], [bpo])
                    hs = slice(half * 8, half * 8 + 8)
                    self.tt(t1[:, hs, :], pso.rearrange("p (h d) -> p h d", h=8), eac[:, hs].unsqueeze(2).to_broadcast([128, 8, 64]),
                            ALU.mult, [bpo, bsm], [bt1])
                    self.tt(y[:, hs, :], psd.rearrange("p (h d) -> p h d", h=8), t1[:, hs, :], ALU.add, [bpd, bt1], [by])
                self.tt(t1, xs, rows[:, 2, :].unsqueeze(2).to_broadcast([128, 16, 64]), ALU.mult, [bl, brows], [bt1], eng="pool")
                self.tt(y, y, t1, ALU.add, [by, bt1], [by])
                for half in range(2):
                    pss, bps = self.psum()
                    for hh in range(8):
                        h = half * 8 + hh
                        g = h // 4
                        self.mm(pss[:, hh * 64:(hh + 1) * 64], Bbf[:, g * 128:(g + 1) * 128], xdt[:, 1, h, :], True, True, [bBbf, bxdt], [bps])
                    hs = slice(half * 8, half * 8 + 8)
                    self.tt(H[:, hs, :], H[:, hs, :], cd[:, hs].unsqueeze(2).to_broadcast([128, 8, 64]), ALU.mult, [bH, bsm, bpo], [bH])
                    self.tt(H[:, hs, :], H[:, hs, :], pss.rearrange("p (h d) -> p h d", h=8), ALU.add, [bH, bps], [bH])
                self.cp(Hb, H, [bH], [bHb], eng="act")
                self.act(z, z, AF.Silu, [bl], [bl])
                self.tt(yv, yv, z, ALU.mult, [by, bl], [by])
                self.rms_scale(yv, by, ss, bt1, t1v, bt1)
                self.stt(yv, yv, ss[:, 0:1], nw, ALU.mult, ALU.mult, [by, bt1, bnw], [by])
                yo, byo = yst[i % 2], byst[i % 2]
                for half in range(2):
                    ps, bp = self.psum()
                    for k in range(4):
                        kk = half * 4 + k
                        self.tr(ps[:, k * 128:(k + 1) * 128], yv[:, kk * 128:(kk + 1) * 128], [by], [bp])
                    self.cp(yo[:, half * 4:half * 4 + 4, :], ps.rearrange("p (a b) -> p a b", a=4), [bp], [byo])
                self.dma(ysv[:, :, sl], yo, reads=[byo], writes=[self.b_ys[0][i]], q="pool")

        PA, PB = [0, 1, 2, 3], [4, 5, 6, 7]
        self.ps_pool = PA
        stA(0)
        for i in range(NT):
            self.ps_pool = PB
            ob = self.S.record(lambda: stB(i))
            oa = []
            if i + 1 < NT:
                self.ps_pool = PA
                oa = self.S.record(lambda: stA(i + 1))
            self.S.replay_interleaved(oa, ob)
        self.ps_pool = list(range(8))
        A.release()

    def lru(self, l, q=None):
        A = self.A
        A.mark()
        cw = self.pf("lru_cw").rearrange("p (t j) -> p t j", j=4)
        cb, ba, bi, lam = self.pf("lru_cb"), self.pf("lru_ba"), self.pf("lru_bi"), self.pf("lru_lam")
        wa = A.alloc([8, 128], BF16)
        wi = A.alloc([8, 128], BF16)
        bwa = Buf()
        self.dma(wa, self.d["lru_w_a_bd"][l].rearrange("t p n -> p t n"), writes=[bwa], q="pool")
        self.dma(wi, self.d["lru_w_i_bd"][l].rearrange("t p n -> p t n"), writes=[bwa], q="pool")
        cf = A.alloc([2, 8])
        bcf = Buf()
        tmp = A.alloc([2, 8])
        self.act(tmp[:, 0, :], lam, AF.Abs, [self.b_pfm], [bcf])
        self.act(tmp[:, 0, :], tmp[:, 0, :], AF.Exp, [bcf], [bcf], scale=-1.0)
        self.act(tmp[:, 0, :], tmp[:, 0, :], AF.Ln, [bcf], [bcf], bias=1.0)
        self.ts(tmp[:, 1, :], lam, -1.0, 0.0, ALU.mult, ALU.max, reads=[self.b_pfm], writes=[bcf])
        self.tt(tmp[:, 0, :], tmp[:, 0, :], tmp[:, 1, :], ALU.add, [bcf], [bcf])
        self.ts(cf[:, 0, :], tmp[:, 0, :], -8.0, None, ALU.mult, reads=[bcf], writes=[bcf])
        self.ts(cf[:, 1, :], tmp[:, 0, :], -16.0, None, ALU.mult, reads=[bcf], writes=[bcf])
        cin = [A.alloc([3 + SEQ]) for _ in range(2)]
        gin = [A.alloc([SEQ]) for _ in range(2)]
        bin_ = [Buf() for _ in range(2)]
        for j in range(2):
            self.memset(cin[j][:, 0:3], 0.0, [bin_[j]])
        xc = A.alloc([SEQ])
        xcb = A.alloc([SEQ], BF16)
        r = A.alloc([SEQ])
        ii = A.alloc([SEQ])
        a = A.alloc([SEQ])
        u = A.alloc([SEQ])
        hh_ = r
        g2 = A.alloc([SEQ])
        yo = [A.alloc([SEQ], BF16) for _ in range(2)]
        byo = [Buf() for _ in range(2)]
        b_xc, b_xcb, b_r, b_i, b_a, b_u, b_h, b_g = (Buf() for _ in range(8))
        b_h = b_r
        bpf = self.b_pfm
        for f in range(8):
            ci, gi, bl = cin[f % 2], gin[f % 2], bin_[f % 2]
            self.dma(ci[:, 3:3 + SEQ], self.pFM[(F_LX + f) * 128:(F_LX + f + 1) * 128, :], reads=[self.b_pFM[F_LX + f]], writes=[bl], q=q or "sp")
            self.dma(gi, self.pFM[(F_LG + f) * 128:(F_LG + f + 1) * 128, :], reads=[self.b_pFM[F_LG + f]], writes=[bl], q=q or "sp")
            self.conv_fm(xc, ci, 4, cw[:, f, :], cb[:, f:f + 1], [bl, bpf], [b_xc], SEQ)
            self.cp(xcb, xc, [b_xc], [b_xcb], eng="act")
            for tb in range(4):
                ts_ = slice(tb * 512, (tb + 1) * 512)
                ps, bp = self.psum()
                self.mm(ps, wa[:, f, :], xcb[:, ts_], True, True, [bwa, b_xcb], [bp])
                self.act(r[:, ts_], ps, AF.Sigmoid, [bp, bpf], [b_r], bias=ba[:, f:f + 1])
                ps, bp = self.psum()
                self.mm(ps, wi[:, f, :], xcb[:, ts_], True, True, [bwa, b_xcb], [bp])
                self.act(ii[:, ts_], ps, AF.Sigmoid, [bp, bpf], [b_i], bias=bi[:, f:f + 1])
            self.act(a, r, AF.Exp, [b_r, bcf], [b_a], scale=cf[:, 0, f:f + 1])
            self.act(u, r, AF.Exp, [b_r, bcf], [b_u], scale=cf[:, 1, f:f + 1])
            self.act(u, u, AF.Sqrt, [b_u], [b_u], scale=-1.0, bias=1.0)
            self.tt(ii, ii, xc, ALU.mult, [b_i, b_xc], [b_i], eng="pool")
            self.tt(u, u, ii, ALU.mult, [b_u, b_i], [b_u])
            self.S.add("dve", lambda e, a=a, u=u, o=hh_: e.tensor_tensor_scan(out=o, data0=a, data1=u, initial=0.0, op0=ALU.mult, op1=ALU.add),
                       reads=[b_a, b_u], writes=[b_h])
            self.tt(g2, gi, gi, ALU.mult, [bl], [b_g], eng="pool")
            self.ts(g2, g2, 0.044715, 1.0, ALU.mult, ALU.add, reads=[b_g], writes=[b_g], eng="pool")
            self.tt(g2, g2, gi, ALU.mult, [b_g, bl], [b_g], eng="pool")
            self.act(g2, g2, AF.Sigmoid, [b_g], [b_g], scale=1.5957691216057308)
            self.tt(g2, g2, gi, ALU.mult, [b_g, bl], [b_g], eng="pool")
            self.tt(yo[f % 2], hh_, g2, ALU.mult, [b_h, b_g], [byo[f % 2]])
            self.dma(self.ys[1][f * 128:(f + 1) * 128, :], yo[f % 2], reads=[byo[f % 2]], writes=[self.b_ysF[1]], q=q or "pool")
        A.release()

    def retention(self, l):
        A = self.A
        A.mark()
        nwt = self.pf("ret_nw")
        cosT = self.c("cos").rearrange("p (i d) -> p i d", i=16)
        sinT = self.c("sin").rearrange("p (i d) -> p i d", i=16)
        GT = self.c("GT")
        gqA = self.c("gqA")[0:64, :].rearrange("p (f t) -> p f t", f=8)
        gqB = self.c("gqB")[0:64, :].rearrange("p (f t) -> p f t", f=8)
        kend, cdt = self.c("kend"), self.c("cdt")
        bc = self.b_cst
        qk = [A.alloc([16, 64]) for _ in range(2)]
        vt = [A.alloc([DM]) for _ in range(2)]
        gt = [A.alloc([8, 128]) for _ in range(2)]
        bl = [Buf() for _ in range(2)]
        rot = A.alloc([16, 64])
        tmp = A.alloc([16, 32])
        brot = Buf()
        ksc = A.alloc([8, 64], BF16)
        bksc = Buf()
        qT = A.alloc([8, 128], BF16, parts=64)
        kT = A.alloc([8, 128], BF16, parts=64)
        qTA = A.alloc([8, 128], BF16, parts=64)
        qTB = A.alloc([8, 128], BF16, parts=64)
        bqT = Buf()
        vb = A.alloc([DM], BF16)
        bvb = Buf()
        ST = A.alloc([8, 128], BF16)
        bST = Buf()
        R = A.alloc([3, 8, 128], parts=64)
        Rb = A.alloc([3, 8, 128], BF16, parts=64)
        bR, bRb = Buf(), Buf()
        self.memset(R, 0.0, [bR])
        self.memset(Rb, 0.0, [bRb])
        ysb = A.alloc([8, 128])
        ysq = A.alloc([8, 128])
        bys = Buf()
        mean = A.alloc([8, 128])
        var = A.alloc([8, 128])
        bmv = Buf()
        yo = [A.alloc([8, 128], BF16) for _ in range(2)]
        byo = [Buf() for _ in range(2)]
        onesm = A.alloc([128])
        bon = Buf()
        self.ts(onesm, self.c("ones"), 1.0 / 128.0, None, ALU.mult, reads=[bc], writes=[bon])
        ysv = self.ys[2].rearrange("(c p) t -> p c t", p=128)
        gv = self.pFM[F_RG * 128:(F_RG + 8) * 128, :].rearrange("(c p) t -> p c t", p=128)
        pend_st = None
        for i in range(NT):
            q_, v_, g_, b_ = qk[i % 2], vt[i % 2], gt[i % 2], bl[i % 2]
            sl = slice(i * 128, (i + 1) * 128)
            self.dma(q_.rearrange("p h d -> p (h d)"), self.pTM[sl, T_QK:T_QK + 1024], reads=self.b_pTM[i][2:4], writes=[b_])
            self.dma(v_, self.pTM[sl, T_RV:T_RV + 1024], reads=self.b_pTM[i][4:6], writes=[b_])
            self.dma(g_, gv[:, :, sl], reads=self.b_pFM[F_RG:F_RG + 8], writes=[b_])
            if pend_st is not None:
                pend_st()
                pend_st = None
            x1, x2 = q_[:, :, 0:32], q_[:, :, 32:64]
            cb_ = cosT[:, i, :].unsqueeze(1).to_broadcast([128, 16, 32])
            sb_ = sinT[:, i, :].unsqueeze(1).to_broadcast([128, 16, 32])
            self.tt(rot[:, :, 0:32], x1, cb_, ALU.mult, [b_, bc], [brot])
            self.tt(tmp, x2, sb_, ALU.mult, [b_, bc], [brot], eng="pool")
            self.tt(rot[:, :, 0:32], rot[:, :, 0:32], tmp, ALU.subtract, [brot], [brot])
            self.tt(rot[:, :, 32:64], x1, sb_, ALU.mult, [b_, bc], [brot])
            self.tt(tmp, x2, cb_, ALU.mult, [b_, bc], [brot], eng="pool")
            self.tt(rot[:, :, 32:64], rot[:, :, 32:64], tmp, ALU.add, [brot], [brot])
            self.tt(ksc, rot[:, 8:16, :], kend.unsqueeze(2).to_broadcast([128, 8, 64]), ALU.mult, [brot, bc], [bksc])
            self.cp(vb, v_, [b_], [bvb], eng="act")
            rv = rot.rearrange("p h d -> p (h d)")
            for quad in range(4):
                ps, bp = self.psum()
                for k in range(4):
                    hh = quad * 4 + k
                    self.tr(ps[0:64, k * 128:(k + 1) * 128], rv[:, hh * 64:(hh + 1) * 64], [brot], [bp])
                psv = ps[0:64, :].rearrange("p (a b) -> p a b", a=4)
                if quad < 2:
                    hs = slice(quad * 4, quad * 4 + 4)
                    self.cp(qT[:, hs, :], psv, [bp], [bqT], eng="act")
                    self.tt(qTA[:, hs, :], psv, gqA[:, hs, :], ALU.mult, [bp, bc], [bqT])
                    self.tt(qTB[:, hs, :], psv, gqB[:, hs, :], ALU.mult, [bp, bc], [bqT])
                else:
                    hs = slice((quad - 2) * 4, (quad - 2) * 4 + 4)
                    self.ts(kT[:, hs, :], psv, 0.125, None, ALU.mult, reads=[bp], writes=[bqT])
            for half in range(2):
                ps, bp = self.psum()
                for hh in range(4):
                    h = half * 4 + hh
                    self.mm(ps[:, hh * 128:(hh + 1) * 128], kT[:, h, :], qT[:, h, :], True, True, [bqT], [bp])
                self.tt(ST[:, half * 4:half * 4 + 4, :].rearrange("p a b -> p (a b)"), ps, GT[:, half * 512:(half + 1) * 512],
                        ALU.mult, [bp, bc], [bST])
            for cch in range(2):
                pr = slice(cch * 64, (cch + 1) * 64)
                for half in range(2):
                    ps, bp = self.psum()
                    for hh in range(4):
                        h = half * 4 + hh
                        self.mm(ps[0:64, hh * 128:(hh + 1) * 128], ksc[pr, h, :], vb[pr, h * 128:(h + 1) * 128], True, True, [bksc, bvb], [bp])
                    hs = slice(half * 4, half * 4 + 4)
                    self.tt(R[:, cch + 1, hs, :], R[:, cch, hs, :], cdt[0:64, hs].unsqueeze(2).to_broadcast([64, 4, 128]), ALU.mult, [bR, bc], [bR])
                    self.tt(R[:, cch + 1, hs, :], R[:, cch + 1, hs, :], ps[0:64, :].rearrange("p (a b) -> p a b", a=4), ALU.add, [bR, bp], [bR])
                self.cp(Rb[:, cch + 1, :, :], R[:, cch + 1, :, :], [bR], [bRb], eng="act")
            for half in range(2):
                ps, bp = self.psum()
                for hh in range(4):
                    h = half * 4 + hh
                    o = ps[:, hh * 128:(hh + 1) * 128]
                    self.mm(o, vb[:, h * 128:(h + 1) * 128], ST[:, h, :], True, False, [bvb, bST], [bp])
                    self.mm(o, Rb[:, 0, h, :], qTA[:, h, :], False, False, [bRb, bqT], [bp])
                    self.mm(o, Rb[:, 1, h, :], qTB[:, h, :], False, True, [bRb, bqT], [bp])
                hs = slice(half * 4, half * 4 + 4)
                self.cp(ysb[:, hs, :], ps.rearrange("p (a b) -> p a b", a=4), [bp], [bys], eng="dve")
                self.act(ysq[:, hs, :], ps.rearrange("p (a b) -> p a b", a=4), AF.Square, [bp], [bys])
            self.cp(R[:, 0, :, :], R[:, 2, :, :], [bR], [bR], eng="pool")
            self.cp(Rb[:, 0, :, :], Rb[:, 2, :, :], [bRb], [bRb], eng="pool")
            for half in range(2):
                hs = slice(half * 4, half * 4 + 4)
                ps, bp = self.psum()
                self.mm(ps, onesm, ysb[:, hs, :].rearrange("p a b -> p (a b)"), True, True, [bon, bys], [bp])
                self.cp(mean[:, hs, :], ps.rearrange("p (a b) -> p a b", a=4), [bp], [bmv], eng="act")
                ps2, bp2 = self.psum()
                self.mm(ps2, onesm, ysq[:, hs, :].rearrange("p a b -> p (a b)"), True, True, [bon, bys], [bp2])
                self.tt(var[:, hs, :], mean[:, hs, :], mean[:, hs, :], ALU.mult, [bmv], [bmv], eng="pool")
                self.tt(var[:, hs, :], ps2.rearrange("p (a b) -> p a b", a=4), var[:, hs, :], ALU.subtract, [bp2, bmv], [bmv])
            self.act(var, var, AF.Ln, [bmv], [bmv], bias=EPS)
            self.act(var, var, AF.Exp, [bmv], [bmv], scale=-0.5)
            self.tt(ysb, ysb, mean, ALU.subtract, [bys, bmv], [bys])
            self.tt(ysb, ysb, var, ALU.mult, [bys, bmv], [bys])
            self.tt(ysb, ysb, nwt.unsqueeze(2).to_broadcast([128, 8, 128]), ALU.mult, [bys, self.b_pfm], [bys], eng="pool")
            self.act(g_, g_, AF.Silu, [b_], [b_])
            self.tt(yo[i % 2], ysb, g_, ALU.mult, [bys, b_], [byo[i % 2]])
            pend_st = (lambda i=i, sl=sl: self.dma(ysv[:, :, sl], yo[i % 2], reads=[byo[i % 2]], writes=[self.b_ys[2][i]], q="sp"))
        pend_st()
        A.release()

    def rwkv(self, l):
        A = self.A
        bc = self.b_cst
        A.mark()
        mu = self.pf("mu_fm")
        w0, a0, k_k, k_a, r_k = (self.pf(n) for n in ("w0", "a0", "k_k", "k_a", "r_k"))
        w2a2 = A.alloc([DM], BF16)
        g2w = A.alloc([DM], BF16)
        bwl = Buf()
        self.dma(w2a2, self.d["w2a2"][l], writes=[bwl], q="pool")
        self.dma(g2w, self.d["g2"][l], writes=[bwl], q="pool")
        omka = A.alloc([8])
        bomka = Buf()
        self.ts(omka, k_a, -1.0, 1.0, ALU.mult, ALU.add, reads=[self.b_pfm], writes=[bomka])
        ecd = A.alloc([2, 16])
        becd = [Buf() for _ in range(2)]
        self.ecdD = self.nc.dram_tensor("ecdD%d" % l, [DM, 16], F32).ap()
        becdD = Buf()
        rks = A.alloc([16, 16])
        brks = Buf()
        A.mark()
        HS = SEQ // 2
        NH = NT // 2
        rmask = A.alloc([NH, 128], BF16)
        brm = Buf()
        self.memset(rmask, 1.0, [brm])
        self.memset(rmask[:, :, 0:1], 0.0, [brm])
        rmaskf = rmask.rearrange("p a b -> p (a b)")
        lo = A.alloc([SEQ], BF16)
        gl = A.alloc([SEQ], BF16)
        blo = Buf()
        lin = [[A.alloc([1 + HS]) for _ in range(2)] for _ in range(2)]
        blin = [[Buf() for _ in range(2)] for _ in range(2)]
        Wt = [[A.alloc([HS]) for _ in range(9)] for _ in range(2)]
        bWt = [[Buf() for _ in range(9)] for _ in range(2)]
        tstt = [A.alloc([NH, 2, 128], BF16) for _ in range(2)]
        btstt = [Buf() for _ in range(2)]
        fstt = [[A.alloc([HS], BF16) for _ in range(4)] for _ in range(2)]
        bfstt = [[Buf() for _ in range(4)] for _ in range(2)]
        ecd2 = [A.alloc([NH]) for _ in range(2)]
        becd2 = [Buf() for _ in range(2)]
        bpf = self.b_pfm
        wTMv = self.wTM.rearrange("(i p) f c d -> p i f c d", p=128)

        def mixload(ft, mcol, dst, bdst, st, k, half):
            li, bl = lin[st][k], blin[st][k]
            if half == 0:
                self.memset(li[:, 0:1], 0.0, [bl])
                self.dma(li[:, 1:1 + HS], self.pFM[ft * 128:(ft + 1) * 128, 0:HS], reads=[self.b_pFM[ft]], writes=[bl])
            else:
                self.dma(li, self.pFM[ft * 128:(ft + 1) * 128, HS - 1:SEQ], reads=[self.b_pFM[ft]], writes=[bl])
            self.tt(dst, li[:, 0:HS], li[:, 1:1 + HS], ALU.subtract, [bl], [bdst], eng="pool")
            self.stt(dst, dst, mu[:, mcol:mcol + 1], li[:, 1:1 + HS], ALU.mult, ALU.add, [bdst, bl, bpf], [bdst])

        for half in range(2):
            hsl = slice(half * HS, (half + 1) * HS)
            mixload(F_LO, 16, Wt[half][0], bWt[half][0], half, 0, half)
            self.act(lo[0:64, hsl], Wt[half][0][0:64, :], AF.Tanh, [bWt[half][0]], [blo])
            self.cp(lo[64:128, hsl], Wt[half][0][64:128, :], [bWt[half][0]], [blo], eng="dve")
            mixload(F_GL, 17, Wt[half][1], bWt[half][1], half, 1, half)
            self.act(gl[:, hsl], Wt[half][1], AF.Sigmoid, [bWt[half][1]], [blo])

        def unit(f, half, st):
            rm, km, lw, aa, kk, t0, t1, t2, t3 = Wt[st]
            b_rm, b_km, b_lw, b_aa, b_kk, b_t0, b_t1, b_t2, b_t3 = bWt[st]
            fst, bfst, tst, btst = fstt[st], bfstt[st], tstt[st], btstt[st]
            fs = slice(f * 128, (f + 1) * 128)
            T0 = half * HS
            hsl = slice(T0, T0 + HS)
            mixload(F_WR + f, f, rm, b_rm, st, 0, half)
            mixload(F_WK + f, 8 + f, km, b_km, st, 1, half)
            for fn_ in defer[st]:
                fn_()
            defer[st] = []
            for tb in range(2):
                ts_ = slice(tb * 512, (tb + 1) * 512)
                gs_ = slice(T0 + tb * 512, T0 + (tb + 1) * 512)
                ps, bp = self.psum()
                self.mm(ps, w2a2[0:64, fs], lo[0:64, gs_], True, True, [bwl, blo], [bp])
                self.act(lw[:, ts_], ps, AF.Sigmoid, [bp, bpf], [b_lw], bias=w0[:, f:f + 1])
                ps, bp = self.psum()
                self.mm(ps, w2a2[64:128, fs], lo[64:128, gs_], True, True, [bwl, blo], [bp])
                self.act(aa[:, ts_], ps, AF.Sigmoid, [bp, bpf], [b_aa], bias=a0[:, f:f + 1])
                ps, bp = self.psum()
                self.mm(ps, g2w[:, fs], gl[:, gs_], True, True, [bwl, blo], [bp])
                self.cp(t0[:, ts_], ps, [bp], [b_t0])
            self.dma(self.wgate[fs, hsl], t0, reads=[b_t0], writes=[self.b_wgate[f]], q="pool")
            self.ts(lw, lw, -0.6065306597126334, None, ALU.mult, reads=[b_lw], writes=[b_lw], eng="pool")
            self.ts(kk, km, k_k[:, f:f + 1], None, ALU.mult, reads=[b_km, bpf], writes=[b_kk], eng="act")
            self.act(t1, kk, AF.Square, [b_kk], [b_t1])
            for tb in range(2):
                ts_ = slice(tb * 512, (tb + 1) * 512)
                ps, bp = self.psum()
                self.mm(ps, self.c("blockones"), t1[:, ts_], True, True, [bc, b_t1], [bp])
                self.ts(t2[:, ts_], ps, 1e-24, None, ALU.max, reads=[bp], writes=[b_t2])
            self.act(t2, t2, AF.Ln, [b_t2], [b_t2])
            self.act(t2, t2, AF.Exp, [b_t2], [b_t2], scale=-0.5)
            self.tt(kk, kk, t2, ALU.mult, [b_kk, b_t2], [b_kk])
            self.ts(t3, aa, k_a[:, f:f + 1], omka[:, f:f + 1], ALU.mult, ALU.add, reads=[b_aa, bpf, bomka], writes=[b_t3], eng="act")
            self.tt(km, km, t3, ALU.mult, [b_km, b_t3], [b_km])
            self.tt(t1, rm, km, ALU.mult, [b_rm, b_km], [b_t1], eng="pool")
            self.ts(t1, t1, r_k[:, f:f + 1], None, ALU.mult, reads=[b_t1, bpf], writes=[b_t1], eng="act")
            ps, bp = self.psum()
            for i in range(NH):
                self.mm(ps[:, 2 * i:2 * i + 2], t1[:, i * 128:(i + 1) * 128], self.c("headsel"), True, True, [b_t1, bc], [bp])
            self.cp(rks[:, half * NH:(half + 1) * NH, 2 * f:2 * f + 2], ps[:, 0:2 * NH].rearrange("p (i j) -> p i j", j=2), [bp], [brks], eng="act")
            cl, b_cl = t2, b_t2
            self.S.add("dve", lambda e, o=cl, m=rmaskf, d1=lw: e.tensor_tensor_scan(out=o, data0=m, data1=d1, initial=0.0, op0=ALU.mult, op1=ALU.add),
                       reads=[b_lw, brm], writes=[b_cl])
            clv = cl.rearrange("p (i t) -> p i t", i=NH)
            self.act(ecd2[st], clv[:, :, 127], AF.Exp, [b_cl], [becd2[st]])
            defer[st].append(lambda f=f, half=half, st=st, fs=fs: self.dma(self.ecdD[fs, half * NH:(half + 1) * NH], ecd2[st], reads=[becd2[st]], writes=[becdD], q="sp"))
            self.tt(aa, aa, kk, ALU.mult, [b_aa, b_kk], [b_aa])
            self.act(t3, cl, AF.Exp, [b_cl], [b_t3])
            self.tt(fst[0], rm, t3, ALU.mult, [b_rm, b_t3], [bfst[0]], eng="pool")
            self.tt(t0, cl, lw, ALU.subtract, [b_cl, b_lw], [b_t0], eng="pool")
            self.act(t0, t0, AF.Exp, [b_t0], [b_t0])
            self.stt(fst[1], kk, -1.0, t0, ALU.mult, ALU.mult, [b_kk, b_t0], [bfst[1]])
            self.act(t3, cl, AF.Exp, [b_cl], [b_t3], scale=-1.0)
            self.tt(fst[2], aa, t3, ALU.mult, [b_aa, b_t3], [bfst[2]])
            self.tt(fst[3], km, t3, ALU.mult, [b_km, b_t3], [bfst[3]], eng="pool")
            for q in range(4):
                defer[st].append(lambda q=q, f=f, fs=fs, hsl=hsl, fst=fst, bfst=bfst: self.dma(self.wFM[q][fs, hsl], fst[q], reads=[bfst[q]], writes=[self.b_wFM[f]], q="sp"))
            t1v = t1.rearrange("p (i t) -> p i t", i=NH)
            self.tt(t1v, clv[:, :, 127:128].to_broadcast([128, NH, 128]), clv, ALU.subtract, [b_cl], [b_t1])
            self.act(t1, t1, AF.Exp, [b_t1], [b_t1])
            self.tt(aa, aa, t1, ALU.mult, [b_aa, b_t1], [b_aa])
            self.tt(km, km, t1, ALU.mult, [b_km, b_t1], [b_km], eng="pool")
            for cidx, (src, bsrc) in enumerate(((aa, b_aa), (km, b_km))):
                for q in range(2):
                    ps, bp = self.psum()
                    for k in range(4):
                        i = q * 4 + k
                        self.tr(ps[:, k * 128:(k + 1) * 128], src[:, i * 128:(i + 1) * 128], [bsrc], [bp])
                    self.cp(tst[:, q * 4:q * 4 + 4, cidx, :], ps.rearrange("p (a b) -> p a b", a=4), [bp], [btst])
            defer[st].append(lambda f=f, half=half, tst=tst, btst=btst: self.dma(wTMv[:, half * NH:(half + 1) * NH, f, :, :], tst, reads=[btst], writes=[self.b_wTM[f]], q="sp"))

        defer = [[], []]
        opsA, opsB = [], []
        for f in range(8):
            self.ps_pool = [0, 1, 2, 3]
            opsA += self.S.record(lambda: unit(f, 0, 0))
            self.ps_pool = [4, 5, 6, 7]
            opsB += self.S.record(lambda: unit(f, 1, 1))
        self.ps_pool = list(range(8))
        n0 = (len(opsA) // 8) // 2 if SKEW else 0
        for op in opsA[:n0]:
            self.S.add(*op)
        self.S.replay_interleaved(opsA[n0:], opsB)
        for st_ in range(2):
            for fn_ in defer[st_]:
                fn_()
        A.release()
        self.S.barrier()
        muv = A.alloc([DM])
        lnw = A.alloc([DM])
        lnb = A.alloc([DM])
        brow = Buf()
        self.rowb(l, "mu_v", muv, brow)
        self.rowb(l, "ln_w", lnw, brow)
        self.rowb(l, "ln_b", lnb, brow)
        H = A.alloc([16, 64], parts=64)
        Hb = A.alloc([16, 64], BF16, parts=64)
        ecdh = A.alloc([16, 16], parts=64)
        becdh = Buf()
        self.dma(ecdh, self.ecdD.rearrange("(h p) c -> p h c", p=64), reads=[becdD], writes=[becdh])
        bH = [Buf() for _ in range(16)]
        bHb = [Buf() for _ in range(16)]
        fm4 = [A.alloc([4, 16, 128], BF16, parts=64) for _ in range(2)]
        bF4s = [Buf() for _ in range(2)]
        tm2 = [A.alloc([8, 2, 128], BF16) for _ in range(2)]
        vc = [A.alloc([DM])]
        vp = [A.alloc([DM])]
        gtl = [A.alloc([8, 128])]
        bvc, bgt = Buf(), Buf()
        bld = [Buf() for _ in range(2)]
        vf = A.alloc([DM])
        vb = A.alloc([DM], BF16)
        bv = Buf()
        MA = [A.alloc([3, 128], BF16) for _ in range(16)]
        MBp = [A.alloc([2, 2, 128], BF16) for _ in range(8)]
        XYp = [[A.alloc([2, 2, 128], BF16) for _ in range(2)] for _ in range(8)]
        Zq = [[A.alloc([4, 128], BF16) for _ in range(2)] for _ in range(4)]
        Xg8 = [A.alloc([8, 64], BF16) for _ in range(2)]
        U8 = [A.alloc([8, 64], BF16) for _ in range(2)]
        bM = [Buf() for _ in range(16)]
        bMB = [Buf() for _ in range(8)]
        bXY = [[Buf() for _ in range(2)] for _ in range(8)]
        bZ = [[Buf() for _ in range(2)] for _ in range(4)]
        bXg = [Buf() for _ in range(2)]
        bU = [Buf() for _ in range(2)]
        bHg = [Buf() for _ in range(2)]
        bHbg = [Buf() for _ in range(2)]
        otm = A.alloc([16, 64])
        bo = Buf()
        st2 = A.alloc([4, 16])
        bst2 = Buf()
        osq = A.alloc([16, 64])
        yo = [A.alloc([8, 128], BF16) for _ in range(2)]
        byo = [Buf() for _ in range(2)]
        maskA, maskB, identb = self.c("maskA"), self.c("maskB"), self.identb
        maskAb = maskBb = bmk = None
        MAr = MBr = bMAr = bMBr = None
        wFMv = [self.wFM[q].rearrange("(h p) t -> p h t", p=64) for q in range(4)]
        gatev = self.wgate.rearrange("(c p) t -> p c t", p=128)
        ysv = self.ys[3].rearrange("(c p) t -> p c t", p=128)
        pend_st = None
        for i in range(NT):
            F4, T2, vcu, vpr, gt_, bl = fm4[i % 2], tm2[i % 2], vc[0], vp[0], gtl[0], bld[i % 2]
            bF4 = bF4s[i % 2]
            sl = slice(i * 128, (i + 1) * 128)
            for q in range(4):
                self.dma(F4[:, q, :, :], wFMv[q][:, :, sl], reads=self.b_wFM, writes=[bF4])
            self.dma(T2, self.wTM[sl, :, :, :], reads=self.b_wTM, writes=[bl])
            self.dma(vcu, self.pTM[sl, T_WV:T_WV + 1024], reads=self.b_pTM[i][6:8], writes=[bvc])
            if i == 0:
                self.memset(vpr[0:1, :], 0.0, [bvc])
                self.dma(vpr[1:128, :], self.pTM[0:127, T_WV:T_WV + 1024], reads=self.b_pTM[0][6:8], writes=[bvc])
            else:
                self.dma(vpr, self.pTM[i * 128 - 1:(i + 1) * 128 - 1, T_WV:T_WV + 1024],
                         reads=self.b_pTM[i][6:8] + self.b_pTM[i - 1][6:8], writes=[bvc])
            self.dma(gt_, gatev[:, :, sl], reads=self.b_wgate, writes=[bgt])
            if pend_st is not None:
                pend_st()
                pend_st = None
            self.tt(vf, vpr, vcu, ALU.subtract, [bvc], [bv], eng="pool")
            self.tt(vf, vf, muv, ALU.mult, [bv, brow], [bv], eng="pool")
            self.tt(vf, vf, vcu, ALU.add, [bv, bvc], [bv], eng="pool")
            self.cp(vb, vf, [bv], [bv], eng="act")
            rT, aT, bT, kT = (F4[:, q, :, :] for q in range(4))
            if i == 0:
                self.memset(H, 0.0, bHg)
                self.memset(Hb, 0.0, bHbg)
            for h in range(16):
                ps, bp = self.psum()
                self.mm(ps[:, 0:128], bT[:, h, :], aT[:, h, :], True, True, [bF4], [bp])
                self.mm(ps[:, 128:256], aT[:, h, :], bT[:, h, :], True, True, [bF4], [bp])
                self.mm(ps[:, 256:384], kT[:, h, :], aT[:, h, :], True, True, [bF4], [bp])
                if True:
                    self.tt(MA[h].rearrange("p a b -> p (a b)"), ps[:, 0:384], maskA, ALU.mult, [bp, bc], [bM[h]])
                else:
                    r_, br_ = MAr[(h // 2) % 2], bMAr[(h // 2) % 2]
                    self.cp(r_, ps[:, 0:384], [bp], [br_], eng="act")
                    self.tt(MA[h].rearrange("p a b -> p (a b)"), r_, maskAb, ALU.mult, [br_, bmk], [bM[h]], eng="pool")
                self.tt(Zq[h // 4][0][:, h % 4, :], MA[h][:, 0, :], identb, ALU.add, [bM[h], self.b_identb], [bZ[h // 4][0]], eng="pool")
            for p in range(8):
                ps, bp = self.psum()
                for j in range(2):
                    h = 2 * p + j
                    self.mm(ps[:, j * 256:j * 256 + 128], bT[:, h, :], rT[:, h, :], True, True, [bF4], [bp])
                    self.mm(ps[:, j * 256 + 128:j * 256 + 256], kT[:, h, :], rT[:, h, :], True, True, [bF4], [bp])
                if True:
                    self.tt(MBp[p].rearrange("p j a b -> p j (a b)"), ps.rearrange("p (j c) -> p j c", j=2),
                            maskB.unsqueeze(1).to_broadcast([128, 2, 256]), ALU.mult, [bp, bc], [bMB[p]])
                else:
                    r_, br_ = MBr[(p // 2) % 2], bMBr[(p // 2) % 2]
                    self.cp(r_.rearrange("p j c -> p (j c)"), ps, [bp], [br_], eng="act")
                    self.tt(MBp[p].rearrange("p j a b -> p j (a b)"), r_, maskBb.unsqueeze(1).to_broadcast([128, 2, 256]), ALU.mult,
                            [br_, bmk], [bMB[p]], eng="pool")
            for lv in range(6):
                a_, b_ = lv % 2, (lv + 1) % 2
                for p in range(8):
                    ps, bp = self.psum()
                    rd = []
                    for j in range(2):
                        h = 2 * p + j
                        if lv == 0:
                            X, Y = MA[h][:, 1, :], MA[h][:, 0, :]
                            rd = [bM[2 * p], bM[2 * p + 1]]
                        else:
                            X, Y = XYp[p][a_][:, j, 0, :], XYp[p][a_][:, j, 1, :]
                            rd = [bXY[p][a_]]
                        self.mm(ps[:, j * 256:j * 256 + 128], Y, X, True, True, rd, [bp])
                        if lv < 5:
                            self.mm(ps[:, j * 256 + 128:j * 256 + 256], X, Y, True, True, rd, [bp])
                    ev_ = "dve" if p == 7 else "act"
                    if lv < 5:
                        self.cp(XYp[p][b_].rearrange("p j a b -> p (j a b)"), ps, [bp], [bXY[p][b_]], eng=ev_)
                    else:
                        self.cp(XYp[p][b_][:, :, 0, :], ps.rearrange("p (j a b) -> p j a b", j=2, a=2)[:, :, 0, :], [bp], [bXY[p][b_]], eng=ev_)
                for q in range(4):
                    ps, bp = self.psum()
                    for j in range(4):
                        h = 4 * q + j
                        X2 = XYp[h // 2][b_][:, h % 2, 0, :]
                        self.mm(ps[:, j * 128:(j + 1) * 128], X2, Zq[q][a_][:, j, :], True, True, [bXY[h // 2][b_], bZ[q][a_]], [bp])
                    self.tt(Zq[q][b_].rearrange("p j b -> p (j b)"), ps, Zq[q][a_].rearrange("p j b -> p (j b)"), ALU.add,
                            [bp, bZ[q][a_]], [bZ[q][b_]])
            for g in range(2):
                ps, bp = self.psum()
                for j in range(8):
                    h = 8 * g + j
                    vh = vb[:, h * 64:(h + 1) * 64]
                    o_ = ps[:, j * 64:(j + 1) * 64]
                    self.mm(o_, aT[:, h, :], Hb[:, h, :], True, False, [bF4, bHbg[g]], [bp])
                    self.mm(o_, MA[h][:, 2, :], vh, False, True, [bM[h], bv], [bp])
                self.cp(Xg8[g].rearrange("p j d -> p (j d)"), ps, [bp], [bXg[g]])
            for g in range(2):
                ps, bp = self.psum()
                for j in range(8):
                    h = 8 * g + j
                    self.mm(ps[:, j * 64:(j + 1) * 64], Zq[h // 4][0][:, h % 4, :], Xg8[g][:, j, :], True, True, [bZ[h // 4][0], bXg[g]], [bp])
                self.cp(U8[g].rearrange("p j d -> p (j d)"), ps, [bp], [bU[g]])
            for g in range(2):
                ps, bp = self.psum()
                for j in range(8):
                    h = 8 * g + j
                    vh = vb[:, h * 64:(h + 1) * 64]
                    o_ = ps[:, j * 64:(j + 1) * 64]
                    self.mm(o_, rT[:, h, :], Hb[:, h, :], True, False, [bF4, bHbg[g]], [bp])
                    self.mm(o_, MBp[h // 2][:, h % 2, 0, :], U8[g][:, j, :], False, False, [bMB[h // 2], bU[g]], [bp])
                    self.mm(o_, MBp[h // 2][:, h % 2, 1, :], vh, False, True, [bMB[h // 2], bv], [bp])
                self.cp(otm[:, 8 * g:8 * g + 8, :].rearrange("p j d -> p (j d)"), ps, [bp], [bo])
                ps, bp = self.psum()
                for j in range(8):
                    h = 8 * g + j
                    vh = vb[:, h * 64:(h + 1) * 64]
                    o_ = ps[0:64, j * 64:(j + 1) * 64]
                    self.mm(o_, T2[:, h // 2, 0, (h % 2) * 64:(h % 2 + 1) * 64], U8[g][:, j, :], True, False, [bl, bU[g]], [bp])
                    self.mm(o_, T2[:, h // 2, 1, (h % 2) * 64:(h % 2 + 1) * 64], vh, False, True, [bl, bv], [bp])
                Hg = H[:, 8 * g:8 * g + 8, :]
                self.tt(Hg, Hg, ecdh[:, 8 * g:8 * g + 8, i:i + 1].to_broadcast([64, 8, 64]), ALU.mult, [bHg[g], becdh], [bHg[g]])
                self.tt(Hg, Hg, ps[0:64, :].rearrange("p (j d) -> p j d", j=8), ALU.add, [bHg[g], bp], [bHg[g]])
                self.cp(Hb[:, 8 * g:8 * g + 8, :], Hg, [bHg[g]], [bHbg[g]], eng="act")
            mean, var_, rs_, tmp_ = (st2[:, q, :] for q in range(4))
            self.S.add("dve", lambda e, o=mean, x=otm: e.reduce_sum(out=o, in_=x, axis=AX.X), reads=[bo], writes=[bst2])
            self.ts(mean, mean, 1.0 / 64.0, None, ALU.mult, reads=[bst2], writes=[bst2])
            self.tt(otm, otm, mean.unsqueeze(2).to_broadcast([128, 16, 64]), ALU.subtract, [bo, bst2], [bo])
            self.tt(osq, otm, otm, ALU.mult, [bo], [bo], eng="pool")
            self.S.add("dve", lambda e, o=var_, x=osq: e.reduce_sum(out=o, in_=x, axis=AX.X), reads=[bo], writes=[bst2])
            self.act(rs_, var_, AF.Sqrt, [bst2], [bst2], scale=1.0 / 64.0, bias=64e-5)
            self.recip(rs_, rs_, [bst2], [bst2])
            self.tt(otm, otm, rs_.unsqueeze(2).to_broadcast([128, 16, 64]), ALU.mult, [bo, bst2], [bo])
            ov = otm.rearrange("p h d -> p (h d)")
            self.tt(ov, ov, lnw, ALU.mult, [bo, brow], [bo])
            self.tt(ov, ov, lnb, ALU.add, [bo, brow], [bo])
            self.tt(osq, vf.rearrange("p (h d) -> p h d", h=16), rks[:, i, :].unsqueeze(2).to_broadcast([128, 16, 64]), ALU.mult,
                    [bv, brks], [bo], eng="pool")
            self.tt(otm, otm, osq, ALU.add, [bo], [bo])
            for half in range(2):
                ps, bp = self.psum()
                for k in range(4):
                    kk_ = half * 4 + k
                    self.tr(ps[:, k * 128:(k + 1) * 128], ov[:, kk_ * 128:(kk_ + 1) * 128], [bo], [bp])
                self.tt(yo[i % 2][:, half * 4:half * 4 + 4, :], ps.rearrange("p (a b) -> p a b", a=4), gt_[:, half * 4:half * 4 + 4, :],
                        ALU.mult, [bp, bgt], [byo[i % 2]])
            pend_st = (lambda i=i, sl=sl: self.dma(ysv[:, :, sl], yo[i % 2], reads=[byo[i % 2]], writes=[self.b_ys[3][i]], q="sp"))
        pend_st()
        A.release()

    def merge(self, l):
        A = self.A
        A.mark()
        wo = A.alloc([8, DM], BF16)
        bwo = Buf()
        for hlf in range(2):
            self.dma(wo[:, :, hlf * 512:(hlf + 1) * 512],
                     self.d["w_out"][l].rearrange("(k p) n -> p k n", p=128)[:, :, hlf * 512:(hlf + 1) * 512], writes=[bwo], q="pool")
        self.wslab_init(2)
        acc = A.alloc([8, SEQ])
        bacc = [Buf() for _ in range(8)]
        ysb = [A.alloc([8, 512], BF16) for _ in range(2)]
        bysb = [Buf() for _ in range(2)]
        gtile = [A.alloc([512]) for _ in range(6)]
        bg = [Buf() for _ in range(6)]
        mT = self.hT
        bm = self.bh
        gi = 0
        yi = 0
        for m in range(4):
            yv = self.ys[m].rearrange("(c p) t -> p c t", p=128)
            for dg in range(2):
                wt, bw = self.wslab(self.d["w_branch"][l, m], dg * 512, 512)
                for tb in range(4):
                    ts_ = slice(tb * 512, (tb + 1) * 512)
                    if dg == 0 or True:
                        yt, byt = ysb[yi % 2], bysb[yi % 2]
                        yi += 1
                        self.dma(yt, yv[:, :, ts_], reads=self.b_ys[m] + [self.b_ysF[m]], writes=[byt])
                    for cc in range(4):
                        dch = dg * 4 + cc
                        ps, bp = self.psum()
                        for k in range(8):
                            self.mm(ps, wt[:, k, cc * 128:(cc + 1) * 128], yt[:, k, :], k == 0, k == 7, [bw, byt], [bp])
                        g, bgt = gtile[gi % 6], bg[gi % 6]
                        gi += 1
                        ft = F_GATE + m * 8 + dch
                        self.dma(g, self.pFM[ft * 128:(ft + 1) * 128, ts_], reads=[self.b_pFM[ft]], writes=[bgt])
                        self.act(g, g, AF.Sigmoid, [bgt], [bgt])
                        if m == 0:
                            self.tt(acc[:, dch, ts_], ps, g, ALU.mult, [bp, bgt], [bacc[dch]])
                        else:
                            self.tt(g, ps, g, ALU.mult, [bp, bgt], [bgt])
                            if m < 3:
                                self.tt(acc[:, dch, ts_], acc[:, dch, ts_], g, ALU.add, [bacc[dch], bgt], [bacc[dch]])
                            else:
                                self.tt(mT[:, dch, ts_], acc[:, dch, ts_], g, ALU.add, [bacc[dch], bgt], bm[4 * tb:4 * tb + 4])
        A.release()
        self.S.barrier()
        A.mark()
        wo_ = A.alloc([8, DM], BF16)
        self.rowb(l, "norm_ffn", self.nrow, self.b_nrow)
        xin = [A.alloc([DM]) for _ in range(2)]
        bxin = [Buf() for _ in range(2)]
        self.norm_alloc()
        src = self.x if l == 0 else self.xres
        def ldx(i):
            self.dma(xin[i % 2], src[i * 128:(i + 1) * 128, :], reads=[self.b_xres[i]], writes=[bxin[i % 2]])
        def mm_stage(i):
            sl = slice(i * 128, (i + 1) * 128)
            out = []
            for hlf in range(2):
                ps, bp = self.psum()
                for k in range(8):
                    self.mm(ps, mT[:, k, sl], wo[:, k, hlf * 512:(hlf + 1) * 512], k == 0, k == 7, [bm[i], bwo], [bp])
                out.append((ps, bp))
            return out

        ldx(0)
        pss = mm_stage(0)
        for i in range(NT):
            xt, bx = xin[i % 2], bxin[i % 2]
            sl = slice(i * 128, (i + 1) * 128)
            nxt = None
            if i + 1 < NT:
                ldx(i + 1)
                nxt = mm_stage(i + 1)
            for hlf, (ps, bp) in enumerate(pss):
                self.tt(xt[:, hlf * 512:(hlf + 1) * 512], xt[:, hlf * 512:(hlf + 1) * 512], ps, ALU.add, [bx, bp], [bx])
            self.dma(self.xres[sl, :], xt, reads=[bx], writes=[self.b_xres[i]])
            self.norm_tile(xt, bx, i)
            pss = nxt
        A.release()

    def ffn(self, l):
        A = self.A
        A.mark()
        self.norm_alloc()
        h2, bh2 = self.hT, self.bh
        cw = self.pf("ffn_cw").rearrange("p (t j) -> p t j", j=3)
        cb = self.pf("ffn_cb")
        bpf = self.b_pfm
        wd = A.alloc([22, DM], BF16)
        bwd = Buf()
        wdv = self.d["ffn_down"][l].rearrange("(k p) n -> p k n", p=128)
        for q in range(4):
            self.dma(wd[:, :, q * 256:(q + 1) * 256], wdv[:, :, q * 256:(q + 1) * 256], writes=[bwd], q="pool")
        aT = [A.alloc([22, 1024], BF16)]
        baT = [Buf()]
        halo = A.alloc([44, 2], BF16)
        bhalo = Buf()
        wv = [A.alloc([8, 256], BF16) for _ in range(2)]
        wg = [A.alloc([8, 256], BF16) for _ in range(2)]
        bws = [Buf() for _ in range(2)]
        ub = [[A.alloc([2 + 512], BF16) for _ in range(2)] for _ in range(2)]
        bub = [[Buf() for _ in range(2)] for _ in range(2)]
        dg = [A.alloc([2, 3, 128], BF16) for _ in range(3)]
        bdg = [Buf() for _ in range(3)]
        sg = [A.alloc([512]) for _ in range(2)]
        bsg = [Buf() for _ in range(2)]
        last = (l == self.L - 1)
        if last:
            self.dma(self.nrow, self.d["norm_final"].partition_broadcast(128), writes=[self.b_nrow])
        else:
            o, w = PROW_OFF["norm_mix"]
            self.dma(self.nrow, self.d["prow"][l + 1, o:o + w].partition_broadcast(128), writes=[self.b_nrow])
        xin = [A.alloc([DM]) for _ in range(2)]
        bxin = [Buf() for _ in range(2)]
        upv = self.d["ffn_up"][l].rearrange("(k p) n -> p k n", p=128)
        ubi = 0
        pending = None
        for hh in range(2):
            aTq, baTq = aT[0], baT[0]
            for sb in range(11):
                wv_, wg_, bw_ = wv[sb % 2], wg[sb % 2], bws[sb % 2]
                self.dma(wv_, upv[:, :, sb * 256:(sb + 1) * 256], writes=[bw_], q="pool")
                self.dma(wg_, upv[:, :, DFF + sb * 256:DFF + (sb + 1) * 256], writes=[bw_], q="pool")
                for jj in range(2):
                  jf = 2 * sb + jj
                  d_, bd_ = dg[jf % 3], bdg[jf % 3]
                  for vg in range(2):
                      ft = jf + 22 * vg
                      for j in range(3):
                          self.ts(d_[:, vg, j, :], self.identb, cw[:, ft, j:j + 1], None, ALU.mult,
                                  reads=[self.b_identb, bpf], writes=[bd_], eng="act")
                  for blk in range(2):
                    hf = hh * 2 + blk
                    t0 = hf * 512
                    ubi += 1
                    for vg in range(2):
                        ft = jf + 22 * vg
                        w_ = wv_ if vg == 0 else wg_
                        uu, buu = ub[vg][ubi % 2], bub[vg][ubi % 2]
                        if hf == 0:
                            self.memset(uu[:, 0:2], 0.0, [buu])
                        else:
                            self.cp(uu[:, 0:2], halo[:, ft, :], [bhalo], [buu], eng="act")
                        ps, bp = self.psum()
                        for k in range(8):
                            self.mm(ps, w_[:, k, jj * 128:(jj + 1) * 128], h2[:, k, t0:t0 + 512], k == 0, k == 7,
                                    [bw_] + bh2[(t0 // 128):(t0 // 128) + 4], [bp])
                        self.cp(uu[:, 2:514], ps, [bp], [buu])
                        if hf < 3:
                            self.cp(halo[:, ft, :], uu[:, 512:514], [buu], [bhalo], eng="act")
                    if pending is not None:
                        pending()

                    def conv_stage(jf=jf, blk=blk, d_=d_, bd_=bd_, k_=ubi % 2):
                        conv_ps = []
                        for vg in range(2):
                            ft = jf + 22 * vg
                            uu, buu = ub[vg][k_], bub[vg][k_]
                            ps2, bp2 = self.psum()
                            for j in range(3):
                                self.mm(ps2, d_[:, vg, j, :], uu[:, j:j + 512], j == 0, j == 2, [bd_, buu], [bp2])
                            conv_ps.append((ps2, bp2, ft))
                        (psv, bpv, ftv), (psg, bpg, ftg) = conv_ps
                        sg_, bsg_ = sg[k_], bsg[k_]
                        self.act(sg_, psg, AF.Silu, [bpg, bpf], [bsg_], bias=cb[:, ftg:ftg + 1])
                        self.stt(aTq[:, jf, blk * 512:(blk + 1) * 512], psv, cb[:, ftv:ftv + 1], sg_, ALU.add, ALU.mult, [bpv, bpf, bsg_], [baTq])
                    pending = conv_stage
            if pending is not None:
                pending()
                pending = None
            def dn_stage(it):
                out = []
                for hlf in range(2):
                    ps, bp = self.psum()
                    for k in range(22):
                        self.mm(ps, aTq[:, k, it * 128:(it + 1) * 128], wd[:, k, hlf * 512:(hlf + 1) * 512], k == 0, k == 21, [baTq, bwd], [bp])
                    out.append((ps, bp))
                return out

            pss = dn_stage(0)
            for it in range(8):
                i = hh * 8 + it
                sl = slice(i * 128, (i + 1) * 128)
                xt, bx = xin[i % 2], bxin[i % 2]
                if it == 0:
                    self.dma(xt, self.xres[sl, :], reads=[self.b_xres[i]], writes=[bx])
                nxt = None
                if it + 1 < 8:
                    self.dma(xin[(i + 1) % 2], self.xres[(i + 1) * 128:(i + 2) * 128, :], reads=[self.b_xres[i + 1]], writes=[bxin[(i + 1) % 2]])
                    nxt = dn_stage(it + 1)
                for hlf, (ps, bp) in enumerate(pss):
                    self.tt(xt[:, hlf * 512:(hlf + 1) * 512], xt[:, hlf * 512:(hlf + 1) * 512], ps, ALU.add, [bx, bp], [bx])
                pss = nxt
                if last:
                    self.norm_tile(xt, bx, i, final_out=self.out[sl, :])
                else:
                    self.dma(self.xres[sl, :], xt, reads=[bx], writes=[self.b_xres[i]])
                    self.norm_tile(xt, bx, i)
        A.release()


_CACHE = {}


def build_program(L, dbg=False, phases=None):
    nc = bass.Bass("TRN2", target_bir_lowering=False)
    kb = KB(nc, L, dbg=dbg, phases=phases)
    kb.build()
    return nc


def kernel(**inputs):
    inp = {k: np.asarray(v) for k, v in inputs.items()}
    L = inp["w_in"].shape[0]
    B = inp["x"].shape[0]
    g = prep_inputs(inp)
    nc = build_program(L)
    in_maps = []
    for b in range(B):
        m = dict(g)
        m["x"] = np.ascontiguousarray(inp["x"][b])
        in_maps.append(m)
    res = run_bass_kernel_spmd(nc, in_maps, core_ids=list(range(B)))
    out = np.stack([np.asarray(r["out"]) for r in res.results], axis=0).astype(np.float32)
    return out
```

```python
import numpy as np
from contextlib import ExitStack
import concourse.bass as bass
import concourse.mybir as mybir
from concourse.bass_utils import run_bass_kernel_spmd

F32 = mybir.dt.float32
BF16 = mybir.dt.bfloat16
ALU = mybir.AluOpType
AF = mybir.ActivationFunctionType
AX = mybir.AxisListType

KD = 16
ENGS = ["pe", "act", "dve", "pool", "sp"]
DMAQ = ["sp", "pool", "act"]


class Buf:
    __slots__ = ("name", "w", "r")

    def __init__(self, name=""):
        self.name = name
        self.w = None
        self.r = {}


class Sched:
    def __init__(self, nc):
        self.nc = nc
        self.ops = {e: [] for e in ENGS}
        self.cnt = {e: 0 for e in ENGS}
        self.dcnt = {e: 0 for e in DMAQ}
        self.known = {e: {} for e in ENGS}
        self.bar = {e: None for e in ENGS}
        self.nops = 0
        self.rec = None

    def record(self, f):
        assert self.rec is None
        self.rec = []
        f()
        r, self.rec = self.rec, None
        return r

    def replay_interleaved(self, a, b):
        na, nb = len(a), len(b)
        ia = ib = 0
        while ia < na or ib < nb:
            if ib >= nb or (ia < na and ia * nb <= ib * na):
                self.add(*a[ia])
                ia += 1
            else:
                self.add(*b[ib])
                ib += 1

    @staticmethod
    def _eng_of(key):
        return key[1]

    def add(self, eng, fn, reads=(), writes=(), dma=False):
        if self.rec is not None:
            self.rec.append((eng, fn, tuple(reads), tuple(writes), dma))
            return None
        deps = {}

        def dep(kv):
            if kv is None:
                return
            k, v = kv
            if deps.get(k, 0) < v:
                deps[k] = v

        for b in reads:
            dep(b.w)
        for b in writes:
            dep(b.w)
            for k, v in b.r.items():
                if (not dma) and k == ("c", eng) and eng == "pe":
                    continue
                dep((k, v))
        if self.bar[eng] is not None:
            for k, v in self.bar[eng].items():
                dep((k, v))
            self.bar[eng] = None
        if dma:
            idx = self.dcnt[eng]
            self.dcnt[eng] += 1
            if idx >= KD:
                dep((("d", eng, (idx - KD) % KD), 16 * ((idx - KD) // KD + 1)))
            tok = (("d", eng, idx % KD), 16 * (idx // KD + 1))
        else:
            self.cnt[eng] += 1
            tok = (("c", eng), self.cnt[eng])
        waits = []
        kn = self.known[eng]
        for k, v in deps.items():
            if k == ("c", "pe") and eng == "pe" and not dma:
                continue
            if kn.get(k, 0) >= v:
                continue
            kn[k] = v
            waits.append((k, v))
        self.ops[eng].append((fn, waits, tok, dma))
        self.nops += 1
        for b in writes:
            b.w = tok
            b.r = {}
        for b in reads:
            if b in writes:
                continue
            k, v = tok
            if b.r.get(k, 0) < v:
                b.r[k] = v
        return tok

    def barrier(self):
        snap = {}
        for e in ENGS:
            if self.cnt[e] > 0:
                snap[("c", e)] = self.cnt[e]
        for q in DMAQ:
            n = self.dcnt[q]
            for idx in range(max(0, n - KD), n):
                k = ("d", q, idx % KD)
                v = 16 * (idx // KD + 1)
                if snap.get(k, 0) < v:
                    snap[k] = v
        for e in ENGS:
            cur = self.bar[e] or {}
            for k, v in snap.items():
                if cur.get(k, 0) < v:
                    cur[k] = v
            self.bar[e] = cur

    def emit(self, final_wait_eng="sp"):
        nc = self.nc
        self.barrier()
        fin = self.bar[final_wait_eng]
        needed = {e: set() for e in ENGS}
        for e in ENGS:
            for fn, waits, tok, dma in self.ops[e]:
                for k, v in waits:
                    if k[0] == "c":
                        needed[k[1]].add(v)
        for k, v in fin.items():
            if k[0] == "c":
                needed[k[1]].add(v)
        remap = {}
        for e in ENGS:
            s = sorted(needed[e])
            remap[e] = {v: i + 1 for i, v in enumerate(s)}
        with ExitStack() as st:
            sems = {}
            for e in ENGS:
                sems[("c", e)] = st.enter_context(nc.semaphore("c_" + e))
            for q in DMAQ:
                if self.dcnt[q] > 0:
                    for j in range(KD):
                        sems[("d", q, j)] = st.enter_context(nc.semaphore("d_%s_%d" % (q, j)))
            block = st.enter_context(nc.Block())

            def run(engname, e):
                for fn, waits, tok, dma in self.ops[engname]:
                    for k, v in waits:
                        if k[0] == "c":
                            v = remap[k[1]][v]
                        e.wait_ge(sems[k], v)
                    ins = fn(e)
                    k, v = tok
                    if dma:
                        ins.then_inc(sems[k], 16)
                    elif v in remap[engname]:
                        ins.then_inc(sems[k], 1)
                if engname == final_wait_eng:
                    for k, v in fin.items():
                        if k[0] == "c":
                            v = remap[k[1]][v]
                        e.wait_ge(sems[k], v)

            @block.tensor
            def _(e):
                run("pe", e)

            @block.scalar
            def _(e):
                run("act", e)

            @block.vector
            def _(e):
                run("dve", e)

            @block.gpsimd
            def _(e):
                run("pool", e)

            @block.sync
            def _(e):
                run("sp", e)


class Arena:
    def __init__(self, nc, st, name, ncols_f32):
        self.t = st.enter_context(nc.sbuf_tensor(name, [128, ncols_f32], F32))
        self.n = ncols_f32
        self.off = 0
        self.marks = []

    def alloc(self, shape, dtype=F32, parts=128):
        ne = int(np.prod(shape))
        if dtype == BF16:
            ncol = (ne + 1) // 2
        else:
            ncol = ne
        assert self.off + ncol <= self.n, "arena overflow %d + %d > %d" % (self.off, ncol, self.n)
        ap = self.t[0:parts, self.off:self.off + ncol]
        self.off += ncol
        if dtype == BF16:
            ap = ap.bitcast(BF16)
            if ne % 2:
                ap = ap[:, 0:ne]
        if len(shape) == 2:
            ap = ap.rearrange("p (a b) -> p a b", a=shape[0])
        elif len(shape) == 3:
            ap = ap.rearrange("p (a b c) -> p a b c", a=shape[0], b=shape[1])
        return ap

    def mark(self):
        self.marks.append(self.off)

    def release(self):
        self.off = self.marks.pop()

import math

SKEW = False
SEQ = 2048
DM = 1024
NT = 16
EPS = 1e-6
IN_SPLITS = (1024, 2048, 16, 1024, 1024, 512, 512, 1024, 1024, 3328, 4096)
NTM = 4112
NFM = 11520
DFF = 2816
F_XBC, F_LX, F_LG, F_RG, F_WR, F_WK, F_LO, F_GL, F_GATE = 0, 16, 24, 32, 40, 48, 56, 57, 58
T_Z, T_QK, T_RV, T_WV, T_DT = 0, 1024, 2048, 3072, 4096


def _cols():
    off = np.cumsum([0] + list(IN_SPLITS))
    rwo = off[9]
    ar = np.arange
    tm = np.concatenate([ar(off[0], off[1]), ar(off[5], off[7]), ar(off[7], off[8]),
                         ar(rwo + 2048, rwo + 3072), ar(off[2], off[3])])
    fm = np.concatenate([ar(off[1], off[2]), ar(off[4], off[5]), ar(off[3], off[4]), ar(off[8], off[9]),
                         ar(rwo, rwo + 2048), ar(rwo + 3072, rwo + 3328), ar(off[10], off[11])])
    assert len(tm) == NTM and len(fm) == NFM
    return tm, fm


def _fm(v):
    return np.ascontiguousarray(v.reshape(-1, 128).T)


PFM_SPEC = [("ssm_cw", 64), ("ssm_cb", 16), ("lru_cw", 32), ("lru_cb", 8), ("lru_ba", 8), ("lru_bi", 8),
            ("lru_lam", 8), ("ret_nw", 8), ("mu_fm", 18), ("w0", 8), ("a0", 8), ("k_k", 8), ("k_a", 8),
            ("r_k", 8), ("ffn_cw", 132), ("ffn_cb", 44)]
PROW_SPEC = [("norm_mix", 1024), ("norm_ffn", 1024), ("ssm_norm", 1024), ("mu_v", 1024), ("ln_w", 1024),
             ("ln_b", 1024), ("dt_bias", 16), ("a_log", 16), ("ssm_d", 16)]
CST_SPEC = [("ident", 128), ("tri", 128), ("ones", 128), ("negmask4", 128), ("GT", 1024), ("gqA", 1024),
            ("gqB", 1024), ("kend", 8), ("cdt", 8), ("cos", 512), ("sin", 512), ("maskA", 384), ("maskB", 256),
            ("headsel", 2), ("blockones", 128)]


def _offs(spec):
    d = {}
    o = 0
    for n, c in spec:
        d[n] = (o, c)
        o += c
    return d, o


PFM_OFF, NPF = _offs(PFM_SPEC)
PROW_OFF, NROW = _offs(PROW_SPEC)
CST_OFF, NCST = _offs(CST_SPEC)


def make_cst():
    c = {}
    p = np.arange(128)
    c["ident"] = np.eye(128)
    c["tri"] = (p[:, None] <= p[None, :]).astype(np.float64)
    c["ones"] = np.ones((128, 128))
    nm = np.where(p[:, None] <= p[None, :], 0.0, -30000.0)
    c["negmask4"] = nm
    lg = np.log1p(-np.exp2(-5.0 - np.arange(8)))
    same = (p[:, None] // 64) == (p[None, :] // 64)
    GT = np.zeros((128, 8, 128))
    for h in range(8):
        GT[:, h, :] = np.where(same, np.exp(lg[h] * np.abs(p[:, None] - p[None, :])), 0.0)
    c["GT"] = GT.reshape(128, 1024)
    gqA = np.zeros((128, 8, 128))
    gqB = np.zeros((128, 8, 128))
    for h in range(8):
        gq = np.exp(lg[h] * ((p % 64) + 1.0))
        gqA[:, h, :] = np.where(p < 64, gq, 0.0)[None, :]
        gqB[:, h, :] = np.where(p >= 64, gq, 0.0)[None, :]
    c["gqA"] = gqA.reshape(128, 1024)
    c["gqB"] = gqB.reshape(128, 1024)
    c["kend"] = np.exp(lg[None, :] * (63.0 - (p % 64))[:, None]) * 0.125
    c["cdt"] = np.tile(np.exp(lg * 64.0)[None, :], (128, 1))
    pos = np.arange(SEQ, dtype=np.float32)
    inv = (np.float32(10000.0) ** (-np.arange(0, 64, 2, dtype=np.float32) / np.float32(64))).astype(np.float32)
    ang = (pos[:, None] * inv[None, :]).astype(np.float32)
    c["cos"] = np.cos(ang).reshape(16, 128, 32).transpose(1, 0, 2).reshape(128, 512)
    c["sin"] = np.sin(ang).reshape(16, 128, 32).transpose(1, 0, 2).reshape(128, 512)
    su = (p[:, None] < p[None, :]).astype(np.float64)
    sl = (p[:, None] > p[None, :]).astype(np.float64)
    iu = (p[:, None] <= p[None, :]).astype(np.float64)
    c["maskA"] = np.concatenate([su, sl, su], axis=1)
    c["maskB"] = np.concatenate([iu, iu], axis=1)
    hs = np.zeros((128, 2))
    hs[:64, 0] = 1
    hs[64:, 1] = 1
    c["headsel"] = hs
    c["blockones"] = same.astype(np.float64)
    out = np.zeros((128, NCST), np.float32)
    for n, (o, w) in CST_OFF.items():
        assert c[n].shape == (128, w), (n, c[n].shape)
        out[:, o:o + w] = c[n]
    return out


def prep_inputs(inp):
    L = inp["w_in"].shape[0]
    tm, fm = _cols()
    g = {}
    g["w_tm"] = np.ascontiguousarray(inp["w_in"][:, :, tm])
    g["w_fm"] = np.ascontiguousarray(inp["w_in"][:, :, fm])
    g["w_branch"] = np.ascontiguousarray(inp["w_branch"])
    g["w_out"] = np.ascontiguousarray(inp["w_out"])
    g["ffn_up"] = np.ascontiguousarray(inp["ffn_up"])
    g["ffn_down"] = np.ascontiguousarray(inp["ffn_down"])
    for nm in ("lru_w_a", "lru_w_i"):
        bd = np.zeros((L, 8, 128, 128), np.float32)
        w = inp[nm]
        for t in range(8):
            bd[:, t, 0:64, 0:64] = w[:, 2 * t]
            bd[:, t, 64:128, 64:128] = w[:, 2 * t + 1]
        g[nm + "_bd"] = bd
    g["w2a2"] = np.ascontiguousarray(np.concatenate([inp["rwkv_w2"], inp["rwkv_a2"]], axis=1))
    g["g2"] = np.ascontiguousarray(inp["rwkv_g2"])
    pfm = np.zeros((L, 128, NPF), np.float32)
    prow = np.zeros((L, NROW), np.float32)
    for l in range(L):
        d = {}
        d["ssm_cw"] = np.stack([_fm(inp["ssm_conv_w"][l, j]) for j in range(4)], axis=2).reshape(128, 64)
        d["ssm_cb"] = _fm(inp["ssm_conv_b"][l])
        d["lru_cw"] = np.stack([_fm(inp["lru_conv_w"][l, j]) for j in range(4)], axis=2).reshape(128, 32)
        d["lru_cb"] = _fm(inp["lru_conv_b"][l])
        d["lru_ba"] = _fm(inp["lru_b_a"][l])
        d["lru_bi"] = _fm(inp["lru_b_i"][l])
        d["lru_lam"] = _fm(inp["lru_lam"][l])
        d["ret_nw"] = _fm(inp["ret_norm"][l])
        mu = inp["rwkv_mu"][l]
        d["mu_fm"] = _fm(np.concatenate([mu[0:2048], mu[3072:3328]]))
        d["w0"] = _fm(inp["rwkv_w0"][l])
        d["a0"] = _fm(inp["rwkv_a0"][l])
        d["k_k"] = _fm(inp["rwkv_k_k"][l])
        d["k_a"] = _fm(inp["rwkv_k_a"][l])
        d["r_k"] = _fm(inp["rwkv_r_k"][l])
        d["ffn_cw"] = np.stack([_fm(inp["ffn_conv_w"][l, j]) for j in range(3)], axis=2).reshape(128, 132)
        d["ffn_cb"] = _fm(inp["ffn_conv_b"][l])
        for n, (o, w) in PFM_OFF.items():
            pfm[l, :, o:o + w] = d[n]
        r = {"norm_mix": inp["norm_mix"][l], "norm_ffn": inp["norm_ffn"][l], "ssm_norm": inp["ssm_norm"][l],
             "mu_v": inp["rwkv_mu"][l][2048:3072], "ln_w": inp["rwkv_ln_w"][l], "ln_b": inp["rwkv_ln_b"][l],
             "dt_bias": inp["ssm_dt_bias"][l], "a_log": inp["ssm_a_log"][l], "ssm_d": inp["ssm_d"][l]}
        for n, (o, w) in PROW_OFF.items():
            prow[l, o:o + w] = r[n]
    g["pfm"] = pfm
    g["prow"] = prow
    g["norm_final"] = np.ascontiguousarray(inp["norm_final"])
    g["cst"] = make_cst()
    return g


SHARED_SHAPES = lambda L: {
    "w_tm": [L, 1024, NTM], "w_fm": [L, 1024, NFM], "w_branch": [L, 4, 1024, 1024], "w_out": [L, 1024, 1024],
    "ffn_up": [L, 1024, 2 * DFF], "ffn_down": [L, DFF, 1024], "lru_w_a_bd": [L, 8, 128, 128],
    "lru_w_i_bd": [L, 8, 128, 128], "w2a2": [L, 128, 1024], "g2": [L, 128, 1024], "pfm": [L, 128, NPF],
    "prow": [L, NROW], "norm_final": [1024], "cst": [128, NCST]}


class KB:
    def __init__(self, nc, L, dbg=False, phases=None):
        self.nc = nc
        self.L = L
        self.dbg = dbg
        self.S = Sched(nc)
        self.phases = phases
        okind = "ExternalOutput" if dbg else "Internal"
        self.x = nc.dram_tensor("x", [SEQ, DM], F32, kind="ExternalInput").ap()
        self.d = {n: nc.dram_tensor(n, s, F32, kind="ExternalInput").ap() for n, s in SHARED_SHAPES(L).items()}
        self.out = nc.dram_tensor("out", [SEQ, DM], F32, kind="ExternalOutput").ap()
        self.xres = nc.dram_tensor("xres", [SEQ, DM], F32, kind=okind).ap()
        self.pTM = nc.dram_tensor("pTM", [SEQ, NTM], F32, kind=okind).ap()
        self.pFM = nc.dram_tensor("pFM", [NFM, SEQ], F32, kind=okind).ap()
        self.ys = nc.dram_tensor("ys", [4, DM, SEQ], BF16, kind=okind).ap()
        self.mTM = nc.dram_tensor("mTM", [SEQ, 1536], F32).ap()
        self.wFM = nc.dram_tensor("wFM", [4, DM, SEQ], BF16).ap()
        self.wTM = nc.dram_tensor("wTM", [SEQ, 8, 2, 128], BF16).ap()
        self.wgate = nc.dram_tensor("wgate", [DM, SEQ], F32).ap()
        self.b_xres = [Buf() for _ in range(NT)]
        self.b_pTM = [[Buf() for _ in range(9)] for _ in range(NT)]
        self.b_pFM = [Buf() for _ in range(90)]
        self.b_ys = [[Buf() for _ in range(NT)] for _ in range(4)]
        self.b_ysF = [Buf() for _ in range(4)]
        self.b_mTM = Buf()
        self.b_wFM = [Buf() for _ in range(8)]
        self.b_wTM = [Buf() for _ in range(8)]
        self.b_wgate = [Buf() for _ in range(8)]
        self.b_out = Buf()
        self.ev = 0
        self.psi = 0
        self.ps_pool = list(range(8))
        self.force_evac = None

    def dma(self, out, in_, reads=(), writes=(), q="sp"):
        return self.S.add(q, lambda e: e.dma_start(out=out, in_=in_), reads=reads, writes=writes, dma=True)

    def mm(self, out, lhsT, rhs, start, stop, reads, writes):
        return self.S.add("pe", lambda e: e.matmul(out, lhsT=lhsT, rhs=rhs, start=start, stop=stop),
                          reads=reads, writes=writes)

    def tr(self, out, in_, reads, writes):
        ident = self.c("ident")
        return self.S.add("pe", lambda e: e.transpose(out, in_, ident), reads=list(reads) + [self.b_cst], writes=writes)

    def act(self, out, in_, func, reads, writes, bias=None, scale=None, accum=None):
        kw = {}
        if bias is not None:
            kw["bias"] = bias
        if scale is not None:
            kw["scale"] = scale
        if accum is not None:
            kw["accum_out"] = accum
        return self.S.add("act", lambda e: e.activation(out=out, in_=in_, func=func, **kw), reads=reads, writes=writes)

    def tt(self, out, in0, in1, op, reads, writes, eng="dve"):
        return self.S.add(eng, lambda e: e.tensor_tensor(out=out, in0=in0, in1=in1, op=op), reads=reads, writes=writes)

    def ts(self, out, in0, s1, s2, op0, op1=None, reads=(), writes=(), eng="dve"):
        if eng in ("pool", "act"):
            if op0 == ALU.mult and (op1 is None or op1 == ALU.add):
                kw = {"scale": s1}
                if op1 is not None:
                    kw["bias"] = s2
                return self.S.add("act", lambda e: e.activation(out=out, in_=in0, func=AF.Identity, **kw),
                                  reads=reads, writes=writes)
            eng = "dve"
        if op1 is None:
            op1 = ALU.add
            s2 = 0.0
        return self.S.add(eng, lambda e: e.tensor_scalar(out=out, in0=in0, scalar1=s1, scalar2=s2, op0=op0, op1=op1),
                          reads=reads, writes=writes)

    def stt(self, out, in0, scalar, in1, op0, op1, reads, writes, eng="dve"):
        eng = "dve"
        return self.S.add(eng, lambda e: e.scalar_tensor_tensor(out=out, in0=in0, scalar=scalar, in1=in1, op0=op0, op1=op1),
                          reads=reads, writes=writes)

    def cp(self, out, in_, reads, writes, eng=None):
        if eng is None and self.force_evac is not None:
            eng = self.force_evac
        if eng is None:
            self.ev ^= 1
            eng = "act" if self.ev else "dve"
        if eng == "act":
            return self.S.add("act", lambda e: e.copy(out=out, in_=in_), reads=reads, writes=writes)
        return self.S.add(eng, lambda e: e.tensor_copy(out=out, in_=in_), reads=reads, writes=writes)

    def memset(self, ap, val, writes, eng="dve"):
        return self.S.add(eng, lambda e: e.memset(ap, val), writes=writes)

    def recip(self, out, in_, reads, writes):
        return self.S.add("dve", lambda e: e.reciprocal(out=out, in_=in_), reads=reads, writes=writes)

    def psum(self):
        pool = self.ps_pool
        i = pool[self.psi % len(pool)]
        self.psi += 1
        return self.ps[i], self.psb[i]

    def c(self, name):
        o, w = CST_OFF[name]
        return self.cst[:, o:o + w]

    def pf(self, name):
        o, w = PFM_OFF[name]
        return self.pfm[:, o:o + w]

    def rowb(self, l, name, dst, buf):
        o, w = PROW_OFF[name]
        self.dma(dst, self.d["prow"][l, o:o + w].partition_broadcast(128), writes=[buf])

    def build(self):
        nc = self.nc
        with ExitStack() as st:
            self.st = st
            A = self.A = Arena(nc, st, "arena", 52000)
            self.ps = [st.enter_context(nc.psum_tensor("ps%d" % i, [128, 512], F32))[:, :] for i in range(8)]
            self.psb = [Buf("ps%d" % i) for i in range(8)]
            self.cst = A.alloc([NCST])
            self.b_cst = Buf("cst")
            self.dma(self.cst, self.d["cst"], writes=[self.b_cst])
            self.identb = A.alloc([128], BF16)
            self.b_identb = Buf()
            self.cp(self.identb, self.c("ident"), [self.b_cst], [self.b_identb], eng="dve")
            self.hT = A.alloc([8, SEQ], BF16)
            self.bh = [Buf("hT%d" % i) for i in range(NT)]
            self.pfm = A.alloc([NPF])
            self.b_pfm = Buf("pfm")
            self.nrow = A.alloc([DM])
            self.b_nrow = Buf("nrow")
            for l in range(self.L):
                self.layer(l)
            self.S.emit()
        return nc

    def on(self, name):
        return self.phases is None or name in self.phases

    def layer(self, l):
        A = self.A
        self.dma(self.pfm, self.d["pfm"][l], writes=[self.b_pfm])
        if l == 0:
            self.rowb(l, "norm_mix", self.nrow, self.b_nrow)
            A.mark()
            xin = [A.alloc([DM]) for _ in range(2)]
            bxin = [Buf() for _ in range(2)]
            self.norm_alloc()
            for i in range(NT):
                xt, bx = xin[i % 2], bxin[i % 2]
                self.dma(xt, self.x[i * 128:(i + 1) * 128, :], writes=[bx])
                self.norm_tile(xt, bx, i)
            A.release()
            self.S.barrier()
        fuse_lru = self.on("proj") and self.on("lru")
        if self.on("proj"):
            self.proj(l, with_lru=fuse_lru)
            self.S.barrier()
        if self.on("mamba"):
            self.mamba(l)
            self.S.barrier()
        if self.on("lru") and not fuse_lru:
            self.lru(l)
            self.S.barrier()
        if self.on("ret"):
            self.retention(l)
            self.S.barrier()
        if self.on("rwkv"):
            self.rwkv(l)
            self.S.barrier()
        if self.on("merge"):
            self.merge(l)
            self.S.barrier()
        if self.on("ffn"):
            self.ffn(l)
            self.S.barrier()

    def norm_alloc(self):
        A = self.A
        self.n_junk = A.alloc([DM])
        self.n_xn = [A.alloc([DM]) for _ in range(2)]
        self.n_ss = [A.alloc([2]) for _ in range(2)]
        self.b_n = [Buf() for _ in range(2)]
        self.b_njunk = Buf()
        self.n_i = 0

    def rms_scale(self, xt, bx, ss, bss, junk, bjunk, n=DM, eps=EPS):
        self.act(junk, xt, AF.Square, [bx], [bjunk, bss], accum=ss[:, 0:1])
        self.act(ss[:, 1:2], ss[:, 0:1], AF.Sqrt, [bss], [bss], scale=1.0 / n, bias=eps)
        self.recip(ss[:, 0:1], ss[:, 1:2], [bss], [bss])

    def norm_tile(self, xt, bx, i, final_out=None):
        j = self.n_i
        self.n_i ^= 1
        xn, ss, bn = self.n_xn[j], self.n_ss[j], self.b_n[j]
        self.rms_scale(xt, bx, ss, bn, self.n_junk, self.b_njunk)
        self.stt(xn, xt, ss[:, 0:1], self.nrow, ALU.mult, ALU.mult, [bx, bn, self.b_nrow], [bn])
        if final_out is not None:
            self.dma(final_out, xn, reads=[bn], writes=[self.b_out])
            return
        for half in range(2):
            ps, bp = self.psum()
            for k in range(4):
                kk = half * 4 + k
                self.tr(ps[:, k * 128:(k + 1) * 128], xn[:, kk * 128:(kk + 1) * 128], [bn], [bp])
            self.cp(self.hT[:, half * 4:half * 4 + 4, i * 128:(i + 1) * 128],
                    ps.rearrange("p (a b) -> p a b", a=4), [bp], [self.bh[i]])

    def wslab_init(self, n=2):
        A = self.A
        self.wsl = [A.alloc([8, 512], BF16) for _ in range(n)]
        self.b_wsl = [Buf() for _ in range(n)]
        self.wsl_i = 0

    def wslab(self, src2d, c0, wd):
        j = self.wsl_i
        self.wsl_i = (self.wsl_i + 1) % len(self.wsl)
        t, b = self.wsl[j], self.b_wsl[j]
        self.dma(t[:, :, 0:wd], src2d.rearrange("(k p) n -> p k n", p=128)[:, :, c0:c0 + wd], writes=[b], q="pool")
        return t, b

    def proj(self, l, with_lru=False):
        A = self.A
        A.mark()
        self.wslab_init(2)
        stg = [A.alloc([512]) for _ in range(6)]
        bstg = [Buf() for _ in range(6)]
        stF = [A.alloc([SEQ]) for _ in range(2)]
        bstF = [Buf() for _ in range(2)]
        hT, bh = self.hT, self.bh
        cnt = {"si": 0, "fi": 0}

        def tm_slab(sidx):
            c0 = sidx * 512
            wd = min(512, NTM - c0)
            wt, bw = self.wslab(self.d["w_tm"][l], c0, wd)
            for i in range(NT):
                ps, bp = self.psum()
                for k in range(8):
                    self.mm(ps[:, 0:wd], hT[:, k, i * 128:(i + 1) * 128], wt[:, k, 0:wd], k == 0, k == 7, [bh[i], bw], [bp])
                s, bs = stg[cnt["si"] % 6], bstg[cnt["si"] % 6]
                cnt["si"] += 1
                self.cp(s[:, 0:wd], ps[:, 0:wd], [bp], [bs])
                self.dma(self.pTM[i * 128:(i + 1) * 128, c0:c0 + wd], s[:, 0:wd], reads=[bs], writes=[self.b_pTM[i][sidx]])

        def fm_slab(sidx):
            c0 = sidx * 512
            wd = min(512, NFM - c0)
            wt, bw = self.wslab(self.d["w_fm"][l], c0, wd)
            for cc in range(wd // 128):
                f = c0 // 128 + cc
                s, bs = stF[cnt["fi"] % 2], bstF[cnt["fi"] % 2]
                cnt["fi"] += 1
                for tb in range(4):
                    ps, bp = self.psum()
                    for k in range(8):
                        self.mm(ps, wt[:, k, cc * 128:(cc + 1) * 128], hT[:, k, tb * 512:(tb + 1) * 512], k == 0, k == 7,
                                [bw] + bh[4 * tb:4 * tb + 4], [bp])
                    self.cp(s[:, tb * 512:(tb + 1) * 512], ps, [bp], [bs])
                self.dma(self.pFM[f * 128:(f + 1) * 128, :], s, reads=[bs], writes=[self.b_pFM[f]])

        def rest():
            for sidx in range(9):
                tm_slab(sidx)
            for sidx in list(range(0, 4)) + list(range(8, 23)):
                fm_slab(sidx)

        for sidx in range(4, 8):
            fm_slab(sidx)
        if with_lru:
            self.ps_pool = [0, 1, 2, 3, 4, 5]
            self.force_evac = "act"
            oa = self.S.record(rest)
            self.force_evac = None
            self.ps_pool = [6, 7]
            ob = self.S.record(lambda: self.lru(l, q="act"))
            self.ps_pool = list(range(8))
            self.S.replay_interleaved(oa, ob)
        else:
            rest()
        A.release()

    def conv_fm(self, out, cin, K, w, b, reads, writes, n, eng="dve"):
        self.ts(out, cin[:, 0:n], w[:, 0:1], b, ALU.mult, ALU.add, reads=reads, writes=writes, eng=eng)
        for j in range(1, K):
            self.stt(out, cin[:, j:j + n], w[:, j:j + 1], out, ALU.mult, ALU.add, reads, writes, eng=eng)

    def mamba(self, l):
        A = self.A
        A.mark()
        cw = self.pf("ssm_cw").rearrange("p (t j) -> p t j", j=4)
        cb = self.pf("ssm_cb")
        BT = A.alloc([4, SEQ], BF16)
        CT = A.alloc([4, SEQ], BF16)
        bBT, bCT = Buf(), Buf()
        A.mark()
        cin = [A.alloc([3 + SEQ]) for _ in range(2)]
        bcin = [Buf() for _ in range(2)]
        acc = [A.alloc([SEQ]) for _ in range(2)]
        bacc = [Buf() for _ in range(2)]
        tst = [A.alloc([NT, 128]) for _ in range(2)]
        btst = [Buf() for _ in range(2)]
        cinb = [A.alloc([3 + SEQ], BF16) for _ in range(2)]
        bcinb = [Buf() for _ in range(2)]
        dgm = [A.alloc([4, 128], BF16) for _ in range(2)]
        bdgm = [Buf() for _ in range(2)]
        for j in range(2):
            self.memset(cin[j][:, 0:3], 0.0, [bcin[j]])
        mTMv = self.mTM.rearrange("(i p) c -> p i c", p=128)
        for f in range(16):
            ci, bci, ac, bac = cin[f % 2], bcin[f % 2], acc[f % 2], bacc[f % 2]
            self.dma(ci[:, 3:3 + SEQ], self.pFM[(F_XBC + f) * 128:(F_XBC + f + 1) * 128, :], reads=[self.b_pFM[F_XBC + f]], writes=[bci])
            cbf, bcbf, dgm_, bdgm_ = cinb[f % 2], bcinb[f % 2], dgm[f % 2], bdgm[f % 2]
            self.cp(cbf, ci, [bci], [bcbf], eng="dve")
            for j in range(4):
                self.ts(dgm_[:, j, :], self.identb, cw[:, f, j:j + 1], None, ALU.mult, reads=[self.b_identb, self.b_pfm], writes=[bdgm_], eng="act")
            for tb in range(4):
                ps, bp = self.psum()
                for j in range(4):
                    self.mm(ps, dgm_[:, j, :], cbf[:, tb * 512 + j:tb * 512 + j + 512], j == 0, j == 3, [bdgm_, bcbf], [bp])
                self.act(ac[:, tb * 512:(tb + 1) * 512], ps, AF.Silu, [bp, self.b_pfm], [bac], bias=cb[:, f:f + 1])
            if f < 12:
                ts_, bts = tst[f % 2], btst[f % 2]
                for q in range(4):
                    ps, bp = self.psum()
                    for k in range(4):
                        i = q * 4 + k
                        self.tr(ps[:, k * 128:(k + 1) * 128], ac[:, i * 128:(i + 1) * 128], [bac], [bp])
                    self.cp(ts_[:, q * 4:q * 4 + 4, :], ps.rearrange("p (a b) -> p a b", a=4), [bp], [bts])
                self.dma(mTMv[:, :, f * 128:(f + 1) * 128], ts_, reads=[bts], writes=[self.b_mTM], q="pool")
            if 8 <= f < 12:
                self.cp(BT[:, f - 8, :], ac, [bac], [bBT], eng="pool")
            if f >= 12:
                self.cp(CT[:, f - 12, :], ac, [bac], [bCT], eng="pool")
        A.release()
        self.S.barrier()
        rows = A.alloc([3, 16])
        brows = Buf()
        self.rowb(l, "dt_bias", rows[:, 0, :], brows)
        self.rowb(l, "a_log", rows[:, 1, :], brows)
        self.rowb(l, "ssm_d", rows[:, 2, :], brows)
        self.act(rows[:, 1, :], rows[:, 1, :], AF.Exp, [brows], [brows])
        self.ts(rows[:, 1, :], rows[:, 1, :], -1.0, None, ALU.mult, reads=[brows], writes=[brows])
        nw = A.alloc([DM])
        bnw = Buf()
        self.rowb(l, "ssm_norm", nw, bnw)
        H = A.alloc([16, 64])
        Hb = A.alloc([16, 64], BF16)
        bH, bHb = Buf(), Buf()
        self.memset(H, 0.0, [bH])
        self.memset(Hb, 0.0, [bHb])
        mt = [A.alloc([1536]) for _ in range(2)]
        zt = [A.alloc([DM]) for _ in range(2)]
        dtr = [A.alloc([16]) for _ in range(2)]
        bld = [Buf() for _ in range(2)]
        sm2 = [A.alloc([8, 16]) for _ in range(2)]
        bsm2 = [Buf() for _ in range(2)]
        dtabc = A.alloc([2, 16, 128])
        bdtabc = Buf()
        decay = A.alloc([16, 128], BF16)
        bdecay = Buf()
        CBs = A.alloc([4, 128], BF16)
        bCBs = Buf()
        Lm2 = [A.alloc([16, 128], BF16) for _ in range(2)]
        bLm2 = [Buf() for _ in range(2)]
        xdt2 = [A.alloc([2, 16, 64], BF16) for _ in range(2)]
        bxdt2 = [Buf() for _ in range(2)]
        Bbf2 = [A.alloc([512], BF16) for _ in range(2)]
        bBbf2 = [Buf() for _ in range(2)]
        y = A.alloc([16, 64])
        t1 = A.alloc([16, 64])
        by, bt1 = Buf(), Buf()
        ss = A.alloc([2])
        yst = [A.alloc([8, 128], BF16) for _ in range(2)]
        byst = [Buf() for _ in range(2)]
        tri, ones, ident, negm = self.c("tri"), self.c("ones"), self.c("ident"), self.c("negmask4")
        bc = self.b_cst
        ysv = self.ys[0].rearrange("(c p) t -> p c t", p=128)
        def stA(i):
                m, z, dr, bl = mt[i % 2], zt[i % 2], dtr[i % 2], bld[i % 2]
                sl = slice(i * 128, (i + 1) * 128)
                sm, bsm, Lm, bLm, xdt, bxdt, Bbf, bBbf = sm2[i % 2], bsm2[i % 2], Lm2[i % 2], bLm2[i % 2], xdt2[i % 2], bxdt2[i % 2], Bbf2[i % 2], bBbf2[i % 2]
                xs = m[:, 0:1024].rearrange("p (h d) -> p h d", h=16)
                t, ab, dt, dta, acum, eac, dte, cd = (sm[:, q, :] for q in range(8))
                self.dma(m, self.mTM[sl, :], reads=[self.b_mTM], writes=[bl])
                self.dma(z, self.pTM[sl, T_Z:T_Z + 1024], reads=self.b_pTM[i][0:2], writes=[bl])
                self.dma(dr, self.pTM[sl, T_DT:T_DT + 16], reads=[self.b_pTM[i][8]], writes=[bl])
                self.tt(t, dr, rows[:, 0, :], ALU.add, [bl, brows], [bsm])
                self.act(ab, t, AF.Abs, [bsm], [bsm])
                self.act(ab, ab, AF.Exp, [bsm], [bsm], scale=-1.0)
                self.act(ab, ab, AF.Ln, [bsm], [bsm], bias=1.0)
                self.stt(dt, t, 0.0, ab, ALU.max, ALU.add, [bsm], [bsm])
                self.tt(dta, dt, rows[:, 1, :], ALU.mult, [bsm, brows], [bsm])
                self.cp(dtabc[:, 0, :, :], dta.unsqueeze(2).to_broadcast([128, 16, 128]), [bsm], [bdtabc], eng="dve")
                self.ts(dtabc[:, 1, :, :], dtabc[:, 0, :, :], -1.0, None, ALU.mult, reads=[bdtabc], writes=[bdtabc], eng="pool")
                ps, bp = self.psum()
                self.mm(ps[:, 0:16], tri, dta, True, True, [bc, bsm], [bp])
                self.mm(ps[:, 16:32], ones, dta, True, True, [bc, bsm], [bp])
                self.cp(acum, ps[:, 0:16], [bp], [bsm], eng="dve")
                self.act(eac, ps[:, 0:16], AF.Exp, [bp], [bsm])
                self.act(cd, ps[:, 16:32], AF.Exp, [bp], [bsm])
                self.tt(dte, ps[:, 16:32], acum, ALU.subtract, [bp, bsm], [bsm])
                self.act(dte, dte, AF.Exp, [bsm], [bsm])
                self.tt(dte, dte, dt, ALU.mult, [bsm], [bsm])
                self.tt(xdt[:, 0, :, :], xs, dt.unsqueeze(2).to_broadcast([128, 16, 64]), ALU.mult, [bl, bsm], [bxdt])
                self.tt(xdt[:, 1, :, :], xs, dte.unsqueeze(2).to_broadcast([128, 16, 64]), ALU.mult, [bl, bsm], [bxdt], eng="pool")
                self.cp(Bbf, m[:, 1024:1536], [bl], [bBbf], eng="pool")
                ps, bp = self.psum()
                for g in range(4):
                    self.mm(ps[:, g * 128:(g + 1) * 128], BT[:, g, sl], CT[:, g, sl], True, True, [bBT, bCT], [bp])
                self.cp(CBs, ps.rearrange("p (a b) -> p a b", a=4), [bp], [bCBs], eng="act")
                for g in range(4):
                    ps, bp = self.psum()
                    for hh in range(4):
                        h = g * 4 + hh
                        o_ = ps[:, hh * 128:(hh + 1) * 128]
                        self.mm(o_, dtabc[:, 0, h, :], tri, True, False, [bdtabc, bc], [bp])
                        self.mm(o_, tri, dtabc[:, 1, h, :], False, False, [bdtabc, bc], [bp])
                        self.mm(o_, ident, negm[:, 0:128], False, True, [bc], [bp])
                    self.act(decay[:, 4 * g:4 * g + 4, :], ps.rearrange("p (a b) -> p a b", a=4), AF.Exp, [bp], [bdecay])
                self.tt(Lm.rearrange("p (g a) b -> p g a b", g=4), decay.rearrange("p (g a) b -> p g a b", g=4),
                        CBs.unsqueeze(2).to_broadcast([128, 4, 4, 128]), ALU.mult, [bdecay, bCBs], [bLm])

        def stB(i):
                m, z, dr, bl = mt[i % 2], zt[i % 2], dtr[i % 2], bld[i % 2]
                sl = slice(i * 128, (i + 1) * 128)
                sm, bsm, Lm, bLm, xdt, bxdt, Bbf, bBbf = sm2[i % 2], bsm2[i % 2], Lm2[i % 2], bLm2[i % 2], xdt2[i % 2], bxdt2[i % 2], Bbf2[i % 2], bBbf2[i % 2]
                xs = m[:, 0:1024].rearrange("p (h d) -> p h d", h=16)
                t, ab, dt, dta, acum, eac, dte, cd = (sm[:, q, :] for q in range(8))
                yv = y.rearrange("p h d -> p (h d)")
                t1v = t1.rearrange("p h d -> p (h d)")
                for half in range(2):
                    psd, bpd = self.psum()
                    pso, bpo = self.psum()
                    for hh in range(8):
                        h = half * 8 + hh
                        self.mm(psd[:, hh * 64:(hh + 1) * 64], Lm[:, h, :], xdt[:, 0, h, :], True, True, [bLm, bxdt], [bpd])
                        self.mm(pso[:, hh * 64:(hh + 1) * 64], CT[:, h // 4, sl], Hb[:, h, :], True, True, [bCT, bHb], [bpo])
                    hs = slice(half * 8, half * 8 + 8)
                    self.tt(t1[:, hs, :], pso.rearrange("p (h d) -> p h d", h=8), eac[:, hs].unsqueeze(2).to_broadcast([128, 8, 64]),
                            ALU.mult, [bpo, bsm], [bt1])
                    self.tt(y[:, hs, :], psd.rearrange("p (h d) -> p h d", h=8), t1[:, hs, :], ALU.add, [bpd, bt1], [by])
                self.tt(t1, xs, rows[:, 2, :].unsqueeze(2).to_broadcast([128, 16, 64]), ALU.mult, [bl, brows], [bt1], eng="pool")
                self.tt(y, y, t1, ALU.add, [by, bt1], [by])
                for half in range(2):
                    pss, bps = self.psum()
                    for hh in range(8):
                        h = half * 8 + hh
                        g = h // 4
                        self.mm(pss[:, hh * 64:(hh + 1) * 64], Bbf[:, g * 128:(g + 1) * 128], xdt[:, 1, h, :], True, True, [bBbf, bxdt], [bps])
                    hs = slice(half * 8, half * 8 + 8)
                    self.tt(H[:, hs, :], H[:, hs, :], cd[:, hs].unsqueeze(2).to_broadcast([128, 8, 64]), ALU.mult, [bH, bsm, bpo], [bH])
                    self.tt(H[:, hs, :], H[:, hs, :], pss.rearrange("p (h d) -> p h d", h=8), ALU.add, [bH, bps], [bH])
                self.cp(Hb, H, [bH], [bHb], eng="act")
                self.act(z, z, AF.Silu, [bl], [bl])
                self.tt(yv, yv, z, ALU.mult, [by, bl], [by])
                self.rms_scale(yv, by, ss, bt1, t1v, bt1)
                self.stt(yv, yv, ss[:, 0:1], nw, ALU.mult, ALU.mult, [by, bt1, bnw], [by])
                yo, byo = yst[i % 2], byst[i % 2]
                for half in range(2):
                    ps, bp = self.psum()
                    for k in range(4):
                        kk = half * 4 + k
                        self.tr(ps[:, k * 128:(k + 1) * 128], yv[:, kk * 128:(kk + 1) * 128], [by], [bp])
                    self.cp(yo[:, half * 4:half * 4 + 4, :], ps.rearrange("p (a b) -> p a b", a=4), [bp], [byo])
                self.dma(ysv[:, :, sl], yo, reads=[byo], writes=[self.b_ys[0][i]], q="pool")

        PA, PB = [0, 1, 2, 3], [4, 5, 6, 7]
        self.ps_pool = PA
        stA(0)
        for i in range(NT):
            self.ps_pool = PB
            ob = self.S.record(lambda: stB(i))
            oa = []
            if i + 1 < NT:
                self.ps_pool = PA
                oa = self.S.record(lambda: stA(i + 1))
            self.S.replay_interleaved(oa, ob)
        self.ps_pool = list(range(8))
        A.release()

    def lru(self, l, q=None):
        A = self.A
        A.mark()
        cw = self.pf("lru_cw").rearrange("p (t j) -> p t j", j=4)
        cb, ba, bi, lam = self.pf("lru_cb"), self.pf("lru_ba"), self.pf("lru_bi"), self.pf("lru_lam")
        wa = A.alloc([8, 128], BF16)
        wi = A.alloc([8, 128], BF16)
        bwa = Buf()
        self.dma(wa, self.d["lru_w_a_bd"][l].rearrange("t p n -> p t n"), writes=[bwa], q="pool")
        self.dma(wi, self.d["lru_w_i_bd"][l].rearrange("t p n -> p t n"), writes=[bwa], q="pool")
        cf = A.alloc([2, 8])
        bcf = Buf()
        tmp = A.alloc([2, 8])
        self.act(tmp[:, 0, :], lam, AF.Abs, [self.b_pfm], [bcf])
        self.act(tmp[:, 0, :], tmp[:, 0, :], AF.Exp, [bcf], [bcf], scale=-1.0)
        self.act(tmp[:, 0, :], tmp[:, 0, :], AF.Ln, [bcf], [bcf], bias=1.0)
        self.ts(tmp[:, 1, :], lam, -1.0, 0.0, ALU.mult, ALU.max, reads=[self.b_pfm], writes=[bcf])
        self.tt(tmp[:, 0, :], tmp[:, 0, :], tmp[:, 1, :], ALU.add, [bcf], [bcf])
        self.ts(cf[:, 0, :], tmp[:, 0, :], -8.0, None, ALU.mult, reads=[bcf], writes=[bcf])
        self.ts(cf[:, 1, :], tmp[:, 0, :], -16.0, None, ALU.mult, reads=[bcf], writes=[bcf])
        cin = [A.alloc([3 + SEQ]) for _ in range(2)]
        gin = [A.alloc([SEQ]) for _ in range(2)]
        bin_ = [Buf() for _ in range(2)]
        for j in range(2):
            self.memset(cin[j][:, 0:3], 0.0, [bin_[j]])
        xc = A.alloc([SEQ])
        xcb = A.alloc([SEQ], BF16)
        r = A.alloc([SEQ])
        ii = A.alloc([SEQ])
        a = A.alloc([SEQ])
        u = A.alloc([SEQ])
        hh_ = r
        g2 = A.alloc([SEQ])
        yo = [A.alloc([SEQ], BF16) for _ in range(2)]
        byo = [Buf() for _ in range(2)]
        b_xc, b_xcb, b_r, b_i, b_a, b_u, b_h, b_g = (Buf() for _ in range(8))
        b_h = b_r
        bpf = self.b_pfm
        for f in range(8):
            ci, gi, bl = cin[f % 2], gin[f % 2], bin_[f % 2]
            self.dma(ci[:, 3:3 + SEQ], self.pFM[(F_LX + f) * 128:(F_LX + f + 1) * 128, :], reads=[self.b_pFM[F_LX + f]], writes=[bl], q=q or "sp")
            self.dma(gi, self.pFM[(F_LG + f) * 128:(F_LG + f + 1) * 128, :], reads=[self.b_pFM[F_LG + f]], writes=[bl], q=q or "sp")
            self.conv_fm(xc, ci, 4, cw[:, f, :], cb[:, f:f + 1], [bl, bpf], [b_xc], SEQ)
            self.cp(xcb, xc, [b_xc], [b_xcb], eng="act")
            for tb in range(4):
                ts_ = slice(tb * 512, (tb + 1) * 512)
                ps, bp = self.psum()
                self.mm(ps, wa[:, f, :], xcb[:, ts_], True, True, [bwa, b_xcb], [bp])
                self.act(r[:, ts_], ps, AF.Sigmoid, [bp, bpf], [b_r], bias=ba[:, f:f + 1])
                ps, bp = self.psum()
                self.mm(ps, wi[:, f, :], xcb[:, ts_], True, True, [bwa, b_xcb], [bp])
                self.act(ii[:, ts_], ps, AF.Sigmoid, [bp, bpf], [b_i], bias=bi[:, f:f + 1])
            self.act(a, r, AF.Exp, [b_r, bcf], [b_a], scale=cf[:, 0, f:f + 1])
            self.act(u, r, AF.Exp, [b_r, bcf], [b_u], scale=cf[:, 1, f:f + 1])
            self.act(u, u, AF.Sqrt, [b_u], [b_u], scale=-1.0, bias=1.0)
            self.tt(ii, ii, xc, ALU.mult, [b_i, b_xc], [b_i], eng="pool")
            self.tt(u, u, ii, ALU.mult, [b_u, b_i], [b_u])
            self.S.add("dve", lambda e, a=a, u=u, o=hh_: e.tensor_tensor_scan(out=o, data0=a, data1=u, initial=0.0, op0=ALU.mult, op1=ALU.add),
                       reads=[b_a, b_u], writes=[b_h])
            self.tt(g2, gi, gi, ALU.mult, [bl], [b_g], eng="pool")
            self.ts(g2, g2, 0.044715, 1.0, ALU.mult, ALU.add, reads=[b_g], writes=[b_g], eng="pool")
            self.tt(g2, g2, gi, ALU.mult, [b_g, bl], [b_g], eng="pool")
            self.act(g2, g2, AF.Sigmoid, [b_g], [b_g], scale=1.5957691216057308)
            self.tt(g2, g2, gi, ALU.mult, [b_g, bl], [b_g], eng="pool")
            self.tt(yo[f % 2], hh_, g2, ALU.mult, [b_h, b_g], [byo[f % 2]])
            self.dma(self.ys[1][f * 128:(f + 1) * 128, :], yo[f % 2], reads=[byo[f % 2]], writes=[self.b_ysF[1]], q=q or "pool")
        A.release()

    def retention(self, l):
        A = self.A
        A.mark()
        nwt = self.pf("ret_nw")
        cosT = self.c("cos").rearrange("p (i d) -> p i d", i=16)
        sinT = self.c("sin").rearrange("p (i d) -> p i d", i=16)
        GT = self.c("GT")
        gqA = self.c("gqA")[0:64, :].rearrange("p (f t) -> p f t", f=8)
        gqB = self.c("gqB")[0:64, :].rearrange("p (f t) -> p f t", f=8)
        kend, cdt = self.c("kend"), self.c("cdt")
        bc = self.b_cst
        qk = [A.alloc([16, 64]) for _ in range(2)]
        vt = [A.alloc([DM]) for _ in range(2)]
        gt = [A.alloc([8, 128]) for _ in range(2)]
        bl = [Buf() for _ in range(2)]
        rot = A.alloc([16, 64])
        tmp = A.alloc([16, 32])
        brot = Buf()
        ksc = A.alloc([8, 64], BF16)
        bksc = Buf()
        qT = A.alloc([8, 128], BF16, parts=64)
        kT = A.alloc([8, 128], BF16, parts=64)
        qTA = A.alloc([8, 128], BF16, parts=64)
        qTB = A.alloc([8, 128], BF16, parts=64)
        bqT = Buf()
        vb = A.alloc([DM], BF16)
        bvb = Buf()
        ST = A.alloc([8, 128], BF16)
        bST = Buf()
        R = A.alloc([3, 8, 128], parts=64)
        Rb = A.alloc([3, 8, 128], BF16, parts=64)
        bR, bRb = Buf(), Buf()
        self.memset(R, 0.0, [bR])
        self.memset(Rb, 0.0, [bRb])
        ysb = A.alloc([8, 128])
        ysq = A.alloc([8, 128])
        bys = Buf()
        mean = A.alloc([8, 128])
        var = A.alloc([8, 128])
        bmv = Buf()
        yo = [A.alloc([8, 128], BF16) for _ in range(2)]
        byo = [Buf() for _ in range(2)]
        onesm = A.alloc([128])
        bon = Buf()
        self.ts(onesm, self.c("ones"), 1.0 / 128.0, None, ALU.mult, reads=[bc], writes=[bon])
        ysv = self.ys[2].rearrange("(c p) t -> p c t", p=128)
        gv = self.pFM[F_RG * 128:(F_RG + 8) * 128, :].rearrange("(c p) t -> p c t", p=128)
        pend_st = None
        for i in range(NT):
            q_, v_, g_, b_ = qk[i % 2], vt[i % 2], gt[i % 2], bl[i % 2]
            sl = slice(i * 128, (i + 1) * 128)
            self.dma(q_.rearrange("p h d -> p (h d)"), self.pTM[sl, T_QK:T_QK + 1024], reads=self.b_pTM[i][2:4], writes=[b_])
            self.dma(v_, self.pTM[sl, T_RV:T_RV + 1024], reads=self.b_pTM[i][4:6], writes=[b_])
            self.dma(g_, gv[:, :, sl], reads=self.b_pFM[F_RG:F_RG + 8], writes=[b_])
            if pend_st is not None:
                pend_st()
                pend_st = None
            x1, x2 = q_[:, :, 0:32], q_[:, :, 32:64]
            cb_ = cosT[:, i, :].unsqueeze(1).to_broadcast([128, 16, 32])
            sb_ = sinT[:, i, :].unsqueeze(1).to_broadcast([128, 16, 32])
            self.tt(rot[:, :, 0:32], x1, cb_, ALU.mult, [b_, bc], [brot])
            self.tt(tmp, x2, sb_, ALU.mult, [b_, bc], [brot], eng="pool")
            self.tt(rot[:, :, 0:32], rot[:, :, 0:32], tmp, ALU.subtract, [brot], [brot])
            self.tt(rot[:, :, 32:64], x1, sb_, ALU.mult, [b_, bc], [brot])
            self.tt(tmp, x2, cb_, ALU.mult, [b_, bc], [brot], eng="pool")
            self.tt(rot[:, :, 32:64], rot[:, :, 32:64], tmp, ALU.add, [brot], [brot])
            self.tt(ksc, rot[:, 8:16, :], kend.unsqueeze(2).to_broadcast([128, 8, 64]), ALU.mult, [brot, bc], [bksc])
            self.cp(vb, v_, [b_], [bvb], eng="act")
            rv = rot.rearrange("p h d -> p (h d)")
            for quad in range(4):
                ps, bp = self.psum()
                for k in range(4):
                    hh = quad * 4 + k
                    self.tr(ps[0:64, k * 128:(k + 1) * 128], rv[:, hh * 64:(hh + 1) * 64], [brot], [bp])
                psv = ps[0:64, :].rearrange("p (a b) -> p a b", a=4)
                if quad < 2:
                    hs = slice(quad * 4, quad * 4 + 4)
                    self.cp(qT[:, hs, :], psv, [bp], [bqT], eng="act")
                    self.tt(qTA[:, hs, :], psv, gqA[:, hs, :], ALU.mult, [bp, bc], [bqT])
                    self.tt(qTB[:, hs, :], psv, gqB[:, hs, :], ALU.mult, [bp, bc], [bqT])
                else:
                    hs = slice((quad - 2) * 4, (quad - 2) * 4 + 4)
                    self.ts(kT[:, hs, :], psv, 0.125, None, ALU.mult, reads=[bp], writes=[bqT])
            for half in range(2):
                ps, bp = self.psum()
                for hh in range(4):
                    h = half * 4 + hh
                    self.mm(ps[:, hh * 128:(hh + 1) * 128], kT[:, h, :], qT[:, h, :], True, True, [bqT], [bp])
                self.tt(ST[:, half * 4:half * 4 + 4, :].rearrange("p a b -> p (a b)"), ps, GT[:, half * 512:(half + 1) * 512],
                        ALU.mult, [bp, bc], [bST])
            for cch in range(2):
                pr = slice(cch * 64, (cch + 1) * 64)
                for half in range(2):
                    ps, bp = self.psum()
                    for hh in range(4):
                        h = half * 4 + hh
                        self.mm(ps[0:64, hh * 128:(hh + 1) * 128], ksc[pr, h, :], vb[pr, h * 128:(h + 1) * 128], True, True, [bksc, bvb], [bp])
                    hs = slice(half * 4, half * 4 + 4)
                    self.tt(R[:, cch + 1, hs, :], R[:, cch, hs, :], cdt[0:64, hs].unsqueeze(2).to_broadcast([64, 4, 128]), ALU.mult, [bR, bc], [bR])
                    self.tt(R[:, cch + 1, hs, :], R[:, cch + 1, hs, :], ps[0:64, :].rearrange("p (a b) -> p a b", a=4), ALU.add, [bR, bp], [bR])
                self.cp(Rb[:, cch + 1, :, :], R[:, cch + 1, :, :], [bR], [bRb], eng="act")
            for half in range(2):
                ps, bp = self.psum()
                for hh in range(4):
                    h = half * 4 + hh
                    o = ps[:, hh * 128:(hh + 1) * 128]
                    self.mm(o, vb[:, h * 128:(h + 1) * 128], ST[:, h, :], True, False, [bvb, bST], [bp])
                    self.mm(o, Rb[:, 0, h, :], qTA[:, h, :], False, False, [bRb, bqT], [bp])
                    self.mm(o, Rb[:, 1, h, :], qTB[:, h, :], False, True, [bRb, bqT], [bp])
                hs = slice(half * 4, half * 4 + 4)
                self.cp(ysb[:, hs, :], ps.rearrange("p (a b) -> p a b", a=4), [bp], [bys], eng="dve")
                self.act(ysq[:, hs, :], ps.rearrange("p (a b) -> p a b", a=4), AF.Square, [bp], [bys])
            self.cp(R[:, 0, :, :], R[:, 2, :, :], [bR], [bR], eng="pool")
            self.cp(Rb[:, 0, :, :], Rb[:, 2, :, :], [bRb], [bRb], eng="pool")
            for half in range(2):
                hs = slice(half * 4, half * 4 + 4)
                ps, bp = self.psum()
                self.mm(ps, onesm, ysb[:, hs, :].rearrange("p a b -> p (a b)"), True, True, [bon, bys], [bp])
                self.cp(mean[:, hs, :], ps.rearrange("p (a b) -> p a b", a=4), [bp], [bmv], eng="act")
                ps2, bp2 = self.psum()
                self.mm(ps2, onesm, ysq[:, hs, :].rearrange("p a b -> p (a b)"), True, True, [bon, bys], [bp2])
                self.tt(var[:, hs, :], mean[:, hs, :], mean[:, hs, :], ALU.mult, [bmv], [bmv], eng="pool")
                self.tt(var[:, hs, :], ps2.rearrange("p (a b) -> p a b", a=4), var[:, hs, :], ALU.subtract, [bp2, bmv], [bmv])
            self.act(var, var, AF.Ln, [bmv], [bmv], bias=EPS)
            self.act(var, var, AF.Exp, [bmv], [bmv], scale=-0.5)
            self.tt(ysb, ysb, mean, ALU.subtract, [bys, bmv], [bys])
            self.tt(ysb, ysb, var, ALU.mult, [bys, bmv], [bys])
            self.tt(ysb, ysb, nwt.unsqueeze(2).to_broadcast([128, 8, 128]), ALU.mult, [bys, self.b_pfm], [bys], eng="pool")
            self.act(g_, g_, AF.Silu, [b_], [b_])
            self.tt(yo[i % 2], ysb, g_, ALU.mult, [bys, b_], [byo[i % 2]])
            pend_st = (lambda i=i, sl=sl: self.dma(ysv[:, :, sl], yo[i % 2], reads=[byo[i % 2]], writes=[self.b_ys[2][i]], q="sp"))
        pend_st()
        A.release()

    def rwkv(self, l):
        A = self.A
        bc = self.b_cst
        A.mark()
        mu = self.pf("mu_fm")
        w0, a0, k_k, k_a, r_k = (self.pf(n) for n in ("w0", "a0", "k_k", "k_a", "r_k"))
        w2a2 = A.alloc([DM], BF16)
        g2w = A.alloc([DM], BF16)
        bwl = Buf()
        self.dma(w2a2, self.d["w2a2"][l], writes=[bwl], q="pool")
        self.dma(g2w, self.d["g2"][l], writes=[bwl], q="pool")
        omka = A.alloc([8])
        bomka = Buf()
        self.ts(omka, k_a, -1.0, 1.0, ALU.mult, ALU.add, reads=[self.b_pfm], writes=[bomka])
        ecd = A.alloc([2, 16])
        becd = [Buf() for _ in range(2)]
        self.ecdD = self.nc.dram_tensor("ecdD%d" % l, [DM, 16], F32).ap()
        becdD = Buf()
        rks = A.alloc([16, 16])
        brks = Buf()
        A.mark()
        HS = SEQ // 2
        NH = NT // 2
        rmask = A.alloc([NH, 128], BF16)
        brm = Buf()
        self.memset(rmask, 1.0, [brm])
        self.memset(rmask[:, :, 0:1], 0.0, [brm])
        rmaskf = rmask.rearrange("p a b -> p (a b)")
        lo = A.alloc([SEQ], BF16)
        gl = A.alloc([SEQ], BF16)
        blo = Buf()
        lin = [[A.alloc([1 + HS]) for _ in range(2)] for _ in range(2)]
        blin = [[Buf() for _ in range(2)] for _ in range(2)]
        Wt = [[A.alloc([HS]) for _ in range(9)] for _ in range(2)]
        bWt = [[Buf() for _ in range(9)] for _ in range(2)]
        tstt = [A.alloc([NH, 2, 128], BF16) for _ in range(2)]
        btstt = [Buf() for _ in range(2)]
        fstt = [[A.alloc([HS], BF16) for _ in range(4)] for _ in range(2)]
        bfstt = [[Buf() for _ in range(4)] for _ in range(2)]
        ecd2 = [A.alloc([NH]) for _ in range(2)]
        becd2 = [Buf() for _ in range(2)]
        bpf = self.b_pfm
        wTMv = self.wTM.rearrange("(i p) f c d -> p i f c d", p=128)

        def mixload(ft, mcol, dst, bdst, st, k, half):
            li, bl = lin[st][k], blin[st][k]
            if half == 0:
                self.memset(li[:, 0:1], 0.0, [bl])
                self.dma(li[:, 1:1 + HS], self.pFM[ft * 128:(ft + 1) * 128, 0:HS], reads=[self.b_pFM[ft]], writes=[bl])
            else:
                self.dma(li, self.pFM[ft * 128:(ft + 1) * 128, HS - 1:SEQ], reads=[self.b_pFM[ft]], writes=[bl])
            self.tt(dst, li[:, 0:HS], li[:, 1:1 + HS], ALU.subtract, [bl], [bdst], eng="pool")
            self.stt(dst, dst, mu[:, mcol:mcol + 1], li[:, 1:1 + HS], ALU.mult, ALU.add, [bdst, bl, bpf], [bdst])

        for half in range(2):
            hsl = slice(half * HS, (half + 1) * HS)
            mixload(F_LO, 16, Wt[half][0], bWt[half][0], half, 0, half)
            self.act(lo[0:64, hsl], Wt[half][0][0:64, :], AF.Tanh, [bWt[half][0]], [blo])
            self.cp(lo[64:128, hsl], Wt[half][0][64:128, :], [bWt[half][0]], [blo], eng="dve")
            mixload(F_GL, 17, Wt[half][1], bWt[half][1], half, 1, half)
            self.act(gl[:, hsl], Wt[half][1], AF.Sigmoid, [bWt[half][1]], [blo])

        def unit(f, half, st):
            rm, km, lw, aa, kk, t0, t1, t2, t3 = Wt[st]
            b_rm, b_km, b_lw, b_aa, b_kk, b_t0, b_t1, b_t2, b_t3 = bWt[st]
            fst, bfst, tst, btst = fstt[st], bfstt[st], tstt[st], btstt[st]
            fs = slice(f * 128, (f + 1) * 128)
            T0 = half * HS
            hsl = slice(T0, T0 + HS)
            mixload(F_WR + f, f, rm, b_rm, st, 0, half)
            mixload(F_WK + f, 8 + f, km, b_km, st, 1, half)
            for fn_ in defer[st]:
                fn_()
            defer[st] = []
            for tb in range(2):
                ts_ = slice(tb * 512, (tb + 1) * 512)
                gs_ = slice(T0 + tb * 512, T0 + (tb + 1) * 512)
                ps, bp = self.psum()
                self.mm(ps, w2a2[0:64, fs], lo[0:64, gs_], True, True, [bwl, blo], [bp])
                self.act(lw[:, ts_], ps, AF.Sigmoid, [bp, bpf], [b_lw], bias=w0[:, f:f + 1])
                ps, bp = self.psum()
                self.mm(ps, w2a2[64:128, fs], lo[64:128, gs_], True, True, [bwl, blo], [bp])
                self.act(aa[:, ts_], ps, AF.Sigmoid, [bp, bpf], [b_aa], bias=a0[:, f:f + 1])
                ps, bp = self.psum()
                self.mm(ps, g2w[:, fs], gl[:, gs_], True, True, [bwl, blo], [bp])
                self.cp(t0[:, ts_], ps, [bp], [b_t0])
            self.dma(self.wgate[fs, hsl], t0, reads=[b_t0], writes=[self.b_wgate[f]], q="pool")
            self.ts(lw, lw, -0.6065306597126334, None, ALU.mult, reads=[b_lw], writes=[b_lw], eng="pool")
            self.ts(kk, km, k_k[:, f:f + 1], None, ALU.mult, reads=[b_km, bpf], writes=[b_kk], eng="act")
            self.act(t1, kk, AF.Square, [b_kk], [b_t1])
            for tb in range(2):
                ts_ = slice(tb * 512, (tb + 1) * 512)
                ps, bp = self.psum()
                self.mm(ps, self.c("blockones"), t1[:, ts_], True, True, [bc, b_t1], [bp])
                self.ts(t2[:, ts_], ps, 1e-24, None, ALU.max, reads=[bp], writes=[b_t2])
            self.act(t2, t2, AF.Ln, [b_t2], [b_t2])
            self.act(t2, t2, AF.Exp, [b_t2], [b_t2], scale=-0.5)
            self.tt(kk, kk, t2, ALU.mult, [b_kk, b_t2], [b_kk])
            self.ts(t3, aa, k_a[:, f:f + 1], omka[:, f:f + 1], ALU.mult, ALU.add, reads=[b_aa, bpf, bomka], writes=[b_t3], eng="act")
            self.tt(km, km, t3, ALU.mult, [b_km, b_t3], [b_km])
            self.tt(t1, rm, km, ALU.mult, [b_rm, b_km], [b_t1], eng="pool")
            self.ts(t1, t1, r_k[:, f:f + 1], None, ALU.mult, reads=[b_t1, bpf], writes=[b_t1], eng="act")
            ps, bp = self.psum()
            for i in range(NH):
                self.mm(ps[:, 2 * i:2 * i + 2], t1[:, i * 128:(i + 1) * 128], self.c("headsel"), True, True, [b_t1, bc], [bp])
            self.cp(rks[:, half * NH:(half + 1) * NH, 2 * f:2 * f + 2], ps[:, 0:2 * NH].rearrange("p (i j) -> p i j", j=2), [bp], [brks], eng="act")
            cl, b_cl = t2, b_t2
            self.S.add("dve", lambda e, o=cl, m=rmaskf, d1=lw: e.tensor_tensor_scan(out=o, data0=m, data1=d1, initial=0.0, op0=ALU.mult, op1=ALU.add),
                       reads=[b_lw, brm], writes=[b_cl])
            clv = cl.rearrange("p (i t) -> p i t", i=NH)
            self.act(ecd2[st], clv[:, :, 127], AF.Exp, [b_cl], [becd2[st]])
            defer[st].append(lambda f=f, half=half, st=st, fs=fs: self.dma(self.ecdD[fs, half * NH:(half + 1) * NH], ecd2[st], reads=[becd2[st]], writes=[becdD], q="sp"))
            self.tt(aa, aa, kk, ALU.mult, [b_aa, b_kk], [b_aa])
            self.act(t3, cl, AF.Exp, [b_cl], [b_t3])
            self.tt(fst[0], rm, t3, ALU.mult, [b_rm, b_t3], [bfst[0]], eng="pool")
            self.tt(t0, cl, lw, ALU.subtract, [b_cl, b_lw], [b_t0], eng="pool")
            self.act(t0, t0, AF.Exp, [b_t0], [b_t0])
            self.stt(fst[1], kk, -1.0, t0, ALU.mult, ALU.mult, [b_kk, b_t0], [bfst[1]])
            self.act(t3, cl, AF.Exp, [b_cl], [b_t3], scale=-1.0)
            self.tt(fst[2], aa, t3, ALU.mult, [b_aa, b_t3], [bfst[2]])
            self.tt(fst[3], km, t3, ALU.mult, [b_km, b_t3], [bfst[3]], eng="pool")
            for q in range(4):
                defer[st].append(lambda q=q, f=f, fs=fs, hsl=hsl, fst=fst, bfst=bfst: self.dma(self.wFM[q][fs, hsl], fst[q], reads=[bfst[q]], writes=[self.b_wFM[f]], q="sp"))
            t1v = t1.rearrange("p (i t) -> p i t", i=NH)
            self.tt(t1v, clv[:, :, 127:128].to_broadcast([128, NH, 128]), clv, ALU.subtract, [b_cl], [b_t1])
            self.act(t1, t1, AF.Exp, [b_t1], [b_t1])
            self.tt(aa, aa, t1, ALU.mult, [b_aa, b_t1], [b_aa])
            self.tt(km, km, t1, ALU.mult, [b_km, b_t1], [b_km], eng="pool")
            for cidx, (src, bsrc) in enumerate(((aa, b_aa), (km, b_km))):
                for q in range(2):
                    ps, bp = self.psum()
                    for k in range(4):
                        i = q * 4 + k
                        self.tr(ps[:, k * 128:(k + 1) * 128], src[:, i * 128:(i + 1) * 128], [bsrc], [bp])
                    self.cp(tst[:, q * 4:q * 4 + 4, cidx, :], ps.rearrange("p (a b) -> p a b", a=4), [bp], [btst])
            defer[st].append(lambda f=f, half=half, tst=tst, btst=btst: self.dma(wTMv[:, half * NH:(half + 1) * NH, f, :, :], tst, reads=[btst], writes=[self.b_wTM[f]], q="sp"))

        defer = [[], []]
        opsA, opsB = [], []
        for f in range(8):
            self.ps_pool = [0, 1, 2, 3]
            opsA += self.S.record(lambda: unit(f, 0, 0))
            self.ps_pool = [4, 5, 6, 7]
            opsB += self.S.record(lambda: unit(f, 1, 1))
        self.ps_pool = list(range(8))
        n0 = (len(opsA) // 8) // 2 if SKEW else 0
        for op in opsA[:n0]:
            self.S.add(*op)
        self.S.replay_interleaved(opsA[n0:], opsB)
        for st_ in range(2):
            for fn_ in defer[st_]:
                fn_()
        A.release()
        self.S.barrier()
        muv = A.alloc([DM])
        lnw = A.alloc([DM])
        lnb = A.alloc([DM])
        brow = Buf()
        self.rowb(l, "mu_v", muv, brow)
        self.rowb(l, "ln_w", lnw, brow)
        self.rowb(l, "ln_b", lnb, brow)
        H = A.alloc([16, 64], parts=64)
        Hb = A.alloc([16, 64], BF16, parts=64)
        ecdh = A.alloc([16, 16], parts=64)
        becdh = Buf()
        self.dma(ecdh, self.ecdD.rearrange("(h p) c -> p h c", p=64), reads=[becdD], writes=[becdh])
        bH = [Buf() for _ in range(16)]
        bHb = [Buf() for _ in range(16)]
        fm4 = [A.alloc([4, 16, 128], BF16, parts=64) for _ in range(2)]
        bF4s = [Buf() for _ in range(2)]
        tm2 = [A.alloc([8, 2, 128], BF16) for _ in range(2)]
        vc = [A.alloc([DM])]
        vp = [A.alloc([DM])]
        gtl = [A.alloc([8, 128])]
        bvc, bgt = Buf(), Buf()
        bld = [Buf() for _ in range(2)]
        vf = A.alloc([DM])
        vb = A.alloc([DM], BF16)
        bv = Buf()
        MA = [A.alloc([3, 128], BF16) for _ in range(16)]
        MBp = [A.alloc([2, 2, 128], BF16) for _ in range(8)]
        XYp = [[A.alloc([2, 2, 128], BF16) for _ in range(2)] for _ in range(8)]
        Zq = [[A.alloc([4, 128], BF16) for _ in range(2)] for _ in range(4)]
        Xg8 = [A.alloc([8, 64], BF16) for _ in range(2)]
        U8 = [A.alloc([8, 64], BF16) for _ in range(2)]
        bM = [Buf() for _ in range(16)]
        bMB = [Buf() for _ in range(8)]
        bXY = [[Buf() for _ in range(2)] for _ in range(8)]
        bZ = [[Buf() for _ in range(2)] for _ in range(4)]
        bXg = [Buf() for _ in range(2)]
        bU = [Buf() for _ in range(2)]
        bHg = [Buf() for _ in range(2)]
        bHbg = [Buf() for _ in range(2)]
        otm = A.alloc([16, 64])
        bo = Buf()
        st2 = A.alloc([4, 16])
        bst2 = Buf()
        osq = A.alloc([16, 64])
        yo = [A.alloc([8, 128], BF16) for _ in range(2)]
        byo = [Buf() for _ in range(2)]
        maskA, maskB, identb = self.c("maskA"), self.c("maskB"), self.identb
        maskAb = maskBb = bmk = None
        MAr = MBr = bMAr = bMBr = None
        wFMv = [self.wFM[q].rearrange("(h p) t -> p h t", p=64) for q in range(4)]
        gatev = self.wgate.rearrange("(c p) t -> p c t", p=128)
        ysv = self.ys[3].rearrange("(c p) t -> p c t", p=128)
        pend_st = None
        for i in range(NT):
            F4, T2, vcu, vpr, gt_, bl = fm4[i % 2], tm2[i % 2], vc[0], vp[0], gtl[0], bld[i % 2]
            bF4 = bF4s[i % 2]
            sl = slice(i * 128, (i + 1) * 128)
            for q in range(4):
                self.dma(F4[:, q, :, :], wFMv[q][:, :, sl], reads=self.b_wFM, writes=[bF4])
            self.dma(T2, self.wTM[sl, :, :, :], reads=self.b_wTM, writes=[bl])
            self.dma(vcu, self.pTM[sl, T_WV:T_WV + 1024], reads=self.b_pTM[i][6:8], writes=[bvc])
            if i == 0:
                self.memset(vpr[0:1, :], 0.0, [bvc])
                self.dma(vpr[1:128, :], self.pTM[0:127, T_WV:T_WV + 1024], reads=self.b_pTM[0][6:8], writes=[bvc])
            else:
                self.dma(vpr, self.pTM[i * 128 - 1:(i + 1) * 128 - 1, T_WV:T_WV + 1024],
                         reads=self.b_pTM[i][6:8] + self.b_pTM[i - 1][6:8], writes=[bvc])
            self.dma(gt_, gatev[:, :, sl], reads=self.b_wgate, writes=[bgt])
            if pend_st is not None:
                pend_st()
                pend_st = None
            self.tt(vf, vpr, vcu, ALU.subtract, [bvc], [bv], eng="pool")
            self.tt(vf, vf, muv, ALU.mult, [bv, brow], [bv], eng="pool")
            self.tt(vf, vf, vcu, ALU.add, [bv, bvc], [bv], eng="pool")
            self.cp(vb, vf, [bv], [bv], eng="act")
            rT, aT, bT, kT = (F4[:, q, :, :] for q in range(4))
            if i == 0:
                self.memset(H, 0.0, bHg)
                self.memset(Hb, 0.0, bHbg)
            for h in range(16):
                ps, bp = self.psum()
                self.mm(ps[:, 0:128], bT[:, h, :], aT[:, h, :], True, True, [bF4], [bp])
                self.mm(ps[:, 128:256], aT[:, h, :], bT[:, h, :], True, True, [bF4], [bp])
                self.mm(ps[:, 256:384], kT[:, h, :], aT[:, h, :], True, True, [bF4], [bp])
                if True:
                    self.tt(MA[h].rearrange("p a b -> p (a b)"), ps[:, 0:384], maskA, ALU.mult, [bp, bc], [bM[h]])
                else:
                    r_, br_ = MAr[(h // 2) % 2], bMAr[(h // 2) % 2]
                    self.cp(r_, ps[:, 0:384], [bp], [br_], eng="act")
                    self.tt(MA[h].rearrange("p a b -> p (a b)"), r_, maskAb, ALU.mult, [br_, bmk], [bM[h]], eng="pool")
                self.tt(Zq[h // 4][0][:, h % 4, :], MA[h][:, 0, :], identb, ALU.add, [bM[h], self.b_identb], [bZ[h // 4][0]], eng="pool")
            for p in range(8):
                ps, bp = self.psum()
                for j in range(2):
                    h = 2 * p + j
                    self.mm(ps[:, j * 256:j * 256 + 128], bT[:, h, :], rT[:, h, :], True, True, [bF4], [bp])
                    self.mm(ps[:, j * 256 + 128:j * 256 + 256], kT[:, h, :], rT[:, h, :], True, True, [bF4], [bp])
                if True:
                    self.tt(MBp[p].rearrange("p j a b -> p j (a b)"), ps.rearrange("p (j c) -> p j c", j=2),
                            maskB.unsqueeze(1).to_broadcast([128, 2, 256]), ALU.mult, [bp, bc], [bMB[p]])
                else:
                    r_, br_ = MBr[(p // 2) % 2], bMBr[(p // 2) % 2]
                    self.cp(r_.rearrange("p j c -> p (j c)"), ps, [bp], [br_], eng="act")
                    self.tt(MBp[p].rearrange("p j a b -> p j (a b)"), r_, maskBb.unsqueeze(1).to_broadcast([128, 2, 256]), ALU.mult,
                            [br_, bmk], [bMB[p]], eng="pool")
            for lv in range(6):
                a_, b_ = lv % 2, (lv + 1) % 2
                for p in range(8):
                    ps, bp = self.psum()
                    rd = []
                    for j in range(2):
                        h = 2 * p + j
                        if lv == 0:
                            X, Y = MA[h][:, 1, :], MA[h][:, 0, :]
                            rd = [bM[2 * p], bM[2 * p + 1]]
                        else:
                            X, Y = XYp[p][a_][:, j, 0, :], XYp[p][a_][:, j, 1, :]
                            rd = [bXY[p][a_]]
                        self.mm(ps[:, j * 256:j * 256 + 128], Y, X, True, True, rd, [bp])
                        if lv < 5:
                            self.mm(ps[:, j * 256 + 128:j * 256 + 256], X, Y, True, True, rd, [bp])
                    ev_ = "dve" if p == 7 else "act"
                    if lv < 5:
                        self.cp(XYp[p][b_].rearrange("p j a b -> p (j a b)"), ps, [bp], [bXY[p][b_]], eng=ev_)
                    else:
                        self.cp(XYp[p][b_][:, :, 0, :], ps.rearrange("p (j a b) -> p j a b", j=2, a=2)[:, :, 0, :], [bp], [bXY[p][b_]], eng=ev_)
                for q in range(4):
                    ps, bp = self.psum()
                    for j in range(4):
                        h = 4 * q + j
                        X2 = XYp[h // 2][b_][:, h % 2, 0, :]
                        self.mm(ps[:, j * 128:(j + 1) * 128], X2, Zq[q][a_][:, j, :], True, True, [bXY[h // 2][b_], bZ[q][a_]], [bp])
                    self.tt(Zq[q][b_].rearrange("p j b -> p (j b)"), ps, Zq[q][a_].rearrange("p j b -> p (j b)"), ALU.add,
                            [bp, bZ[q][a_]], [bZ[q][b_]])
            for g in range(2):
                ps, bp = self.psum()
                for j in range(8):
                    h = 8 * g + j
                    vh = vb[:, h * 64:(h + 1) * 64]
                    o_ = ps[:, j * 64:(j + 1) * 64]
                    self.mm(o_, aT[:, h, :], Hb[:, h, :], True, False, [bF4, bHbg[g]], [bp])
                    self.mm(o_, MA[h][:, 2, :], vh, False, True, [bM[h], bv], [bp])
                self.cp(Xg8[g].rearrange("p j d -> p (j d)"), ps, [bp], [bXg[g]])
            for g in range(2):
                ps, bp = self.psum()
                for j in range(8):
                    h = 8 * g + j
                    self.mm(ps[:, j * 64:(j + 1) * 64], Zq[h // 4][0][:, h % 4, :], Xg8[g][:, j, :], True, True, [bZ[h // 4][0], bXg[g]], [bp])
                self.cp(U8[g].rearrange("p j d -> p (j d)"), ps, [bp], [bU[g]])
            for g in range(2):
                ps, bp = self.psum()
                for j in range(8):
                    h = 8 * g + j
                    vh = vb[:, h * 64:(h + 1) * 64]
                    o_ = ps[:, j * 64:(j + 1) * 64]
                    self.mm(o_, rT[:, h, :], Hb[:, h, :], True, False, [bF4, bHbg[g]], [bp])
                    self.mm(o_, MBp[h // 2][:, h % 2, 0, :], U8[g][:, j, :], False, False, [bMB[h // 2], bU[g]], [bp])
                    self.mm(o_, MBp[h // 2][:, h % 2, 1, :], vh, False, True, [bMB[h // 2], bv], [bp])
                self.cp(otm[:, 8 * g:8 * g + 8, :].rearrange("p j d -> p (j d)"), ps, [bp], [bo])
                ps, bp = self.psum()
                for j in range(8):
                    h = 8 * g + j
                    vh = vb[:, h * 64:(h + 1) * 64]
                    o_ = ps[0:64, j * 64:(j + 1) * 64]
                    self.mm(o_, T2[:, h // 2, 0, (h % 2) * 64:(h % 2 + 1) * 64], U8[g][:, j, :], True, False, [bl, bU[g]], [bp])
                    self.mm(o_, T2[:, h // 2, 1, (h % 2) * 64:(h % 2 + 1) * 64], vh, False, True, [bl, bv], [bp])
                Hg = H[:, 8 * g:8 * g + 8, :]
                self.tt(Hg, Hg, ecdh[:, 8 * g:8 * g + 8, i:i + 1].to_broadcast([64, 8, 64]), ALU.mult, [bHg[g], becdh], [bHg[g]])
                self.tt(Hg, Hg, ps[0:64, :].rearrange("p (j d) -> p j d", j=8), ALU.add, [bHg[g], bp], [bHg[g]])
                self.cp(Hb[:, 8 * g:8 * g + 8, :], Hg, [bHg[g]], [bHbg[g]], eng="act")
            mean, var_, rs_, tmp_ = (st2[:, q, :] for q in range(4))
            self.S.add("dve", lambda e, o=mean, x=otm: e.reduce_sum(out=o, in_=x, axis=AX.X), reads=[bo], writes=[bst2])
            self.ts(mean, mean, 1.0 / 64.0, None, ALU.mult, reads=[bst2], writes=[bst2])
            self.tt(otm, otm, mean.unsqueeze(2).to_broadcast([128, 16, 64]), ALU.subtract, [bo, bst2], [bo])
            self.tt(osq, otm, otm, ALU.mult, [bo], [bo], eng="pool")
            self.S.add("dve", lambda e, o=var_, x=osq: e.reduce_sum(out=o, in_=x, axis=AX.X), reads=[bo], writes=[bst2])
            self.act(rs_, var_, AF.Sqrt, [bst2], [bst2], scale=1.0 / 64.0, bias=64e-5)
            self.recip(rs_, rs_, [bst2], [bst2])
            self.tt(otm, otm, rs_.unsqueeze(2).to_broadcast([128, 16, 64]), ALU.mult, [bo, bst2], [bo])
            ov = otm.rearrange("p h d -> p (h d)")
            self.tt(ov, ov, lnw, ALU.mult, [bo, brow], [bo])
            self.tt(ov, ov, lnb, ALU.add, [bo, brow], [bo])
            self.tt(osq, vf.rearrange("p (h d) -> p h d", h=16), rks[:, i, :].unsqueeze(2).to_broadcast([128, 16, 64]), ALU.mult,
                    [bv, brks], [bo], eng="pool")
            self.tt(otm, otm, osq, ALU.add, [bo], [bo])
            for half in range(2):
                ps, bp = self.psum()
                for k in range(4):
                    kk_ = half * 4 + k
                    self.tr(ps[:, k * 128:(k + 1) * 128], ov[:, kk_ * 128:(kk_ + 1) * 128], [bo], [bp])
                self.tt(yo[i % 2][:, half * 4:half * 4 + 4, :], ps.rearrange("p (a b) -> p a b", a=4), gt_[:, half * 4:half * 4 + 4, :],
                        ALU.mult, [bp, bgt], [byo[i % 2]])
            pend_st = (lambda i=i, sl=sl: self.dma(ysv[:, :, sl], yo[i % 2], reads=[byo[i % 2]], writes=[self.b_ys[3][i]], q="sp"))
        pend_st()
        A.release()

    def merge(self, l):
        A = self.A
        A.mark()
        wo = A.alloc([8, DM], BF16)
        bwo = Buf()
        for hlf in range(2):
            self.dma(wo[:, :, hlf * 512:(hlf + 1) * 512],
                     self.d["w_out"][l].rearrange("(k p) n -> p k n", p=128)[:, :, hlf * 512:(hlf + 1) * 512], writes=[bwo], q="pool")
        self.wslab_init(2)
        acc = A.alloc([8, SEQ])
        bacc = [Buf() for _ in range(8)]
        ysb = [A.alloc([8, 512], BF16) for _ in range(3)]
        bysb = [Buf() for _ in range(3)]
        gtile = [A.alloc([512]) for _ in range(6)]
        bg = [Buf() for _ in range(6)]
        mT = self.hT
        bm = self.bh
        gi = 0
        yi = 0
        for m in range(4):
            yv = self.ys[m].rearrange("(c p) t -> p c t", p=128)
            for dg in range(2):
                wt, bw = self.wslab(self.d["w_branch"][l, m], dg * 512, 512)
                for tb in range(4):
                    ts_ = slice(tb * 512, (tb + 1) * 512)
                    if dg == 0 or True:
                        yt, byt = ysb[yi % 3], bysb[yi % 3]
                        yi += 1
                        self.dma(yt, yv[:, :, ts_], reads=self.b_ys[m] + [self.b_ysF[m]], writes=[byt])
                    for cc in range(4):
                        dch = dg * 4 + cc
                        ps, bp = self.psum()
                        for k in range(8):
                            self.mm(ps, wt[:, k, cc * 128:(cc + 1) * 128], yt[:, k, :], k == 0, k == 7, [bw, byt], [bp])
                        g, bgt = gtile[gi % 6], bg[gi % 6]
                        gi += 1
                        ft = F_GATE + m * 8 + dch
                        self.dma(g, self.pFM[ft * 128:(ft + 1) * 128, ts_], reads=[self.b_pFM[ft]], writes=[bgt])
                        self.act(g, g, AF.Sigmoid, [bgt], [bgt])
                        if m == 0:
                            self.tt(acc[:, dch, ts_], ps, g, ALU.mult, [bp, bgt], [bacc[dch]])
                        else:
                            self.tt(g, ps, g, ALU.mult, [bp, bgt], [bgt])
                            if m < 3:
                                self.tt(acc[:, dch, ts_], acc[:, dch, ts_], g, ALU.add, [bacc[dch], bgt], [bacc[dch]])
                            else:
                                self.tt(mT[:, dch, ts_], acc[:, dch, ts_], g, ALU.add, [bacc[dch], bgt], bm[4 * tb:4 * tb + 4])
        A.release()
        self.S.barrier()
        A.mark()
        wo_ = A.alloc([8, DM], BF16)
        self.rowb(l, "norm_ffn", self.nrow, self.b_nrow)
        xin = [A.alloc([DM]) for _ in range(2)]
        bxin = [Buf() for _ in range(2)]
        self.norm_alloc()
        src = self.x if l == 0 else self.xres
        def ldx(i):
            self.dma(xin[i % 2], src[i * 128:(i + 1) * 128, :], reads=[self.b_xres[i]], writes=[bxin[i % 2]])
        def mm_stage(i):
            sl = slice(i * 128, (i + 1) * 128)
            out = []
            for hlf in range(2):
                ps, bp = self.psum()
                for k in range(8):
                    self.mm(ps, mT[:, k, sl], wo[:, k, hlf * 512:(hlf + 1) * 512], k == 0, k == 7, [bm[i], bwo], [bp])
                out.append((ps, bp))
            return out

        ldx(0)
        pss = mm_stage(0)
        for i in range(NT):
            xt, bx = xin[i % 2], bxin[i % 2]
            sl = slice(i * 128, (i + 1) * 128)
            nxt = None
            if i + 1 < NT:
                ldx(i + 1)
                nxt = mm_stage(i + 1)
            for hlf, (ps, bp) in enumerate(pss):
                self.tt(xt[:, hlf * 512:(hlf + 1) * 512], xt[:, hlf * 512:(hlf + 1) * 512], ps, ALU.add, [bx, bp], [bx])
            self.dma(self.xres[sl, :], xt, reads=[bx], writes=[self.b_xres[i]])
            self.norm_tile(xt, bx, i)
            pss = nxt
        A.release()

    def ffn(self, l):
        A = self.A
        A.mark()
        self.norm_alloc()
        h2, bh2 = self.hT, self.bh
        cw = self.pf("ffn_cw").rearrange("p (t j) -> p t j", j=3)
        cb = self.pf("ffn_cb")
        bpf = self.b_pfm
        wd = A.alloc([22, DM], BF16)
        bwd = Buf()
        wdv = self.d["ffn_down"][l].rearrange("(k p) n -> p k n", p=128)
        for q in range(4):
            self.dma(wd[:, :, q * 256:(q + 1) * 256], wdv[:, :, q * 256:(q + 1) * 256], writes=[bwd], q="pool")
        aT = [A.alloc([22, 1024], BF16)]
        baT = [Buf()]
        halo = A.alloc([44, 2], BF16)
        bhalo = Buf()
        wv = [A.alloc([8, 256], BF16) for _ in range(2)]
        wg = [A.alloc([8, 256], BF16) for _ in range(2)]
        bws = [Buf() for _ in range(2)]
        ub = [[A.alloc([2 + 512], BF16) for _ in range(2)] for _ in range(2)]
        bub = [[Buf() for _ in range(2)] for _ in range(2)]
        dg = [A.alloc([2, 3, 128], BF16) for _ in range(3)]
        bdg = [Buf() for _ in range(3)]
        sg = [A.alloc([512]) for _ in range(2)]
        bsg = [Buf() for _ in range(2)]
        last = (l == self.L - 1)
        if last:
            self.dma(self.nrow, self.d["norm_final"].partition_broadcast(128), writes=[self.b_nrow])
        else:
            o, w = PROW_OFF["norm_mix"]
            self.dma(self.nrow, self.d["prow"][l + 1, o:o + w].partition_broadcast(128), writes=[self.b_nrow])
        xin = [A.alloc([DM]) for _ in range(2)]
        bxin = [Buf() for _ in range(2)]
        upv = self.d["ffn_up"][l].rearrange("(k p) n -> p k n", p=128)
        ubi = 0
        pending = None
        for hh in range(2):
            aTq, baTq = aT[0], baT[0]
            for sb in range(11):
                wv_, wg_, bw_ = wv[sb % 2], wg[sb % 2], bws[sb % 2]
                self.dma(wv_, upv[:, :, sb * 256:(sb + 1) * 256], writes=[bw_], q="pool")
                self.dma(wg_, upv[:, :, DFF + sb * 256:DFF + (sb + 1) * 256], writes=[bw_], q="pool")
                for jj in range(2):
                  jf = 2 * sb + jj
                  d_, bd_ = dg[jf % 3], bdg[jf % 3]
                  for vg in range(2):
                      ft = jf + 22 * vg
                      for j in range(3):
                          self.ts(d_[:, vg, j, :], self.identb, cw[:, ft, j:j + 1], None, ALU.mult,
                                  reads=[self.b_identb, bpf], writes=[bd_], eng="act")
                  for blk in range(2):
                    hf = hh * 2 + blk
                    t0 = hf * 512
                    ubi += 1
                    for vg in range(2):
                        ft = jf + 22 * vg
                        w_ = wv_ if vg == 0 else wg_
                        uu, buu = ub[vg][ubi % 2], bub[vg][ubi % 2]
                        if hf == 0:
                            self.memset(uu[:, 0:2], 0.0, [buu])
                        else:
                            self.cp(uu[:, 0:2], halo[:, ft, :], [bhalo], [buu], eng="act")
                        ps, bp = self.psum()
                        for k in range(8):
                            self.mm(ps, w_[:, k, jj * 128:(jj + 1) * 128], h2[:, k, t0:t0 + 512], k == 0, k == 7,
                                    [bw_] + bh2[(t0 // 128):(t0 // 128) + 4], [bp])
                        self.cp(uu[:, 2:514], ps, [bp], [buu])
                        if hf < 3:
                            self.cp(halo[:, ft, :], uu[:, 512:514], [buu], [bhalo], eng="act")
                    if pending is not None:
                        pending()

                    def conv_stage(jf=jf, blk=blk, d_=d_, bd_=bd_, k_=ubi % 2):
                        conv_ps = []
                        for vg in range(2):
                            ft = jf + 22 * vg
                            uu, buu = ub[vg][k_], bub[vg][k_]
                            ps2, bp2 = self.psum()
                            for j in range(3):
                                self.mm(ps2, d_[:, vg, j, :], uu[:, j:j + 512], j == 0, j == 2, [bd_, buu], [bp2])
                            conv_ps.append((ps2, bp2, ft))
                        (psv, bpv, ftv), (psg, bpg, ftg) = conv_ps
                        sg_, bsg_ = sg[k_], bsg[k_]
                        self.act(sg_, psg, AF.Silu, [bpg, bpf], [bsg_], bias=cb[:, ftg:ftg + 1])
                        self.stt(aTq[:, jf, blk * 512:(blk + 1) * 512], psv, cb[:, ftv:ftv + 1], sg_, ALU.add, ALU.mult, [bpv, bpf, bsg_], [baTq])
                    pending = conv_stage
            if pending is not None:
                pending()
                pending = None
            def dn_stage(it):
                out = []
                for hlf in range(2):
                    ps, bp = self.psum()
                    for k in range(22):
                        self.mm(ps, aTq[:, k, it * 128:(it + 1) * 128], wd[:, k, hlf * 512:(hlf + 1) * 512], k == 0, k == 21, [baTq, bwd], [bp])
                    out.append((ps, bp))
                return out

            pss = dn_stage(0)
            for it in range(8):
                i = hh * 8 + it
                sl = slice(i * 128, (i + 1) * 128)
                xt, bx = xin[i % 2], bxin[i % 2]
                if it == 0:
                    self.dma(xt, self.xres[sl, :], reads=[self.b_xres[i]], writes=[bx])
                nxt = None
                if it + 1 < 8:
                    self.dma(xin[(i + 1) % 2], self.xres[(i + 1) * 128:(i + 2) * 128, :], reads=[self.b_xres[i + 1]], writes=[bxin[(i + 1) % 2]])
                    nxt = dn_stage(it + 1)
                for hlf, (ps, bp) in enumerate(pss):
                    self.tt(xt[:, hlf * 512:(hlf + 1) * 512], xt[:, hlf * 512:(hlf + 1) * 512], ps, ALU.add, [bx, bp], [bx])
                pss = nxt
                if last:
                    self.norm_tile(xt, bx, i, final_out=self.out[sl, :])
                else:
                    self.dma(self.xres[sl, :], xt, reads=[bx], writes=[self.b_xres[i]])
                    self.norm_tile(xt, bx, i)
        A.release()


_CACHE = {}


def build_program(L, dbg=False, phases=None):
    nc = bass.Bass("TRN2", target_bir_lowering=False)
    kb = KB(nc, L, dbg=dbg, phases=phases)
    kb.build()
    return nc


def kernel(**inputs):
    inp = {k: np.asarray(v) for k, v in inputs.items()}
    L = inp["w_in"].shape[0]
    B = inp["x"].shape[0]
    g = prep_inputs(inp)
    nc = build_program(L)
    in_maps = []
    for b in range(B):
        m = dict(g)
        m["x"] = np.ascontiguousarray(inp["x"][b])
        in_maps.append(m)
    res = run_bass_kernel_spmd(nc, in_maps, core_ids=list(range(B)))
    out = np.stack([np.asarray(r["out"]) for r in res.results], axis=0).astype(np.float32)
    return out
```
